# Optimizing a Trainium2 kernel written in Bass

```python
import math
import jax, jax.numpy as jnp
from jax import lax
import numpy as np

D_MODEL = 1024
BATCH = 16
SEQ = 2048
DEPTH = 2

HEAD_DIM = 64
A_HEADS = 6
A_BRANCHES = ((128, 1), (512, 4), (2048, 16))
B_HEADS = 4
B_KEY_DIM = 32
B_VAL_DIM = 64
B_GATE_RANK = 16
B_GATE_TAU = 16.0
B_CHUNK = 16
C_HEADS = 6
C_BLOCK = 256
C_TOPK = 3
C_QCHUNK = 16
REL_BUCKETS = 32
REL_MAX_DIST = 2048
D_FF = 4 * D_MODEL
EPS = 1e-6
NEG_INF = -1e30

A_WIDTH = A_HEADS * HEAD_DIM
B_QK = B_HEADS * B_KEY_DIM
B_WIDTH = B_HEADS * B_VAL_DIM
C_WIDTH = C_HEADS * HEAD_DIM
MIX_WIDTH = A_WIDTH + B_WIDTH + C_WIDTH
IN_SIZES = (A_WIDTH, A_WIDTH, A_WIDTH, B_QK, B_QK, B_WIDTH, B_WIDTH, B_GATE_RANK, C_WIDTH, C_WIDTH, C_WIDTH)
IN_WIDTH = sum(IN_SIZES)

kernel_name = "hybrid_dilated_gla_moba_block"


def rmsnorm(x, w):
    xf = x.astype(jnp.float32)
    y = xf * lax.rsqrt(jnp.mean(xf * xf, axis=-1, keepdims=True) + EPS)
    return (y * w.astype(jnp.float32)).astype(x.dtype)


def rel_bucket(dist):
    n = jnp.maximum(dist, 0)
    exact = REL_BUCKETS // 2
    logv = jnp.log(jnp.maximum(n, 1).astype(jnp.float32) / exact) / math.log(REL_MAX_DIST / exact)
    large = jnp.minimum(exact + (logv * (REL_BUCKETS - exact)).astype(jnp.int32), REL_BUCKETS - 1)
    return jnp.where(n < exact, n, large)


def dilated_branch(q, k, v, bias_a, window, dilation):
    Bsz, S, H, Dh = q.shape
    L = S // dilation
    span = window // dilation
    blk = min(span, L)
    nb = -(-L // blk)
    Lp = nb * blk

    def sub(t):
        t = t.reshape(Bsz, L, dilation, H, Dh).transpose(0, 2, 1, 3, 4)
        t = jnp.pad(t, ((0, 0), (0, 0), (0, Lp - L), (0, 0), (0, 0)))
        return t.reshape(Bsz, dilation, nb, blk, H, Dh)

    def with_prev(t):
        prev = jnp.pad(t, ((0, 0), (0, 0), (1, 0), (0, 0), (0, 0), (0, 0)))[:, :, :-1]
        return jnp.concatenate([prev, t], axis=3)

    qb = sub(q)
    kc = with_prev(sub(k))
    vc = with_prev(sub(v))
    logits = jnp.einsum('brnqhe,brnkhe->brnhqk', qb, kc, preferred_element_type=jnp.float32) * (Dh ** -0.5)

    steps = jnp.arange(blk)[:, None] + blk - jnp.arange(2 * blk)[None, :]
    band = (steps >= 0) & (steps <= span)
    key_ok = (jnp.arange(nb)[:, None] * blk + jnp.arange(2 * blk)[None, :] - blk) >= 0
    mask = band[None] & key_ok[:, None, :]
    bias = bias_a[:, rel_bucket(steps * dilation)].astype(jnp.float32)
    logits = jnp.where(mask[None, None, :, None], logits + bias[None, None, None], NEG_INF)

    lse = jax.nn.logsumexp(logits, axis=-1)
    p = jnp.exp(logits - lse[..., None])
    out = jnp.einsum('brnhqk,brnkhe->brnqhe', p.astype(v.dtype), vc)
    out = out.reshape(Bsz, dilation, Lp, H, Dh)[:, :, :L].transpose(0, 2, 1, 3, 4).reshape(Bsz, S, H, Dh)
    lse = lse.transpose(0, 1, 2, 4, 3).reshape(Bsz, dilation, Lp, H)[:, :, :L].transpose(0, 2, 1, 3).reshape(Bsz, S, H)
    return out.astype(jnp.float32), lse


def dilated_attention(q, k, v, bias_a):
    outs, lses = [], []
    for window, dilation in A_BRANCHES:
        o, l = dilated_branch(q, k, v, bias_a, window, dilation)
        outs.append(o)
        lses.append(l)
    wts = jax.nn.softmax(jnp.stack(lses), axis=0)
    out = jnp.einsum('gbsh,gbshe->bshe', wts, jnp.stack(outs))
    return out.astype(q.dtype)


def gla(q, k, v, r, a_lr, w_a2, b_a, norm_w):
    f32 = jnp.float32
    Bsz, S, H, dk = q.shape
    dv = v.shape[-1]
    C = B_CHUNK
    N = S // C
    log_a = jax.nn.log_sigmoid((a_lr @ w_a2 + b_a).astype(f32)) / B_GATE_TAU

    def chunks(t):
        return t.astype(f32).reshape(Bsz, N, C, H, -1).transpose(0, 3, 1, 2, 4)

    qc = chunks(q) * (dk ** -0.5)
    kc = chunks(k)
    vc = chunks(v)
    b = jnp.cumsum(chunks(log_a.reshape(Bsz, S, H, dk)), axis=3)

    causal = jnp.tril(jnp.ones((C, C), dtype=bool))
    diff = b[..., :, None, :] - b[..., None, :, :]
    decay = jnp.exp(jnp.where(causal[:, :, None], diff, NEG_INF))
    attn = jnp.einsum('bhnic,bhnjc,bhnijc->bhnij', qc, kc, decay)
    o_intra = jnp.einsum('bhnij,bhnjv->bhniv', attn, vc)

    b_last = b[..., -1, :]
    u = jnp.einsum('bhnjc,bhnjv->bhncv', kc * jnp.exp(b_last[..., None, :] - b), vc)

    def step(state, inp):
        dec, uu = inp
        return dec[..., None] * state + uu, state

    _, s_prev = lax.scan(step, jnp.zeros((Bsz, H, dk, dv), f32),
                         (jnp.moveaxis(jnp.exp(b_last), 2, 0), jnp.moveaxis(u, 2, 0)))
    s_prev = jnp.moveaxis(s_prev, 0, 2)
    o_inter = jnp.einsum('bhnic,bhncv->bhniv', qc * jnp.exp(b), s_prev)

    o = (o_intra + o_inter).transpose(0, 2, 3, 1, 4).reshape(Bsz, S, H, dv)
    o = o * lax.rsqrt(jnp.mean(o * o, axis=-1, keepdims=True) + EPS)
    o = o.reshape(Bsz, S, H * dv) * norm_w.astype(f32) * jax.nn.silu(r.astype(f32))
    return o.astype(q.dtype)


def moba_attention(q, k, v, bias_c):
    f32 = jnp.float32
    Bsz, S, H, Dh = q.shape
    nblk = -(-S // C_BLOCK)
    Sp = nblk * C_BLOCK

    def pad(t):
        return jnp.pad(t, ((0, 0), (0, Sp - S), (0, 0), (0, 0))).transpose(0, 2, 1, 3)

    qp, kp, vp = pad(q), pad(k), pad(v)
    kb = kp.reshape(Bsz, H, nblk, C_BLOCK, Dh)
    vb = vp.reshape(Bsz, H, nblk, C_BLOCK, Dh)
    k_mean = jnp.mean(kb.astype(f32), axis=3)

    qblk = jnp.arange(Sp) // C_BLOCK
    past = jnp.arange(nblk)[None, :] < qblk[:, None]
    gate = jnp.where(past, jnp.einsum('bhsd,bhnd->bhsn', qp.astype(f32), k_mean), NEG_INF)
    topk = min(C_TOPK, nblk)
    _, idx = lax.top_k(gate, topk)
    valid = idx < qblk[None, None, :, None]

    n_q = Sp // C_QCHUNK
    qch = qp.reshape(Bsz, H, n_q, C_QCHUNK, Dh).transpose(2, 0, 1, 3, 4)
    idxch = idx.reshape(Bsz, H, n_q, C_QCHUNK, topk).transpose(2, 0, 1, 3, 4)
    valch = valid.reshape(Bsz, H, n_q, C_QCHUNK, topk).transpose(2, 0, 1, 3, 4)
    b_ix = jnp.arange(Bsz)[:, None, None, None]
    h_ix = jnp.arange(H)[None, :, None, None]
    scale = Dh ** -0.5
    nsel = topk * C_BLOCK

    def one_chunk(args):
        ci, qc_, idx_, val_ = args
        qpos = ci * C_QCHUNK + jnp.arange(C_QCHUNK)
        ksel = kb[b_ix, h_ix, idx_].reshape(Bsz, H, C_QCHUNK, nsel, Dh)
        vsel = vb[b_ix, h_ix, idx_].reshape(Bsz, H, C_QCHUNK, nsel, Dh)
        kpos = (idx_[..., None] * C_BLOCK + jnp.arange(C_BLOCK)).reshape(Bsz, H, C_QCHUNK, nsel)
        l_sel = jnp.einsum('bhqd,bhqkd->bhqk', qc_, ksel, preferred_element_type=f32) * scale
        l_sel = l_sel + bias_c[h_ix, rel_bucket(qpos[None, None, :, None] - kpos)].astype(f32)
        l_sel = jnp.where(jnp.repeat(val_, C_BLOCK, axis=-1), l_sel, NEG_INF)
        own = (ci * C_QCHUNK) // C_BLOCK
        kown = lax.dynamic_slice_in_dim(kp, own * C_BLOCK, C_BLOCK, axis=2)
        vown = lax.dynamic_slice_in_dim(vp, own * C_BLOCK, C_BLOCK, axis=2)
        d_own = qpos[:, None] - (own * C_BLOCK + jnp.arange(C_BLOCK))[None, :]
        l_own = jnp.einsum('bhqd,bhkd->bhqk', qc_, kown, preferred_element_type=f32) * scale
        l_own = jnp.where(d_own[None, None] >= 0, l_own + bias_c[:, rel_bucket(d_own)].astype(f32)[None], NEG_INF)
        p = jax.nn.softmax(jnp.concatenate([l_sel, l_own], axis=-1), axis=-1)
        out = (jnp.einsum('bhqk,bhqkd->bhqd', p[..., :nsel].astype(vsel.dtype), vsel)
               + jnp.einsum('bhqk,bhkd->bhqd', p[..., nsel:].astype(vown.dtype), vown))
        return out

    outs = lax.map(one_chunk, (jnp.arange(n_q), qch, idxch, valch))
    return outs.transpose(1, 0, 3, 2, 4).reshape(Bsz, Sp, H, Dh)[:, :S]


def hybrid_layer(x, norm1_w, w_in, w_a2, b_a, gla_norm_w, w_out, norm2_w, w_ff1, w_ff2, rel_bias):
    Bsz, S, _ = x.shape
    h = rmsnorm(x, norm1_w)
    proj = h @ w_in
    aq, ak, av, bq, bk, bv, br, ba, cq, ck, cv = jnp.split(proj, list(np.cumsum(IN_SIZES)[:-1]), axis=-1)

    def heads(t, n):
        return t.reshape(Bsz, S, n, -1)

    bias_a = rel_bias[:, :A_HEADS].T
    bias_c = rel_bias[:, A_HEADS:].T
    y_a = dilated_attention(heads(aq, A_HEADS), heads(ak, A_HEADS), heads(av, A_HEADS), bias_a)
    y_b = gla(heads(bq, B_HEADS), heads(bk, B_HEADS), heads(bv, B_HEADS), br, ba, w_a2, b_a, gla_norm_w)
    y_c = moba_attention(heads(cq, C_HEADS), heads(ck, C_HEADS), heads(cv, C_HEADS), bias_c)
    mix = jnp.concatenate([y_a.reshape(Bsz, S, A_WIDTH), y_b, y_c.reshape(Bsz, S, C_WIDTH)], axis=-1)
    x = x + mix @ w_out

    h2 = rmsnorm(x, norm2_w)
    x = x + jnp.square(jax.nn.relu(h2 @ w_ff1)) @ w_ff2
    return x


def setup_inputs(seed: int = 0) -> dict:
    key = jax.random.key(seed)
    ks = jax.random.split(key, 14)
    f32 = jnp.float32
    nrm = lambda k, shape, s: (jax.random.normal(k, shape, f32) * s)
    return {
        "x": nrm(ks[0], (BATCH, SEQ, D_MODEL), 1.0),
        "norm1_w": 1.0 + nrm(ks[1], (DEPTH, D_MODEL), 0.02),
        "w_in": nrm(ks[2], (DEPTH, D_MODEL, IN_WIDTH), D_MODEL ** -0.5),
        "gla_w_a2": nrm(ks[3], (DEPTH, B_GATE_RANK, B_QK), B_GATE_RANK ** -0.5),
        "gla_b_a": nrm(ks[4], (DEPTH, B_QK), 0.1),
        "gla_norm_w": 1.0 + nrm(ks[5], (DEPTH, B_WIDTH), 0.02),
        "w_out": nrm(ks[6], (DEPTH, MIX_WIDTH, D_MODEL), MIX_WIDTH ** -0.5),
        "norm2_w": 1.0 + nrm(ks[7], (DEPTH, D_MODEL), 0.02),
        "w_ff1": nrm(ks[8], (DEPTH, D_MODEL, D_FF), D_MODEL ** -0.5),
        "w_ff2": nrm(ks[9], (DEPTH, D_FF, D_MODEL), D_FF ** -0.5),
        "rel_bias": nrm(ks[10], (REL_BUCKETS, A_HEADS + C_HEADS), 0.2),
        "final_norm_w": 1.0 + nrm(ks[11], (D_MODEL,), 0.02),
    }


def reference(x, norm1_w, w_in, gla_w_a2, gla_b_a, gla_norm_w, w_out, norm2_w, w_ff1, w_ff2, rel_bias, final_norm_w):
    for i in range(DEPTH):
        x = hybrid_layer(x, norm1_w[i], w_in[i], gla_w_a2[i], gla_b_a[i], gla_norm_w[i], w_out[i],
                         norm2_w[i], w_ff1[i], w_ff2[i], rel_bias)
    return rmsnorm(x, final_norm_w)
```

```python
import math
import contextlib
import numpy as np
import concourse.bass as bass
import concourse.mybir as mybir
from concourse.bass_utils import run_bass_kernel_spmd

F32 = mybir.dt.float32
AF = mybir.ActivationFunctionType
ALU = mybir.AluOpType
AX = mybir.AxisListType

S_LEN = 2048
D = 1024
NCH = 8
DFF = 4096
IN_W = 3088
EPS = 1e-6
BIG = 30000.0
OFF = dict(aq=0, ak=384, av=768, bq=1152, bk=1280, bv=1408, br=1664, ba=1920, cq=1936, ck=2320, cv=2704)
C_IDENT, C_ONES, C_TRI, C_SWAP, C_BLK, C_PASTNEG, C_NEGPAST2, C_HM, C_HMS, NCONST = 0, 128, 256, 384, 512, 640, 768, 896, 900, 912
P_NW1, P_NW2, P_FW, P_GNW, NPAR = 0, 16, 32, 40, 48
GW = 2432
SAME_ENGINE_SYNC = True


class Sched:
    ENGS = ("pe", "act", "dve", "pool", "sp")

    def __init__(self):
        self.ops = {e: [] for e in self.ENGS}
        self.last_w = {}
        self.readers = {}
        self.dma_cnt = {}

    def _deps(self, reads, writes):
        deps = set()
        for r in reads:
            ev = self.last_w.get(r)
            if ev is not None:
                deps.add(ev)
        for w in writes:
            ev = self.last_w.get(w)
            if ev is not None:
                deps.add(ev)
            for ev in self.readers.get(w, {}).values():
                deps.add(ev)
        return deps

    def _record(self, ev, reads, writes):
        for r in reads:
            self.readers.setdefault(r, {})[ev[:2]] = ev
        for w in writes:
            self.last_w[w] = ev
            self.readers[w] = {}

    def op(self, eng, fn, r=(), w=()):
        deps = self._deps(r, w)
        ev = ("e", eng, len(self.ops[eng]))
        self.ops[eng].append((fn, deps, ev))
        self._record(ev, r, w)

    def dma(self, eng, out, in_, sem, r=(), w=()):
        deps = self._deps(r, w)
        self.dma_cnt[sem] = self.dma_cnt.get(sem, 0) + 1
        ev = ("d", sem, self.dma_cnt[sem])
        self.ops[eng].append((lambda e: e.dma_start(out=out, in_=in_), deps, ev))
        self._record(ev, r, w)

    def barrier(self):
        deps = set()
        for e in self.ENGS:
            if self.ops[e]:
                deps.add(self.ops[e][-1][2] if self.ops[e][-1][2][0] == "e" else None)
        deps.discard(None)
        for e in self.ENGS:
            for i in range(len(self.ops[e]) - 1, -1, -1):
                if self.ops[e][i][2][0] == "e":
                    deps.add(self.ops[e][i][2])
                    break
        for k, cnt in self.dma_cnt.items():
            deps.add(("d", k, cnt))
        for e in self.ENGS:
            ev = ("e", e, len(self.ops[e]))
            self.ops[e].append(((lambda eng: eng.nop()), set(deps), ev))

    def emit(self, nc, stack):
        marked = {e: set() for e in self.ENGS}
        for e in self.ENGS:
            for (_, deps, _) in self.ops[e]:
                for d in deps:
                    if d[0] == "e":
                        if d[1] == e and (e == "pe" or not SAME_ENGINE_SYNC):
                            continue
                        marked[d[1]].add(d[2])
        val = {}
        for e in self.ENGS:
            for i, idx in enumerate(sorted(marked[e])):
                val[(e, idx)] = i + 1
        sems = {e: stack.enter_context(nc.semaphore("s_" + e)) for e in self.ENGS}
        dsems = {}
        for k in self.dma_cnt:
            dsems[k] = stack.enter_context(nc.semaphore("d_" + "_".join(str(x) for x in k)))
        block = stack.enter_context(nc.Block())

        def replay(ename, eng):
            known = {}
            for (fn, deps, ev) in self.ops[ename]:
                need = {}
                for d in deps:
                    if d[0] == "e":
                        if d[1] == ename and (ename == "pe" or not SAME_ENGINE_SYNC):
                            continue
                        key, v = ("e", d[1]), val[(d[1], d[2])]
                    else:
                        key, v = ("d", d[1]), 16 * d[2]
                    if v > need.get(key, 0):
                        need[key] = v
                for key, v in need.items():
                    if known.get(key, 0) >= v:
                        continue
                    known[key] = v
                    eng.wait_ge(sems[key[1]] if key[0] == "e" else dsems[key[1]], v)
                ins = fn(eng)
                if ev[0] == "d":
                    ins.then_inc(dsems[ev[1]], 16)
                elif (ename, ev[2]) in val:
                    ins.then_inc(sems[ename], 1)

        block.tensor(lambda e: replay("pe", e))
        block.scalar(lambda e: replay("act", e))
        block.vector(lambda e: replay("dve", e))
        block.gpsimd(lambda e: replay("pool", e))
        block.sync(lambda e: replay("sp", e))


def build_program(nseq, nlayers=2, dbg=None):
    nc = bass.Bass("TRN2", target_bir_lowering=False)
    dt = lambda name, shape, kind="ExternalInput": nc.dram_tensor(name, shape, F32, kind=kind).ap()
    x_d = dt("x", [nseq, S_LEN, D])
    win_d = dt("w_in", [2, D, IN_W])
    wout_d = dt("w_out", [2, D, D])
    wff1_d = dt("w_ff1", [2, D, DFF])
    wff2_d = dt("w_ff2", [2, DFF, D])
    ta_d = dt("ta", [6, 128, 1536])
    g_d = dt("gtab", [6, 128, GW])
    const_d = dt("consts", [128, NCONST])
    ind_d = dt("ind", [8, S_LEN])
    par_d = dt("params", [128, NPAR])
    wa2_d = dt("wa2", [16, 256])
    ba_d = dt("ba", [1, 256])
    out_d = dt("out", [nseq, S_LEN, D], kind="ExternalOutput")

    S = Sched()
    stack = contextlib.ExitStack()
    sb = lambda name, shape: stack.enter_context(nc.sbuf_tensor(name, shape, F32))
    XT = sb("XT", [128, NCH, S_LEN])
    MT = sb("MT", [128, NCH, S_LEN])
    RB = sb("RB", [128, S_LEN])
    RT = sb("RT", [128, 16])
    CN = sb("CN", [128, NCONST])
    PR = sb("PR", [128, NPAR])
    WA2 = sb("WA2", [16, 256])
    BA = sb("BA", [1, 256])
    NSLOT = 6
    WR = sb("WR", [128, NSLOT, 512])
    WORKN = 13312
    WK = sb("WK", [128, WORKN])
    PS = stack.enter_context(nc.psum_tensor("PS", [128, 8, 512], F32))

    IDENT = CN[:, C_IDENT:C_IDENT + 128]
    ONES = CN[:, C_ONES:C_ONES + 128]
    TRI = CN[:, C_TRI:C_TRI + 128]
    SWAP = CN[:, C_SWAP:C_SWAP + 128]
    BLK = CN[:, C_BLK:C_BLK + 128]

    QA = WK[:, 0:2048]
    KA = WK[:, 2048:4096]
    VE = WK[:, 4096:6144].rearrange("p (t f) -> p t f", f=128)
    PT = [WK[:, 6144:6656], WK[:, 6656:7168]]
    ACC = WK[:, 7168:9216]
    TAB = WK[:, 9216:9728]
    GT = WK[:, 7168:9600]
    KM = WK[:, 9600:9608]
    VT = WK[:, 9728:11776]
    MSC = WK[:, 11776:12800]
    RCP = MSC[:, 0:512]
    GM = MSC[:, 512:640]
    TH = MSC[:, 640:768]
    NS = MSC[:, 768:896]
    MK = MSC[:, 896:1024]
    OSB = WK[:, 12800:13312]
    H2 = WK[:, 7168:11264].rearrange("p (c t) -> p c t", t=512)
    BW = WK[:, 0:6272].rearrange("p (c f) -> p c f", f=784)
    BQ = WK[:, 6272:6784]
    BK = WK[:, 6784:7296]
    BR = [WK[:, 7296:7808], WK[:, 7808:8320]]
    ALR = WK[:, 8320:8832]
    BV = WK[:, 8832:9088]
    SP_ = WK[:, 9088:9216]
    EB = WK[:, 9216:9344]
    KG = WK[:, 9344:9472]
    QGM = WK[:, 9472:9984]
    KGT = WK[:, 9984:10112]
    AT = WK[:, 10112:10624]
    OSQ = WK[:, 10624:10880]
    RS = WK[:, 10880:11136]
    SALL = WK[:, 11136:11200]
    XIN = [WK[:, 0:1024], WK[:, 1024:2048]]
    YN = WK[:, 2048:3072]
    OUTT = [WK[:, 3072:4096], WK[:, 4096:5120]]
    SQ = [WK[:, 6144:6656], WK[:, 6656:7168]]
    WKR = ("WK",)

    def tb_sl(tb):
        return slice(tb * 512, (tb + 1) * 512)

    def xt_res(cs, tbs):
        return [("XT", c, tb) for c in cs for tb in tbs]

    ALLC = range(NCH)

    S.dma("sp", CN[:, :], const_d[:, :], ("c", 0), w=[("CN",)])
    S.dma("sp", PR[:, :], par_d[:, :], ("c", 1), w=[("PR",)])
    S.dma("sp", WA2[:, :], wa2_d[:, :], ("c", 2), w=[("WA2",)])
    S.dma("sp", BA[:, :], ba_d[:, :], ("c", 3), w=[("BA",)])

    wstate = {"n": 0}

    def wload(dst_view_fn, src_ap):
        slot = wstate["n"] % NSLOT
        wstate["n"] += 1
        S.dma("sp", dst_view_fn(WR[:, slot, :]), src_ap, ("w", slot), w=[("W", slot)])
        return slot

    def wscale(slot, ncs, fcols, pcol0):
        v = WR[:, slot, :].rearrange("p (c f) -> p c f", f=fcols)
        sc = PR[:, pcol0:pcol0 + ncs].rearrange("p (c o) -> p c o", o=1).broadcast_to([128, ncs, fcols])
        S.op("dve", (lambda e: e.tensor_tensor(out=v, in0=v, in1=sc, op=ALU.mult)), r=[("PR",)], w=[("W", slot)])

    pj = {"n": 0}

    def pj_bank():
        b = pj["n"] % 2
        pj["n"] += 1
        return b

    def norm_stats(tb, bank):
        for c in ALLC:
            sq = SQ[c % 2]
            S.op("act", (lambda e, c=c, sq=sq: e.activation(out=sq, in_=XT[:, c, tb_sl(tb)], func=AF.Square)),
                 r=xt_res([c], [tb]), w=[("PT", c % 2)])
            S.op("pe", (lambda e, c=c, sq=sq: e.matmul(PS[:, bank, :], lhsT=ONES, rhs=sq, start=(c == 0), stop=(c == 7))),
                 r=[("PT", c % 2), ("CN",)], w=[("PS", bank)])
        S.op("act", (lambda e: e.activation(out=RB[:, tb_sl(tb)], in_=PS[:, bank, :], func=AF.Ln, bias=EPS_AP, scale=1.0 / D)),
             r=[("PS", bank), ("CN",)], w=[("RB", tb)])
        S.op("act", (lambda e: e.activation(out=RB[:, tb_sl(tb)], in_=RB[:, tb_sl(tb)], func=AF.Exp, scale=-0.5)),
             r=[("RB", tb)], w=[("RB", tb)])

    EPS_AP = CN[:, NCONST - 1:NCONST]


    def tcols(b, u):
        if b == 1:
            return slice(u * 128, (u + 1) * 128)
        if b == 4:
            r_, st = u // 4, u % 4
            st0 = 512 * st + r_
            return slice(st0, st0 + 4 * 127 + 1, 4)
        return slice(u, u + 16 * 127 + 1, 16)

    QK_ALL = [("QA", tb) for tb in range(4)] + [("KA", tb) for tb in range(4)]

    def vt_project_pair(l, sls):
        for tb in range(4):
            bank = pj_bank()
            for c in ALLC:
                wv = WR[:, sls[c // 4], :].rearrange("p (c f) -> p c f", f=128)
                S.op("pe", (lambda e, c=c, bank=bank, tb=tb, wv=wv: e.matmul(PS[:, bank, :], lhsT=wv[:, c % 4, :], rhs=XT[:, c, tb_sl(tb)], start=(c == 0), stop=(c == 7))),
                     r=[("W", sls[c // 4])] + xt_res([c], [tb]), w=[("PS", bank)])
            S.op("dve", (lambda e, bank=bank, tb=tb: e.tensor_tensor(out=VT[:, tb_sl(tb)], in0=PS[:, bank, :], in1=RB[:, tb_sl(tb)], op=ALU.mult)),
                 r=[("PS", bank), ("RB", tb)], w=[("VT", tb)])

    def ve_build(b, hs):
        for t8 in range(2):
            bank = pj_bank()
            for tt in range(8):
                u = t8 * 8 + tt
                S.op("pe", (lambda e, u=u, tt=tt, bank=bank: e.matmul(PS[:, bank, tt * 64:(tt + 1) * 64], lhsT=VT[hs * 64:hs * 64 + 64, tcols(b, u)], rhs=IDENT[hs * 64:hs * 64 + 64, hs * 64:hs * 64 + 64], start=True, stop=True)),
                     r=[("VT", tb) for tb in range(4)] + [("CN",)], w=[("PS", bank)])
            S.op("act", (lambda e, t8=t8, bank=bank: e.activation(out=VE[:, t8 * 8:(t8 + 1) * 8, 0:64], in_=PS[:, bank, :].rearrange("p (t f) -> p t f", f=64), func=AF.Copy)),
                 r=[("PS", bank)], w=[("VE", u) for u in range(t8 * 8, t8 * 8 + 8)])

    gctr = {"n": 0}

    def normalize_to_mt(src, srckey, chunk, rows, tb):
        S.op("pe", (lambda e: e.matmul(PS[:, 6, :], lhsT=SWAP, rhs=src, start=True, stop=True)),
             r=[srckey, ("CN",)], w=[("PS", 6)])
        S.op("dve", (lambda e: e.reciprocal(out=RCP[0:64, :], in_=PS[0:64, 6, :])), r=[("PS", 6)], w=[("RCP",)])
        S.op("dve", (lambda e: e.tensor_tensor(out=MT[rows, chunk, tb_sl(tb)], in0=src[0:64, :], in1=RCP[0:64, :], op=ALU.mult)),
             r=[srckey, ("RCP",)], w=[("MT", chunk, tb)])

    def branch_A(l, h, bi, b):
        S.dma("sp", TAB, ta_d[h][:, bi * 512:(bi + 1) * 512], ("tab",), w=[("TAB",)])
        ve_build(b, h % 2)
        if b == 16:
            groups = [[(u + i, None) for i in range(4)] for u in range(0, 16, 4)]
        else:
            groups = []
            for u in range(0, 16, 2):
                g_ = []
                for uu in (u, u + 1):
                    has_prev = (uu > 0) if b == 1 else (uu % 4 > 0)
                    g_.append((uu, uu - 1 if has_prev else None))
                groups.append(g_)
        n0 = gctr["n"]
        gctr["n"] += len(groups)

        def score_phase(gi):
            g_ = groups[gi]
            n = n0 + gi
            sbk, pt, ptk = 2 + n % 2, PT[n % 2], ("PT", n % 2)
            if b == 16:
                for i, (uu, _) in enumerate(g_):
                    S.op("pe", (lambda e, i=i, uu=uu: e.matmul(PS[:, sbk, i * 128:(i + 1) * 128], lhsT=KA[0:64, tcols(b, uu)], rhs=QA[0:64, tcols(b, uu)], start=True, stop=True)),
                         r=QK_ALL, w=[("PS", sbk)])
            else:
                (u0_, pv0), (u1_, _) = g_
                c0, c1 = tcols(b, u0_), tcols(b, u1_)
                both = slice(c0.start, c1.stop, c0.step)
                k0 = pv0 if pv0 is not None else u0_
                S.op("pe", (lambda e: e.matmul(PS[:, sbk, 0:128], lhsT=KA[0:64, tcols(b, k0)], rhs=QA[0:64, c0], start=True, stop=True)),
                     r=QK_ALL, w=[("PS", sbk)])
                S.op("pe", (lambda e: e.matmul(PS[:, sbk, 128:384], lhsT=KA[0:64, c0], rhs=QA[0:64, both], start=True, stop=True)),
                     r=QK_ALL, w=[("PS", sbk)])
                S.op("pe", (lambda e: e.matmul(PS[:, sbk, 384:512], lhsT=KA[0:64, c1], rhs=QA[0:64, c1], start=True, stop=True)),
                     r=QK_ALL, w=[("PS", sbk)])
            S.op("dve", (lambda e: e.tensor_tensor(out=pt, in0=PS[:, sbk, :], in1=TAB[:, 0:512], op=ALU.add)),
                 r=[("PS", sbk), ("TAB",)], w=[ptk])
            S.op("act", (lambda e: e.activation(out=pt, in_=pt, func=AF.Exp)), r=[ptk], w=[ptk])

        def pv_phase(gi):
            g_ = groups[gi]
            n = n0 + gi
            obk, pt, ptk = 4 + n % 2, PT[n % 2], ("PT", n % 2)
            if b == 16:
                for i, (uu, _) in enumerate(g_):
                    S.op("pe", (lambda e, i=i, uu=uu: e.matmul(PS[:, obk, i * 128:(i + 1) * 128], lhsT=VE[:, uu, :], rhs=pt[:, i * 128:(i + 1) * 128], start=True, stop=True)),
                         r=[ptk, ("VE", uu)], w=[("PS", obk)])
            else:
                (u0_, pv0), (u1_, _) = g_
                S.op("pe", (lambda e: e.matmul(PS[:, obk, 0:256], lhsT=VE[:, u0_, :], rhs=pt[:, 128:384], start=True, stop=False)),
                     r=[ptk, ("VE", u0_)], w=[("PS", obk)])
                if pv0 is not None:
                    S.op("pe", (lambda e: e.matmul(PS[:, obk, 0:128], lhsT=VE[:, pv0, :], rhs=pt[:, 0:128], start=False, stop=False)),
                         r=[ptk, ("VE", pv0)], w=[("PS", obk)])
                S.op("pe", (lambda e: e.matmul(PS[:, obk, 128:256], lhsT=VE[:, u1_, :], rhs=pt[:, 384:512], start=False, stop=True)),
                     r=[ptk, ("VE", u1_)], w=[("PS", obk)])
            u0 = g_[0][0]
            if b == 1:
                S.op("dve", (lambda e: e.tensor_copy(out=ACC[:, u0 * 128:(u0 + 2) * 128], in_=PS[:, obk, 0:256])),
                     r=[("PS", obk)], w=[("ACC",)])
            elif b == 4:
                r_, st = u0 // 4, u0 % 4
                st0 = 512 * st + r_
                dst = ACC[:, st0:st0 + 4 * 255 + 1:4]
                S.op("dve", (lambda e: e.tensor_tensor(out=dst, in0=PS[:, obk, 0:256], in1=dst, op=ALU.add)),
                     r=[("PS", obk), ("ACC",)], w=[("ACC",)])
            else:
                dst = ACC.rearrange("p (i r) -> p r i", r=16)[:, u0:u0 + 4, :]
                S.op("dve", (lambda e: e.tensor_tensor(out=dst, in0=PS[:, obk, :].rearrange("p (r i) -> p r i", i=128), in1=dst, op=ALU.add)),
                     r=[("PS", obk), ("ACC",)], w=[("ACC",)])

        score_phase(0)
        for gi in range(len(groups)):
            if gi + 1 < len(groups):
                score_phase(gi + 1)
            pv_phase(gi)

    def head_A(l, h):
        chunk, rows = h // 2, slice((h % 2) * 64, (h % 2) * 64 + 64)
        for bi, b in enumerate((1, 4, 16)):
            branch_A(l, h, bi, b)
        for tb in range(4):
            normalize_to_mt(ACC[:, tb_sl(tb)], ("ACC",), chunk, rows, tb)

    def head_C(l, h):
        chunk, rows = 5 + h // 2, slice((h % 2) * 64, (h % 2) * 64 + 64)
        ve_build(1, h % 2)
        S.op("dve", (lambda e: e.tensor_reduce(out=KM[0:64, 0:8], in_=KA[0:64, :].rearrange("p (n j) -> p n j", j=256), axis=AX.X, op=ALU.add)),
             r=QK_ALL, w=[("KM",)])
        for t in range(16):
            S.op("pe", (lambda e, t=t: e.matmul(PS[:, 7, t * 8:(t + 1) * 8], lhsT=QA[0:64, t * 128:(t + 1) * 128], rhs=KM[0:64, 0:8], start=True, stop=True)),
                 r=QK_ALL + [("KM",)], w=[("PS", 7)])
        S.op("dve", (lambda e: e.tensor_tensor(out=GM, in0=PS[:, 7, 0:128], in1=CN[:, C_PASTNEG:C_PASTNEG + 128], op=ALU.add)),
             r=[("PS", 7), ("CN",)], w=[("GM",)])
        for t in range(16):
            S.op("dve", (lambda e, t=t: e.max(out=TH[:, t * 8:(t + 1) * 8], in_=GM[:, t * 8:(t + 1) * 8])), r=[("GM",)], w=[("TH", t)])
            S.op("dve", (lambda e, t=t: e.tensor_single_scalar(out=NS[:, t * 8:(t + 1) * 8], in_=GM[:, t * 8:(t + 1) * 8], scalar=TH[:, t * 8 + 2:t * 8 + 3], op=ALU.is_lt)),
                 r=[("GM",), ("TH", t)], w=[("NS", t)])
        S.op("dve", (lambda e: e.tensor_tensor(out=MK, in0=NS, in1=CN[:, C_NEGPAST2:C_NEGPAST2 + 128], op=ALU.mult)),
             r=[("NS", t) for t in range(16)] + [("CN",)], w=[("MK",)])
        for tb in range(4):
            for tt in range(4):
                t = tb * 4 + tt
                S.op("pe", (lambda e, t=t, tt=tt: e.matmul(PS[64:72, 7, tt * 128:(tt + 1) * 128], lhsT=MK[:, t * 8:(t + 1) * 8], rhs=IDENT, start=True, stop=True)),
                     r=[("MK",), ("CN",)], w=[("PS", 7)])
            S.op("act", (lambda e, tb=tb: e.activation(out=QA[64:72, tb_sl(tb)], in_=PS[64:72, 7, :], func=AF.Copy)),
                 r=[("PS", 7)], w=[("QA", tb)])
        items = [(m, jt) for m in range(4) for jt in range(4 * (m + 1))]
        n0 = gctr["n"]
        gctr["n"] += len(items)

        def score_phase(k):
            m, jt = items[k]
            n = n0 + k
            sbk, pt, ptk = 2 + n % 2, PT[n % 2], ("PT", n % 2)
            S.op("pe", (lambda e: e.matmul(PS[:, sbk, :], lhsT=KA[0:72, jt * 128:(jt + 1) * 128], rhs=QA[0:72, tb_sl(m)], start=True, stop=True)),
                 r=QK_ALL + [("KA", "ind")], w=[("PS", sbk)])
            g0 = 512 * m - 128 * jt + 384
            S.op("dve", (lambda e: e.tensor_tensor(out=pt, in0=PS[:, sbk, :], in1=GT[:, g0:g0 + 512], op=ALU.add)),
                 r=[("PS", sbk), ("TAB",)], w=[ptk])
            S.op("act", (lambda e: e.activation(out=pt, in_=pt, func=AF.Exp)), r=[ptk], w=[ptk])

        def pv_phase(k):
            m, jt = items[k]
            n = n0 + k
            nk = 4 * (m + 1)
            obk, pt, ptk = 4 + m % 2, PT[n % 2], ("PT", n % 2)
            S.op("pe", (lambda e: e.matmul(PS[:, obk, :], lhsT=VE[:, jt, :], rhs=pt, start=(jt == 0), stop=(jt == nk - 1))),
                 r=[ptk, ("VE", jt)], w=[("PS", obk)])
            if jt == nk - 1:
                S.op("act", (lambda e: e.activation(out=OSB, in_=PS[:, obk, :], func=AF.Copy)), r=[("PS", obk)], w=[("OSB",)])
                normalize_to_mt(OSB, ("OSB",), chunk, rows, m)

        score_phase(0)
        for k in range(len(items)):
            if k + 1 < len(items):
                score_phase(k + 1)
            pv_phase(k)

    def mixer_AC(l):
        S.op("pool", (lambda e: e.memset(VE[:, :, 64:128], 1.0)), w=[("VE", u) for u in range(16)])
        S.dma("sp", KA[64:72, :], ind_d[:, :], ("ind",), w=[("KA", "ind")])
        win_v = win_d[l].rearrange("(c p) f -> p c f", p=128)

        def wl2(colsets):
            sls = []
            for hf in range(2):
                slot = wstate["n"] % NSLOT
                wstate["n"] += 1
                v = WR[:, slot, :].rearrange("p (c f) -> p c f", f=128)
                o = 0
                for (c0, w_) in colsets:
                    S.dma("sp", v[:, :, o:o + w_], win_v[:, hf * 4:(hf + 1) * 4, c0:c0 + w_], ("w", slot), w=[("W", slot)])
                    o += w_
                wscale(slot, 4, 128, P_NW1 + l * 8 + hf * 4)
                sls.append(slot)
            return sls

        heads = [(mixn, h) for mixn in ("A", "C") if not (dbg == "onlyA" and mixn == "C") for h in range(6)]

        def load_head(i):
            mixn, h = heads[i]
            qo, ko, vo = (OFF["cq"], OFF["ck"], OFF["cv"]) if mixn == "C" else (OFF["aq"], OFF["ak"], OFF["av"])
            sqk = wl2([(qo + h * 64, 64), (ko + h * 64, 64)])
            svv = wl2([(vo + h * 64, 128)]) if h % 2 == 0 else None
            return sqk, svv

        pend = load_head(0)
        svv_cur = None
        for i, (mixn, h) in enumerate(heads):
            isC = mixn == "C"
            sqk, svv = pend
            if svv is not None:
                svv_cur = svv
            if isC:
                S.dma("sp", GT, g_d[h], ("tab",), w=[("TAB",), ("ACC",)])
            for tb in range(4):
                bank = pj_bank()
                for c in ALLC:
                    wv = WR[:, sqk[c // 4], :].rearrange("p (c f) -> p c f", f=128)
                    S.op("pe", (lambda e, c=c, wv=wv, bank=bank, tb=tb: e.matmul(PS[:, bank, :], lhsT=wv[:, c % 4, :], rhs=XT[:, c, tb_sl(tb)], start=(c == 0), stop=(c == 7))),
                         r=[("W", sqk[c // 4])] + xt_res([c], [tb]), w=[("PS", bank)])
                S.op("dve", (lambda e, bank=bank, tb=tb: e.scalar_tensor_tensor(out=QA[0:64, tb_sl(tb)], in0=PS[0:64, bank, :], scalar=0.125, in1=RB[0:64, tb_sl(tb)], op0=ALU.mult, op1=ALU.mult)),
                     r=[("PS", bank), ("RB", tb)], w=[("QA", tb)])
                S.op("dve", (lambda e, bank=bank, tb=tb: e.tensor_tensor(out=KA[0:64, tb_sl(tb)], in0=PS[64:128, bank, :], in1=RB[64:128, tb_sl(tb)], op=ALU.mult)),
                     r=[("PS", bank), ("RB", tb)], w=[("KA", tb)])
            if h % 2 == 0:
                vt_project_pair(l, svv_cur)
            if i + 1 < len(heads):
                pend = load_head(i + 1)
            if isC:
                head_C(l, h)
            else:
                head_A(l, h)

    def mixer_B(l):
        win_v = win_d[l].rearrange("(c p) f -> p c f", p=128)
        b0 = OFF["bq"]
        S.dma("sp", BW[:, 0:4, :], win_v[:, 0:4, b0:b0 + 784], ("bw",), w=[("BW",)])
        S.dma("sp", BW[:, 4:8, :], win_v[:, 4:8, b0:b0 + 784], ("bw",), w=[("BW",)])
        bwsc = PR[:, P_NW1 + l * 8:P_NW1 + l * 8 + 8].rearrange("p (c o) -> p c o", o=1).broadcast_to([128, 8, 784])
        S.op("dve", (lambda e: e.tensor_tensor(out=BW, in0=BW, in1=bwsc, op=ALU.mult)), r=[("PR",)], w=[("BW",)])
        S.op("pool", (lambda e: e.memset(SALL, 0.0)), w=[("SALL",)])
        i16 = 1.0 / 16.0
        for tb in range(4):
            def proj(cols, M, dst, key, tb=tb):
                bank = pj_bank()
                for c in ALLC:
                    S.op("pe", (lambda e, c=c, bank=bank: e.matmul(PS[0:M, bank, :], lhsT=BW[:, c, cols], rhs=XT[:, c, tb_sl(tb)], start=(c == 0), stop=(c == 7))),
                         r=[("BW",)] + xt_res([c], [tb]), w=[("PS", bank)])
                S.op("dve", (lambda e, bank=bank: e.tensor_tensor(out=dst[0:M, :], in0=PS[0:M, bank, :], in1=RB[0:M, tb_sl(tb)], op=ALU.mult)),
                     r=[("PS", bank), ("RB", tb)], w=[key])
            proj(slice(0, 128), 128, BQ, ("BQ",))
            proj(slice(128, 256), 128, BK, ("BK",))
            for rc in range(2):
                proj(slice(512 + rc * 128, 512 + (rc + 1) * 128), 128, BR[rc], ("BR", rc))
                S.op("act", (lambda e, rc=rc: e.activation(out=BR[rc], in_=BR[rc], func=AF.Silu)), r=[("BR", rc)], w=[("BR", rc)])
            proj(slice(768, 784), 16, ALR, ("ALR",))
            for ch in range(4):
                chunk_B(l, tb, ch)

    def chunk_B(l, tb, ch):
        i16 = 1.0 / 16.0
        g = tb * 4 + ch
        t0 = g * 128
        csl = slice(ch * 128, (ch + 1) * 128)
        bank = pj_bank()
        for c in ALLC:
            S.op("pe", (lambda e, c=c: e.matmul(PS[:, bank, 0:256], lhsT=XT[:, c, t0:t0 + 128], rhs=BW[:, c, 256:512], start=(c == 0), stop=(c == 7))),
                 r=[("BW",)] + xt_res([c], [tb]), w=[("PS", bank)])
        S.op("act", (lambda e: e.activation(out=BV, in_=PS[:, bank, 0:256], func=AF.Copy, scale=RT[:, g:g + 1])),
             r=[("PS", bank), ("RT",)], w=[("BV",)])
        S.op("pe", (lambda e: e.matmul(PS[:, 2, 0:128], lhsT=ALR[0:16, csl], rhs=WA2[0:16, l * 128:(l + 1) * 128], start=True, stop=False)),
             r=[("ALR",), ("WA2",)], w=[("PS", 2)])
        S.op("pe", (lambda e: e.matmul(PS[:, 2, 0:128], lhsT=ONES[0:1, 0:128], rhs=BA[0:1, l * 128:(l + 1) * 128], start=False, stop=True)),
             r=[("CN",), ("BA",)], w=[("PS", 2)])
        S.op("act", (lambda e: e.activation(out=SP_, in_=PS[:, 2, 0:128], func=AF.Exp, scale=-1.0)), r=[("PS", 2)], w=[("SP",)])
        S.op("act", (lambda e: e.activation(out=SP_, in_=SP_, func=AF.Ln, bias=1.0, scale=1.0)), r=[("SP",)], w=[("SP",)])
        S.op("pe", (lambda e: e.matmul(PS[:, 3, 0:128], lhsT=SP_, rhs=TRI, start=True, stop=True)), r=[("SP",), ("CN",)], w=[("PS", 3)])
        S.op("act", (lambda e: e.activation(out=EB, in_=PS[:, 3, 0:128], func=AF.Exp, scale=-i16)), r=[("PS", 3)], w=[("EB",)])
        S.op("act", (lambda e: e.activation(out=KG, in_=PS[:, 3, 0:128], func=AF.Exp, scale=i16)), r=[("PS", 3)], w=[("KG",)])
        S.op("dve", (lambda e: e.tensor_tensor(out=KG, in0=KG, in1=BK[:, csl], op=ALU.mult)), r=[("KG",), ("BK",)], w=[("KG",)])
        for h in range(4):
            S.op("dve", (lambda e, h=h: e.scalar_tensor_tensor(out=QGM[:, h * 128:(h + 1) * 128], in0=BQ[:, csl], scalar=CN[:, C_HMS + h:C_HMS + h + 1], in1=EB, op0=ALU.mult, op1=ALU.mult)),
                 r=[("BQ",), ("EB",), ("CN",)], w=[("QGM", h)])
        S.op("pe", (lambda e: e.matmul(PS[:, 2, 128:256], lhsT=KG, rhs=IDENT, start=True, stop=True)), r=[("KG",), ("CN",)], w=[("PS", 2)])
        S.op("act", (lambda e: e.activation(out=KGT, in_=PS[:, 2, 128:256], func=AF.Copy)), r=[("PS", 2)], w=[("KGT",)])
        for h in range(4):
            S.op("pe", (lambda e, h=h: e.matmul(PS[:, 4, h * 128:(h + 1) * 128], lhsT=KG, rhs=QGM[:, h * 128:(h + 1) * 128], start=True, stop=True)),
                 r=[("KG",), ("QGM", h)], w=[("PS", 4)])
        for h in range(4):
            S.op("dve", (lambda e, h=h: e.tensor_tensor(out=AT[:, h * 128:(h + 1) * 128], in0=PS[:, 4, h * 128:(h + 1) * 128], in1=TRI, op=ALU.mult)),
                 r=[("PS", 4), ("CN",)], w=[("AT", h)])
        for h in range(4):
            oap = PS[(h % 2) * 64:(h % 2) * 64 + 64, 5, (h // 2) * 128:(h // 2 + 1) * 128]
            S.op("pe", (lambda e, h=h, oap=oap: e.matmul(oap, lhsT=BV[:, h * 64:(h + 1) * 64], rhs=AT[:, h * 128:(h + 1) * 128], start=True, stop=False)),
                 r=[("BV",), ("AT", h)], w=[("PS", 5)])
            S.op("pe", (lambda e, h=h, oap=oap: e.matmul(oap, lhsT=SALL, rhs=QGM[:, h * 128:(h + 1) * 128], start=False, stop=True)),
                 r=[("SALL",), ("QGM", h)], w=[("PS", 5)])
        S.op("pe", (lambda e: e.matmul(PS[:, 3, 128:384], lhsT=KGT, rhs=BV, start=True, stop=True)), r=[("KGT",), ("BV",)], w=[("PS", 3)])
        for h in range(4):
            S.op("dve", (lambda e, h=h: e.scalar_tensor_tensor(out=SALL, in0=PS[:, 3, 128 + h * 64:128 + (h + 1) * 64], scalar=CN[:, C_HM + h:C_HM + h + 1], in1=SALL, op0=ALU.mult, op1=ALU.add)),
                 r=[("PS", 3), ("SALL",), ("CN",)], w=[("SALL",)])
        S.op("dve", (lambda e: e.tensor_scalar_mul(out=SALL, in0=SALL, scalar1=EB[:, 127:128])), r=[("SALL",), ("EB",)], w=[("SALL",)])
        S.op("act", (lambda e: e.activation(out=OSQ, in_=PS[:, 5, 0:256], func=AF.Square)), r=[("PS", 5)], w=[("OSQ",)])
        S.op("pe", (lambda e: e.matmul(PS[:, 6, 0:256], lhsT=BLK, rhs=OSQ, start=True, stop=True)), r=[("OSQ",), ("CN",)], w=[("PS", 6)])
        S.op("act", (lambda e: e.activation(out=RS, in_=PS[:, 6, 0:256], func=AF.Ln, bias=EPS_AP, scale=1.0 / 64.0)), r=[("PS", 6), ("CN",)], w=[("RS",)])
        S.op("act", (lambda e: e.activation(out=RS, in_=RS, func=AF.Exp, scale=-0.5)), r=[("RS",)], w=[("RS",)])
        S.op("dve", (lambda e: e.tensor_tensor(out=OSQ, in0=PS[:, 5, 0:256], in1=RS, op=ALU.mult)), r=[("PS", 5), ("RS",), ("OSQ",)], w=[("OSQ",)])
        for rc in range(2):
            S.op("dve", (lambda e, rc=rc: e.scalar_tensor_tensor(out=MT[:, 3 + rc, t0:t0 + 128], in0=OSQ[:, rc * 128:(rc + 1) * 128], scalar=PR[:, P_GNW + l * 2 + rc:P_GNW + l * 2 + rc + 1], in1=BR[rc][:, csl], op0=ALU.mult, op1=ALU.mult)),
                 r=[("OSQ",), ("PR",), ("BR", rc)], w=[("MT", 3 + rc, tb)])

    def ffn_block(s, l, tb, last):
        norm_stats(tb, 6 + tb % 2)
        for c in ALLC:
            S.op("dve", (lambda e, c=c: e.scalar_tensor_tensor(out=H2[:, c, :], in0=XT[:, c, tb_sl(tb)], scalar=PR[:, P_NW2 + l * 8 + c:P_NW2 + l * 8 + c + 1], in1=RB[:, tb_sl(tb)], op0=ALU.mult, op1=ALU.mult)),
                 r=xt_res([c], [tb]) + [("RB", tb), ("PR",)], w=[("H2", c)])
        for fc in range(32):
            sl = []
            for hf in range(2):
                src = wff1_d[l].rearrange("(c p) f -> p c f", p=128)[:, hf * 4:(hf + 1) * 4, fc * 128:(fc + 1) * 128]
                sl.append(wload(lambda v: v.rearrange("p (c f) -> p c f", f=128), src))
            bank = pj_bank()
            for c in ALLC:
                wv = WR[:, sl[c // 4], :].rearrange("p (c f) -> p c f", f=128)
                S.op("pe", (lambda e, c=c, wv=wv, bank=bank: e.matmul(PS[:, bank, :], lhsT=wv[:, c % 4, :], rhs=H2[:, c, :], start=(c == 0), stop=(c == 7))),
                     r=[("W", sl[c // 4]), ("H2", c)], w=[("PS", bank)])
            av = MT[:, fc // 4, (fc % 4) * 512:(fc % 4 + 1) * 512]
            S.op("act", (lambda e, av=av, bank=bank: e.activation(out=av, in_=PS[:, bank, :], func=AF.Relu)),
                 r=[("PS", bank)], w=[("MT", fc // 4, fc % 4)])
            S.op("pool", (lambda e, av=av: e.tensor_tensor(out=av, in0=av, in1=av, op=ALU.mult)),
                 r=[("MT", fc // 4, fc % 4)], w=[("MT", fc // 4, fc % 4)])
        for half in range(2):
            for fc in range(32):
                src = wff2_d[l][fc * 128:(fc + 1) * 128, half * 512:(half + 1) * 512]
                slot = wload(lambda v: v, src)
                av = MT[:, fc // 4, (fc % 4) * 512:(fc % 4 + 1) * 512]
                for o4 in range(4):
                    S.op("pe", (lambda e, slot=slot, o4=o4, av=av, fc=fc: e.matmul(PS[:, 2 + o4, :], lhsT=WR[:, slot, o4 * 128:(o4 + 1) * 128], rhs=av, start=(fc == 0), stop=(fc == 31))),
                         r=[("W", slot), ("MT", fc // 4, fc % 4)], w=[("PS", 2 + o4)])
            for o4 in range(4):
                oc = half * 4 + o4
                S.op("dve", (lambda e, oc=oc, o4=o4: e.tensor_tensor(out=XT[:, oc, tb_sl(tb)], in0=PS[:, 2 + o4, :], in1=XT[:, oc, tb_sl(tb)], op=ALU.add)),
                     r=[("PS", 2 + o4)] + xt_res([oc], [tb]), w=xt_res([oc], [tb]))

        if last:
            norm_stats(tb, 6 + tb % 2)
            for tt in range(4):
                t = tb * 4 + tt
                ob = OUTT[t % 2]
                for c in ALLC:
                    S.op("dve", (lambda e, c=c, t=t: e.scalar_tensor_tensor(out=YN[:, c * 128:(c + 1) * 128], in0=XT[:, c, t * 128:(t + 1) * 128], scalar=PR[:, P_FW + c:P_FW + c + 1], in1=RB[:, t * 128:(t + 1) * 128], op0=ALU.mult, op1=ALU.mult)),
                         r=xt_res([c], [tb]) + [("RB", tb), ("PR",)], w=[("WK", "yn", c)])
                    S.op("pe", (lambda e, c=c: e.matmul(PS[:, c // 4, (c % 4) * 128:(c % 4 + 1) * 128], lhsT=YN[:, c * 128:(c + 1) * 128], rhs=IDENT, start=True, stop=True)),
                         r=[("WK", "yn", c), ("CN",)], w=[("PS", c // 4)])
                for hb in range(2):
                    S.op("act", (lambda e, hb=hb, ob=ob: e.activation(out=ob[:, hb * 512:(hb + 1) * 512], in_=PS[:, hb, :], func=AF.Copy)),
                         r=[("PS", hb)], w=[("WK", "out", t % 2)])
                S.dma("sp", out_d[s, t * 128:(t + 1) * 128, :], ob, ("out", t % 2), r=[("WK", "out", t % 2)], w=[("OUTD",)])


    for s in range(nseq):
        S.barrier()
        for t in range(16):
            xin = XIN[t % 2]
            S.dma("sp", xin, x_d[s, t * 128:(t + 1) * 128, :], ("xin", t % 2), w=[("WK", "xin", t % 2)])
            for c in ALLC:
                S.op("pe", (lambda e, c=c, xin=xin: e.matmul(PS[:, 6 + c // 4, (c % 4) * 128:(c % 4 + 1) * 128], lhsT=xin[:, c * 128:(c + 1) * 128], rhs=IDENT, start=True, stop=True)),
                     r=[("WK", "xin", t % 2), ("CN",)], w=[("PS", 6 + c // 4)])
            for hb in range(2):
                S.op("dve", (lambda e, t=t, hb=hb: e.tensor_copy(out=XT[:, hb * 4:(hb + 1) * 4, t * 128:(t + 1) * 128], in_=PS[:, 6 + hb, :].rearrange("p (k t) -> p k t", t=128))),
                     r=[("PS", 6 + hb)], w=xt_res(range(hb * 4, hb * 4 + 4), [t // 4]))

        for l in range(nlayers):
            last = (l == nlayers - 1)
            for tb in range(4):
                norm_stats(tb, 6 + tb % 2)
            for u in range(16):
                bank = 6 + u % 2
                S.op("pe", (lambda e, u=u, bank=bank: e.matmul(PS[:, bank, 0:1], lhsT=RB[:, u * 128:(u + 1) * 128], rhs=IDENT[:, 0:1], start=True, stop=True)),
                     r=[("RB", u // 4), ("CN",)], w=[("PS", bank)])
                S.op("dve", (lambda e, u=u, bank=bank: e.tensor_copy(out=RT[:, u:u + 1], in_=PS[:, bank, 0:1])),
                     r=[("PS", bank)], w=[("RT",)])
            S.barrier()
            if dbg in ("skipmix", "noB", "onlyA", "onlyB"):
                S.op("pool", (lambda e: e.memset(MT[:, :, :], 0.0)), w=[("MT", c, tb) for c in ALLC for tb in range(4)])
            if dbg != "skipmix":
                if dbg not in ("noB", "onlyA"):
                    mixer_B(l)
                S.barrier()
                if dbg != "onlyB":
                    mixer_AC(l)
            S.barrier()

            for oc in range(NCH):
                sl = []
                for hf in range(2):
                    src = wout_d[l].rearrange("(c p) f -> p c f", p=128)[:, hf * 4:(hf + 1) * 4, oc * 128:(oc + 1) * 128]
                    sl.append(wload(lambda v: v.rearrange("p (c f) -> p c f", f=128), src))
                for tb in range(4):
                    bank = pj_bank()
                    for mc in ALLC:
                        wv = WR[:, sl[mc // 4], :].rearrange("p (c f) -> p c f", f=128)
                        S.op("pe", (lambda e, mc=mc, wv=wv, bank=bank, tb=tb: e.matmul(PS[:, bank, :], lhsT=wv[:, mc % 4, :], rhs=MT[:, mc, tb_sl(tb)], start=(mc == 0), stop=(mc == 7))),
                             r=[("W", sl[mc // 4]), ("MT", mc, tb)], w=[("PS", bank)])
                    S.op("dve", (lambda e, oc=oc, bank=bank, tb=tb: e.tensor_tensor(out=XT[:, oc, tb_sl(tb)], in0=PS[:, bank, :], in1=XT[:, oc, tb_sl(tb)], op=ALU.add)),
                         r=[("PS", bank)] + xt_res([oc], [tb]), w=xt_res([oc], [tb]))

            for tb in range(4):
                ffn_block(s, l, tb, last)

    S.op("sp", (lambda e: e.nop()), r=[("WK", "out", 0), ("WK", "out", 1)], w=[("WK", "out", 0), ("WK", "out", 1)])
    S.emit(nc, stack)
    stack.close()
    return nc


def rel_bucket_np(d):
    n = np.maximum(d, 0)
    exact = 16
    logv = np.log(np.maximum(n, 1).astype(np.float32) / np.float32(exact)) / np.float32(math.log(2048 / exact))
    large = np.minimum(exact + (logv.astype(np.float32) * np.float32(32 - exact)).astype(np.int32), 31)
    return np.where(n < exact, n, large)


def host_tables(rel_bias):
    rel_bias = np.asarray(rel_bias, np.float32)
    NEG = np.float32(-BIG)
    jj = np.arange(128)[:, None]
    ii = np.arange(128)[None, :]
    ta = np.zeros((6, 128, 1536), np.float32)
    for h in range(6):
        def tile(dil, prev):
            d = ii - jj + (128 if prev else 0)
            valid = (d <= 128) if prev else (d >= 0)
            v = rel_bias[rel_bucket_np(np.maximum(d, 0) * dil), h]
            return np.where(valid, v, NEG).astype(np.float32)
        t1 = np.concatenate([tile(1, True), tile(1, False)], 1)
        t2 = np.concatenate([tile(4, True), tile(4, False)], 1)
        t3 = tile(16, False)
        ta[h] = np.concatenate([t1, t1, t2, t2, t3, t3, t3, t3], 1)
    m = np.arange(GW)[None, :]
    d = m - jj - 384
    g = np.zeros((6, 128, GW), np.float32)
    for h in range(6):
        v = rel_bias[rel_bucket_np(np.maximum(d, 0)), 6 + h]
        g[h] = np.where(d >= 0, v, NEG)
    return ta, g


def host_consts():
    c = np.zeros((128, NCONST), np.float32)
    k = np.arange(128)[:, None]
    m = np.arange(128)[None, :]
    c[:, C_IDENT:C_IDENT + 128] = (k == m)
    c[:, C_ONES:C_ONES + 128] = 1.0
    c[:, C_TRI:C_TRI + 128] = (k <= m)
    c[:, C_SWAP:C_SWAP + 128] = (k == (m + 64) % 128)
    c[:, C_BLK:C_BLK + 128] = (k // 64 == m // 64)
    for t in range(16):
        b = t // 2
        for n in range(8):
            c[:, C_PASTNEG + t * 8 + n] = -1e30 if n >= b else 0.0
            c[:, C_NEGPAST2 + t * 8 + n] = -BIG if n < b else 0.0
    for h in range(4):
        c[:, C_HM + h] = (np.arange(128) // 32 == h)
        c[:, C_HMS + h] = (np.arange(128) // 32 == h) * (32 ** -0.5)
    c[:, NCONST - 1] = EPS
    ind = np.zeros((8, S_LEN), np.float32)
    for n in range(8):
        ind[n, n * 256:(n + 1) * 256] = 1.0
    return c, ind


_CACHE = {}


def kernel(x, norm1_w, w_in, gla_w_a2, gla_b_a, gla_norm_w, w_out, norm2_w, w_ff1, w_ff2, rel_bias, final_norm_w):
    ncores = 8
    x = np.ascontiguousarray(np.asarray(x, np.float32))
    nseq = x.shape[0] // ncores
    ta, g = host_tables(rel_bias)
    consts, ind = host_consts()
    par = np.zeros((128, NPAR), np.float32)
    n1 = np.asarray(norm1_w, np.float32).reshape(2, 8, 128)
    n2 = np.asarray(norm2_w, np.float32).reshape(2, 8, 128)
    for l in range(2):
        par[:, P_NW1 + l * 8:P_NW1 + l * 8 + 8] = n1[l].T
        par[:, P_NW2 + l * 8:P_NW2 + l * 8 + 8] = n2[l].T
        par[:, P_GNW + l * 2:P_GNW + l * 2 + 2] = np.asarray(gla_norm_w, np.float32)[l].reshape(2, 128).T
    par[:, P_FW:P_FW + 8] = np.asarray(final_norm_w, np.float32).reshape(8, 128).T
    wa2 = np.ascontiguousarray(np.asarray(gla_w_a2, np.float32).transpose(1, 0, 2).reshape(16, 256))
    ba = np.ascontiguousarray(np.asarray(gla_b_a, np.float32).reshape(1, 256))
    if "nc" not in _CACHE:
        _CACHE["nc"] = build_program(nseq)
    nc = _CACHE["nc"]
    shared = dict(w_in=np.ascontiguousarray(np.asarray(w_in, np.float32)), w_out=np.ascontiguousarray(np.asarray(w_out, np.float32)),
                  w_ff1=np.ascontiguousarray(np.asarray(w_ff1, np.float32)), w_ff2=np.ascontiguousarray(np.asarray(w_ff2, np.float32)),
                  ta=ta, gtab=g, consts=consts, ind=ind, params=par, wa2=wa2, ba=ba)
    in_maps = []
    for i in range(ncores):
        m = dict(shared)
        m["x"] = x[i * nseq:(i + 1) * nseq]
        in_maps.append(m)
    res = run_bass_kernel_spmd(nc, in_maps, core_ids=list(range(ncores)))
    return np.concatenate([r["out"] for r in res.results], axis=0)
```

```python
import math
import contextlib
import numpy as np
import concourse.bass as bass
import concourse.mybir as mybir
from concourse.bass_utils import run_bass_kernel_spmd

F32 = mybir.dt.float32
AF = mybir.ActivationFunctionType
ALU = mybir.AluOpType
AX = mybir.AxisListType

S_LEN = 2048
D = 1024
NCH = 8
DFF = 4096
IN_W = 3088
EPS = 1e-6
BIG = 30000.0
OFF = dict(aq=0, ak=384, av=768, bq=1152, bk=1280, bv=1408, br=1664, ba=1920, cq=1936, ck=2320, cv=2704)
C_IDENT, C_ONES, C_TRI, C_SWAP, C_BLK, C_PASTNEG, C_NEGPAST2, C_HM, C_HMS, NCONST = 0, 128, 256, 384, 512, 640, 768, 896, 900, 912
P_NW1, P_NW2, P_FW, P_GNW, NPAR = 0, 16, 32, 40, 48
GW = 2432
SAME_ENGINE_SYNC = True


class Sched:
    ENGS = ("pe", "act", "dve", "pool", "sp")

    def __init__(self):
        self.ops = {e: [] for e in self.ENGS}
        self.last_w = {}
        self.readers = {}
        self.dma_cnt = {}

    def _deps(self, reads, writes):
        deps = set()
        for r in reads:
            ev = self.last_w.get(r)
            if ev is not None:
                deps.add(ev)
        for w in writes:
            ev = self.last_w.get(w)
            if ev is not None:
                deps.add(ev)
            for ev in self.readers.get(w, {}).values():
                deps.add(ev)
        return deps

    def _record(self, ev, reads, writes):
        for r in reads:
            self.readers.setdefault(r, {})[ev[:2]] = ev
        for w in writes:
            self.last_w[w] = ev
            self.readers[w] = {}

    def op(self, eng, fn, r=(), w=()):
        deps = self._deps(r, w)
        ev = ("e", eng, len(self.ops[eng]))
        self.ops[eng].append((fn, deps, ev))
        self._record(ev, r, w)

    def dma(self, eng, out, in_, sem, r=(), w=()):
        deps = self._deps(r, w)
        self.dma_cnt[sem] = self.dma_cnt.get(sem, 0) + 1
        ev = ("d", sem, self.dma_cnt[sem])
        self.ops[eng].append((lambda e: e.dma_start(out=out, in_=in_), deps, ev))
        self._record(ev, r, w)

    def barrier(self):
        deps = set()
        for e in self.ENGS:
            if self.ops[e]:
                deps.add(self.ops[e][-1][2] if self.ops[e][-1][2][0] == "e" else None)
        deps.discard(None)
        for e in self.ENGS:
            for i in range(len(self.ops[e]) - 1, -1, -1):
                if self.ops[e][i][2][0] == "e":
                    deps.add(self.ops[e][i][2])
                    break
        for k, cnt in self.dma_cnt.items():
            deps.add(("d", k, cnt))
        for e in self.ENGS:
            ev = ("e", e, len(self.ops[e]))
            self.ops[e].append(((lambda eng: eng.nop()), set(deps), ev))

    def emit(self, nc, stack):
        marked = {e: set() for e in self.ENGS}
        for e in self.ENGS:
            for (_, deps, _) in self.ops[e]:
                for d in deps:
                    if d[0] == "e":
                        if d[1] == e and (e == "pe" or not SAME_ENGINE_SYNC):
                            continue
                        marked[d[1]].add(d[2])
        val = {}
        for e in self.ENGS:
            for i, idx in enumerate(sorted(marked[e])):
                val[(e, idx)] = i + 1
        sems = {e: stack.enter_context(nc.semaphore("s_" + e)) for e in self.ENGS}
        dsems = {}
        for k in self.dma_cnt:
            dsems[k] = stack.enter_context(nc.semaphore("d_" + "_".join(str(x) for x in k)))
        block = stack.enter_context(nc.Block())

        def replay(ename, eng):
            known = {}
            for (fn, deps, ev) in self.ops[ename]:
                need = {}
                for d in deps:
                    if d[0] == "e":
                        if d[1] == ename and (ename == "pe" or not SAME_ENGINE_SYNC):
                            continue
                        key, v = ("e", d[1]), val[(d[1], d[2])]
                    else:
                        key, v = ("d", d[1]), 16 * d[2]
                    if v > need.get(key, 0):
                        need[key] = v
                for key, v in need.items():
                    if known.get(key, 0) >= v:
                        continue
                    known[key] = v
                    eng.wait_ge(sems[key[1]] if key[0] == "e" else dsems[key[1]], v)
                ins = fn(eng)
                if ev[0] == "d":
                    ins.then_inc(dsems[ev[1]], 16)
                elif (ename, ev[2]) in val:
                    ins.then_inc(sems[ename], 1)

        block.tensor(lambda e: replay("pe", e))
        block.scalar(lambda e: replay("act", e))
        block.vector(lambda e: replay("dve", e))
        block.gpsimd(lambda e: replay("pool", e))
        block.sync(lambda e: replay("sp", e))


def build_program(nseq, nlayers=2, dbg=None):
    nc = bass.Bass("TRN2", target_bir_lowering=False)
    dt = lambda name, shape, kind="ExternalInput": nc.dram_tensor(name, shape, F32, kind=kind).ap()
    x_d = dt("x", [nseq, S_LEN, D])
    win_d = dt("w_in", [2, D, IN_W])
    wout_d = dt("w_out", [2, D, D])
    wff1_d = dt("w_ff1", [2, D, DFF])
    wff2_d = dt("w_ff2", [2, DFF, D])
    ta_d = dt("ta", [6, 128, 1536])
    g_d = dt("gtab", [6, 128, GW])
    const_d = dt("consts", [128, NCONST])
    ind_d = dt("ind", [8, S_LEN])
    par_d = dt("params", [128, NPAR])
    wa2_d = dt("wa2", [16, 256])
    ba_d = dt("ba", [1, 256])
    out_d = dt("out", [nseq, S_LEN, D], kind="ExternalOutput")

    S = Sched()
    stack = contextlib.ExitStack()
    sb = lambda name, shape: stack.enter_context(nc.sbuf_tensor(name, shape, F32))
    XT = sb("XT", [128, NCH, S_LEN])
    MT = sb("MT", [128, NCH, S_LEN])
    RB = sb("RB", [128, S_LEN])
    RT = sb("RT", [128, 16])
    CN = sb("CN", [128, NCONST])
    PR = sb("PR", [128, NPAR])
    WA2 = sb("WA2", [16, 256])
    BA = sb("BA", [1, 256])
    NSLOT = 6
    WR = sb("WR", [128, NSLOT, 512])
    WORKN = 13312
    WK = sb("WK", [128, WORKN])
    PS = stack.enter_context(nc.psum_tensor("PS", [128, 8, 512], F32))

    IDENT = CN[:, C_IDENT:C_IDENT + 128]
    ONES = CN[:, C_ONES:C_ONES + 128]
    TRI = CN[:, C_TRI:C_TRI + 128]
    SWAP = CN[:, C_SWAP:C_SWAP + 128]
    BLK = CN[:, C_BLK:C_BLK + 128]

    QA = WK[:, 0:2048]
    KA = WK[:, 2048:4096]
    VE = WK[:, 4096:6144].rearrange("p (t f) -> p t f", f=128)
    PT = [WK[:, 6144:6656], WK[:, 6656:7168], WK[:, 12288:12800]]
    SBK = [2, 3, 1]
    ACC = WK[:, 7168:9216]
    TAB = WK[:, 9216:9728]
    GT = WK[:, 7168:9600]
    KM = WK[:, 9600:9608]
    VT = WK[:, 9728:11776]
    MSC = WK[:, 11776:12800]
    RCP = MSC[:, 0:512]
    GM = MSC[:, 512:640]
    TH = MSC[:, 640:768]
    NS = MSC[:, 768:896]
    MK = MSC[:, 896:1024]
    OSB = WK[:, 12800:13312]
    H2 = WK[:, 7168:11264].rearrange("p (c t) -> p c t", t=512)
    BW = WK[:, 0:6272].rearrange("p (c f) -> p c f", f=784)
    BQ = WK[:, 6272:6784]
    BK = WK[:, 6784:7296]
    BR = [WK[:, 7296:7808], WK[:, 7808:8320]]
    ALR = WK[:, 8320:8832]
    BV = WK[:, 8832:9088]
    SP_ = WK[:, 9088:9216]
    EB = WK[:, 9216:9344]
    KG = WK[:, 9344:9472]
    QGM = WK[:, 9472:9984]
    KGT = WK[:, 9984:10112]
    AT = WK[:, 10112:10624]
    OSQ = WK[:, 10624:10880]
    RS = WK[:, 10880:11136]
    SALL = WK[:, 11136:11200]
    XIN = [WK[:, 0:1024], WK[:, 1024:2048]]
    YN = WK[:, 2048:3072]
    OUTT = [WK[:, 3072:4096], WK[:, 4096:5120]]
    SQ = [WK[:, 6144:6656], WK[:, 6656:7168]]
    WKR = ("WK",)

    def tb_sl(tb):
        return slice(tb * 512, (tb + 1) * 512)

    def xt_res(cs, tbs):
        return [("XT", c, tb) for c in cs for tb in tbs]

    ALLC = range(NCH)

    S.dma("sp", CN[:, :], const_d[:, :], ("c", 0), w=[("CN",)])
    S.dma("sp", PR[:, :], par_d[:, :], ("c", 1), w=[("PR",)])
    S.dma("sp", WA2[:, :], wa2_d[:, :], ("c", 2), w=[("WA2",)])
    S.dma("sp", BA[:, :], ba_d[:, :], ("c", 3), w=[("BA",)])

    wstate = {"n": 0}

    def wload(dst_view_fn, src_ap):
        slot = wstate["n"] % NSLOT
        wstate["n"] += 1
        S.dma("sp", dst_view_fn(WR[:, slot, :]), src_ap, ("w", slot), w=[("W", slot)])
        return slot

    def wscale(slot, ncs, fcols, pcol0):
        v = WR[:, slot, :].rearrange("p (c f) -> p c f", f=fcols)
        sc = PR[:, pcol0:pcol0 + ncs].rearrange("p (c o) -> p c o", o=1).broadcast_to([128, ncs, fcols])
        S.op("dve", (lambda e: e.tensor_tensor(out=v, in0=v, in1=sc, op=ALU.mult)), r=[("PR",)], w=[("W", slot)])

    pj = {"n": 0}

    def pj_bank():
        b = pj["n"] % 2
        pj["n"] += 1
        return b

    def norm_stats(tb, bank):
        for c in ALLC:
            sq = SQ[c % 2]
            S.op("act", (lambda e, c=c, sq=sq: e.activation(out=sq, in_=XT[:, c, tb_sl(tb)], func=AF.Square)),
                 r=xt_res([c], [tb]), w=[("PT", c % 2)])
            S.op("pe", (lambda e, c=c, sq=sq: e.matmul(PS[:, bank, :], lhsT=ONES, rhs=sq, start=(c == 0), stop=(c == 7))),
                 r=[("PT", c % 2), ("CN",)], w=[("PS", bank)])
        S.op("act", (lambda e: e.activation(out=RB[:, tb_sl(tb)], in_=PS[:, bank, :], func=AF.Ln, bias=EPS_AP, scale=1.0 / D)),
             r=[("PS", bank), ("CN",)], w=[("RB", tb)])
        S.op("act", (lambda e: e.activation(out=RB[:, tb_sl(tb)], in_=RB[:, tb_sl(tb)], func=AF.Exp, scale=-0.5)),
             r=[("RB", tb)], w=[("RB", tb)])

    EPS_AP = CN[:, NCONST - 1:NCONST]


    def tcols(b, u):
        if b == 1:
            return slice(u * 128, (u + 1) * 128)
        if b == 4:
            r_, st = u // 4, u % 4
            st0 = 512 * st + r_
            return slice(st0, st0 + 4 * 127 + 1, 4)
        return slice(u, u + 16 * 127 + 1, 16)

    QK_ALL = [("QA", tb) for tb in range(4)] + [("KA", tb) for tb in range(4)]

    def vt_project_pair(l, sls):
        for tb in range(4):
            bank = pj_bank()
            for c in ALLC:
                wv = WR[:, sls[c // 4], :].rearrange("p (c f) -> p c f", f=128)
                S.op("pe", (lambda e, c=c, bank=bank, tb=tb, wv=wv: e.matmul(PS[:, bank, :], lhsT=wv[:, c % 4, :], rhs=XT[:, c, tb_sl(tb)], start=(c == 0), stop=(c == 7))),
                     r=[("W", sls[c // 4])] + xt_res([c], [tb]), w=[("PS", bank)])
            S.op("dve", (lambda e, bank=bank, tb=tb: e.tensor_tensor(out=VT[:, tb_sl(tb)], in0=PS[:, bank, :], in1=RB[:, tb_sl(tb)], op=ALU.mult)),
                 r=[("PS", bank), ("RB", tb)], w=[("VT", tb)])

    def ve_build(b, hs):
        for t8 in range(2):
            bank = pj_bank()
            for tt in range(8):
                u = t8 * 8 + tt
                S.op("pe", (lambda e, u=u, tt=tt, bank=bank: e.matmul(PS[:, bank, tt * 64:(tt + 1) * 64], lhsT=VT[hs * 64:hs * 64 + 64, tcols(b, u)], rhs=IDENT[hs * 64:hs * 64 + 64, hs * 64:hs * 64 + 64], start=True, stop=True)),
                     r=[("VT", tb) for tb in range(4)] + [("CN",)], w=[("PS", bank)])
            S.op("act", (lambda e, t8=t8, bank=bank: e.activation(out=VE[:, t8 * 8:(t8 + 1) * 8, 0:64], in_=PS[:, bank, :].rearrange("p (t f) -> p t f", f=64), func=AF.Copy)),
                 r=[("PS", bank)], w=[("VE", u) for u in range(t8 * 8, t8 * 8 + 8)])

    gctr = {"n": 0}

    def normalize_to_mt(src, srckey, chunk, rows, tb):
        S.op("pe", (lambda e: e.matmul(PS[:, 6, :], lhsT=SWAP, rhs=src, start=True, stop=True)),
             r=[srckey, ("CN",)], w=[("PS", 6)])
        S.op("dve", (lambda e: e.reciprocal(out=RCP[0:64, :], in_=PS[0:64, 6, :])), r=[("PS", 6)], w=[("RCP",)])
        S.op("dve", (lambda e: e.tensor_tensor(out=MT[rows, chunk, tb_sl(tb)], in0=src[0:64, :], in1=RCP[0:64, :], op=ALU.mult)),
             r=[srckey, ("RCP",)], w=[("MT", chunk, tb)])

    def branch_A(l, h, bi, b, cb=None):
        S.dma("sp", TAB, ta_d[h][:, bi * 512:(bi + 1) * 512], ("tab",), w=[("TAB",)])
        ve_build(b, h % 2)
        if cb is not None:
            cb()
        if b == 16:
            groups = [[(u + i, None) for i in range(4)] for u in range(0, 16, 4)]
        else:
            groups = []
            for u in range(0, 16, 2):
                g_ = []
                for uu in (u, u + 1):
                    has_prev = (uu > 0) if b == 1 else (uu % 4 > 0)
                    g_.append((uu, uu - 1 if has_prev else None))
                groups.append(g_)
        n0 = gctr["n"]
        gctr["n"] += len(groups)

        def score_phase(gi):
            g_ = groups[gi]
            n = n0 + gi
            sbk, pt, ptk = SBK[n % 3], PT[n % 3], ("PT", n % 3)
            if b == 16:
                for i, (uu, _) in enumerate(g_):
                    S.op("pe", (lambda e, i=i, uu=uu: e.matmul(PS[:, sbk, i * 128:(i + 1) * 128], lhsT=KA[0:64, tcols(b, uu)], rhs=QA[0:64, tcols(b, uu)], start=True, stop=True)),
                         r=QK_ALL, w=[("PS", sbk)])
            else:
                (u0_, pv0), (u1_, _) = g_
                c0, c1 = tcols(b, u0_), tcols(b, u1_)
                both = slice(c0.start, c1.stop, c0.step)
                k0 = pv0 if pv0 is not None else u0_
                S.op("pe", (lambda e: e.matmul(PS[:, sbk, 0:128], lhsT=KA[0:64, tcols(b, k0)], rhs=QA[0:64, c0], start=True, stop=True)),
                     r=QK_ALL, w=[("PS", sbk)])
                S.op("pe", (lambda e: e.matmul(PS[:, sbk, 128:384], lhsT=KA[0:64, c0], rhs=QA[0:64, both], start=True, stop=True)),
                     r=QK_ALL, w=[("PS", sbk)])
                S.op("pe", (lambda e: e.matmul(PS[:, sbk, 384:512], lhsT=KA[0:64, c1], rhs=QA[0:64, c1], start=True, stop=True)),
                     r=QK_ALL, w=[("PS", sbk)])
            S.op("dve", (lambda e: e.tensor_tensor(out=pt, in0=PS[:, sbk, :], in1=TAB[:, 0:512], op=ALU.add)),
                 r=[("PS", sbk), ("TAB",)], w=[ptk])
            S.op("act", (lambda e: e.activation(out=pt, in_=pt, func=AF.Exp)), r=[ptk], w=[ptk])

        def pv_phase(gi):
            g_ = groups[gi]
            n = n0 + gi
            obk, pt, ptk = 4 + n % 2, PT[n % 3], ("PT", n % 3)
            if b == 16:
                for i, (uu, _) in enumerate(g_):
                    S.op("pe", (lambda e, i=i, uu=uu: e.matmul(PS[:, obk, i * 128:(i + 1) * 128], lhsT=VE[:, uu, :], rhs=pt[:, i * 128:(i + 1) * 128], start=True, stop=True)),
                         r=[ptk, ("VE", uu)], w=[("PS", obk)])
            else:
                (u0_, pv0), (u1_, _) = g_
                S.op("pe", (lambda e: e.matmul(PS[:, obk, 0:256], lhsT=VE[:, u0_, :], rhs=pt[:, 128:384], start=True, stop=False)),
                     r=[ptk, ("VE", u0_)], w=[("PS", obk)])
                if pv0 is not None:
                    S.op("pe", (lambda e: e.matmul(PS[:, obk, 0:128], lhsT=VE[:, pv0, :], rhs=pt[:, 0:128], start=False, stop=False)),
                         r=[ptk, ("VE", pv0)], w=[("PS", obk)])
                S.op("pe", (lambda e: e.matmul(PS[:, obk, 128:256], lhsT=VE[:, u1_, :], rhs=pt[:, 384:512], start=False, stop=True)),
                     r=[ptk, ("VE", u1_)], w=[("PS", obk)])
            u0 = g_[0][0]
            if b == 1:
                S.op("dve", (lambda e: e.tensor_copy(out=ACC[:, u0 * 128:(u0 + 2) * 128], in_=PS[:, obk, 0:256])),
                     r=[("PS", obk)], w=[("ACC",)])
            elif b == 4:
                r_, st = u0 // 4, u0 % 4
                st0 = 512 * st + r_
                dst = ACC[:, st0:st0 + 4 * 255 + 1:4]
                S.op("dve", (lambda e: e.tensor_tensor(out=dst, in0=PS[:, obk, 0:256], in1=dst, op=ALU.add)),
                     r=[("PS", obk), ("ACC",)], w=[("ACC",)])
            else:
                dst = ACC.rearrange("p (i r) -> p r i", r=16)[:, u0:u0 + 4, :]
                S.op("dve", (lambda e: e.tensor_tensor(out=dst, in0=PS[:, obk, :].rearrange("p (r i) -> p r i", i=128), in1=dst, op=ALU.add)),
                     r=[("PS", obk), ("ACC",)], w=[("ACC",)])

        score_phase(0)
        score_phase(1)
        for gi in range(len(groups)):
            if gi + 2 < len(groups):
                score_phase(gi + 2)
            pv_phase(gi)

    def head_A(l, h, cb=None):
        chunk, rows = h // 2, slice((h % 2) * 64, (h % 2) * 64 + 64)
        for bi, b in enumerate((1, 4, 16)):
            branch_A(l, h, bi, b, cb if bi == 2 else None)
        for tb in range(4):
            normalize_to_mt(ACC[:, tb_sl(tb)], ("ACC",), chunk, rows, tb)

    def head_C(l, h, cb=None):
        chunk, rows = 5 + h // 2, slice((h % 2) * 64, (h % 2) * 64 + 64)
        S.op("dve", (lambda e: e.tensor_reduce(out=KM[0:64, 0:8], in_=KA[0:64, :].rearrange("p (n j) -> p n j", j=256), axis=AX.X, op=ALU.add)),
             r=QK_ALL, w=[("KM",)])
        for t in range(16):
            S.op("pe", (lambda e, t=t: e.matmul(PS[:, 7, t * 8:(t + 1) * 8], lhsT=QA[0:64, t * 128:(t + 1) * 128], rhs=KM[0:64, 0:8], start=True, stop=True)),
                 r=QK_ALL + [("KM",)], w=[("PS", 7)])
        S.op("dve", (lambda e: e.tensor_tensor(out=GM, in0=PS[:, 7, 0:128], in1=CN[:, C_PASTNEG:C_PASTNEG + 128], op=ALU.add)),
             r=[("PS", 7), ("CN",)], w=[("GM",)])
        for t in range(16):
            S.op("dve", (lambda e, t=t: e.max(out=TH[:, t * 8:(t + 1) * 8], in_=GM[:, t * 8:(t + 1) * 8])), r=[("GM",)], w=[("TH", t)])
            S.op("dve", (lambda e, t=t: e.tensor_single_scalar(out=NS[:, t * 8:(t + 1) * 8], in_=GM[:, t * 8:(t + 1) * 8], scalar=TH[:, t * 8 + 2:t * 8 + 3], op=ALU.is_lt)),
                 r=[("GM",), ("TH", t)], w=[("NS", t)])
        S.op("dve", (lambda e: e.tensor_tensor(out=MK, in0=NS, in1=CN[:, C_NEGPAST2:C_NEGPAST2 + 128], op=ALU.mult)),
             r=[("NS", t) for t in range(16)] + [("CN",)], w=[("MK",)])
        ve_build(1, h % 2)
        if cb is not None:
            cb()
        for tb in range(4):
            for tt in range(4):
                t = tb * 4 + tt
                S.op("pe", (lambda e, t=t, tt=tt: e.matmul(PS[64:72, 7, tt * 128:(tt + 1) * 128], lhsT=MK[:, t * 8:(t + 1) * 8], rhs=IDENT, start=True, stop=True)),
                     r=[("MK",), ("CN",)], w=[("PS", 7)])
            S.op("act", (lambda e, tb=tb: e.activation(out=QA[64:72, tb_sl(tb)], in_=PS[64:72, 7, :], func=AF.Copy)),
                 r=[("PS", 7)], w=[("QA", tb)])
        items = [(m, jt) for m in range(4) for jt in range(4 * (m + 1))]
        n0 = gctr["n"]
        gctr["n"] += len(items)

        def score_phase(k):
            m, jt = items[k]
            n = n0 + k
            sbk, pt, ptk = SBK[n % 3], PT[n % 3], ("PT", n % 3)
            S.op("pe", (lambda e: e.matmul(PS[:, sbk, :], lhsT=KA[0:72, jt * 128:(jt + 1) * 128], rhs=QA[0:72, tb_sl(m)], start=True, stop=True)),
                 r=QK_ALL + [("KA", "ind")], w=[("PS", sbk)])
            g0 = 512 * m - 128 * jt + 384
            S.op("dve", (lambda e: e.tensor_tensor(out=pt, in0=PS[:, sbk, :], in1=GT[:, g0:g0 + 512], op=ALU.add)),
                 r=[("PS", sbk), ("TAB",)], w=[ptk])
            S.op("act", (lambda e: e.activation(out=pt, in_=pt, func=AF.Exp)), r=[ptk], w=[ptk])

        def pv_phase(k):
            m, jt = items[k]
            n = n0 + k
            nk = 4 * (m + 1)
            obk, pt, ptk = 4 + m % 2, PT[n % 3], ("PT", n % 3)
            S.op("pe", (lambda e: e.matmul(PS[:, obk, :], lhsT=VE[:, jt, :], rhs=pt, start=(jt == 0), stop=(jt == nk - 1))),
                 r=[ptk, ("VE", jt)], w=[("PS", obk)])
            if jt == nk - 1:
                S.op("act", (lambda e: e.activation(out=OSB, in_=PS[:, obk, :], func=AF.Copy)), r=[("PS", obk)], w=[("OSB",)])
                normalize_to_mt(OSB, ("OSB",), chunk, rows, m)

        score_phase(0)
        score_phase(1)
        for k in range(len(items)):
            if k + 2 < len(items):
                score_phase(k + 2)
            pv_phase(k)

    def mixer_AC(l):
        S.op("pool", (lambda e: e.memset(VE[:, :, 64:128], 1.0)), w=[("VE", u) for u in range(16)])
        S.dma("sp", KA[64:72, :], ind_d[:, :], ("ind",), w=[("KA", "ind")])
        win_v = win_d[l].rearrange("(c p) f -> p c f", p=128)

        def wl2(colsets):
            sls = []
            for hf in range(2):
                slot = wstate["n"] % NSLOT
                wstate["n"] += 1
                v = WR[:, slot, :].rearrange("p (c f) -> p c f", f=128)
                o = 0
                for (c0, w_) in colsets:
                    S.dma("sp", v[:, :, o:o + w_], win_v[:, hf * 4:(hf + 1) * 4, c0:c0 + w_], ("w", slot), w=[("W", slot)])
                    o += w_
                wscale(slot, 4, 128, P_NW1 + l * 8 + hf * 4)
                sls.append(slot)
            return sls

        heads = [(mixn, h) for mixn in ("A", "C") if not (dbg == "onlyA" and mixn == "C") for h in range(6)]

        def load_head(i):
            mixn, h = heads[i]
            qo, ko, vo = (OFF["cq"], OFF["ck"], OFF["cv"]) if mixn == "C" else (OFF["aq"], OFF["ak"], OFF["av"])
            sqk = wl2([(qo + h * 64, 64), (ko + h * 64, 64)])
            svv = wl2([(vo + h * 64, 128)]) if h % 2 == 0 else None
            return sqk, svv

        pend = load_head(0)
        svv_cur = None
        for i, (mixn, h) in enumerate(heads):
            isC = mixn == "C"
            sqk, svv = pend
            if isC:
                S.dma("sp", GT, g_d[h], ("tab",), w=[("TAB",), ("ACC",)])
            for tb in range(4):
                bank = pj_bank()
                for c in ALLC:
                    wv = WR[:, sqk[c // 4], :].rearrange("p (c f) -> p c f", f=128)
                    S.op("pe", (lambda e, c=c, wv=wv, bank=bank, tb=tb: e.matmul(PS[:, bank, :], lhsT=wv[:, c % 4, :], rhs=XT[:, c, tb_sl(tb)], start=(c == 0), stop=(c == 7))),
                         r=[("W", sqk[c // 4])] + xt_res([c], [tb]), w=[("PS", bank)])
                S.op("dve", (lambda e, bank=bank, tb=tb: e.scalar_tensor_tensor(out=QA[0:64, tb_sl(tb)], in0=PS[0:64, bank, :], scalar=0.125, in1=RB[0:64, tb_sl(tb)], op0=ALU.mult, op1=ALU.mult)),
                     r=[("PS", bank), ("RB", tb)], w=[("QA", tb)])
                S.op("dve", (lambda e, bank=bank, tb=tb: e.tensor_tensor(out=KA[0:64, tb_sl(tb)], in0=PS[64:128, bank, :], in1=RB[64:128, tb_sl(tb)], op=ALU.mult)),
                     r=[("PS", bank), ("RB", tb)], w=[("KA", tb)])
            if i == 0:
                vt_project_pair(l, svv)
            if i + 1 < len(heads):
                pend = load_head(i + 1)
            cb = None
            if h % 2 == 1 and i + 1 < len(heads):
                nsvv = pend[1]
                cb = (lambda nsvv=nsvv: vt_project_pair(l, nsvv))
            if isC:
                head_C(l, h, cb)
            else:
                head_A(l, h, cb)

    def mixer_B(l):
        win_v = win_d[l].rearrange("(c p) f -> p c f", p=128)
        b0 = OFF["bq"]
        S.dma("sp", BW[:, 0:4, :], win_v[:, 0:4, b0:b0 + 784], ("bw",), w=[("BW",)])
        S.dma("sp", BW[:, 4:8, :], win_v[:, 4:8, b0:b0 + 784], ("bw",), w=[("BW",)])
        bwsc = PR[:, P_NW1 + l * 8:P_NW1 + l * 8 + 8].rearrange("p (c o) -> p c o", o=1).broadcast_to([128, 8, 784])
        S.op("dve", (lambda e: e.tensor_tensor(out=BW, in0=BW, in1=bwsc, op=ALU.mult)), r=[("PR",)], w=[("BW",)])
        S.op("pool", (lambda e: e.memset(SALL, 0.0)), w=[("SALL",)])
        i16 = 1.0 / 16.0
        for tb in range(4):
            def proj(cols, M, dst, key, tb=tb):
                bank = pj_bank()
                for c in ALLC:
                    S.op("pe", (lambda e, c=c, bank=bank: e.matmul(PS[0:M, bank, :], lhsT=BW[:, c, cols], rhs=XT[:, c, tb_sl(tb)], start=(c == 0), stop=(c == 7))),
                         r=[("BW",)] + xt_res([c], [tb]), w=[("PS", bank)])
                S.op("dve", (lambda e, bank=bank: e.tensor_tensor(out=dst[0:M, :], in0=PS[0:M, bank, :], in1=RB[0:M, tb_sl(tb)], op=ALU.mult)),
                     r=[("PS", bank), ("RB", tb)], w=[key])
            proj(slice(0, 128), 128, BQ, ("BQ",))
            proj(slice(128, 256), 128, BK, ("BK",))
            for rc in range(2):
                proj(slice(512 + rc * 128, 512 + (rc + 1) * 128), 128, BR[rc], ("BR", rc))
                S.op("act", (lambda e, rc=rc: e.activation(out=BR[rc], in_=BR[rc], func=AF.Silu)), r=[("BR", rc)], w=[("BR", rc)])
            proj(slice(768, 784), 16, ALR, ("ALR",))
            for ch in range(4):
                chunk_B(l, tb, ch)

    def chunk_B(l, tb, ch):
        i16 = 1.0 / 16.0
        g = tb * 4 + ch
        t0 = g * 128
        csl = slice(ch * 128, (ch + 1) * 128)
        bank = pj_bank()
        for c in ALLC:
            S.op("pe", (lambda e, c=c: e.matmul(PS[:, bank, 0:256], lhsT=XT[:, c, t0:t0 + 128], rhs=BW[:, c, 256:512], start=(c == 0), stop=(c == 7))),
                 r=[("BW",)] + xt_res([c], [tb]), w=[("PS", bank)])
        S.op("act", (lambda e: e.activation(out=BV, in_=PS[:, bank, 0:256], func=AF.Copy, scale=RT[:, g:g + 1])),
             r=[("PS", bank), ("RT",)], w=[("BV",)])
        S.op("pe", (lambda e: e.matmul(PS[:, 2, 0:128], lhsT=ALR[0:16, csl], rhs=WA2[0:16, l * 128:(l + 1) * 128], start=True, stop=False)),
             r=[("ALR",), ("WA2",)], w=[("PS", 2)])
        S.op("pe", (lambda e: e.matmul(PS[:, 2, 0:128], lhsT=ONES[0:1, 0:128], rhs=BA[0:1, l * 128:(l + 1) * 128], start=False, stop=True)),
             r=[("CN",), ("BA",)], w=[("PS", 2)])
        S.op("act", (lambda e: e.activation(out=SP_, in_=PS[:, 2, 0:128], func=AF.Exp, scale=-1.0)), r=[("PS", 2)], w=[("SP",)])
        S.op("act", (lambda e: e.activation(out=SP_, in_=SP_, func=AF.Ln, bias=1.0, scale=1.0)), r=[("SP",)], w=[("SP",)])
        S.op("pe", (lambda e: e.matmul(PS[:, 3, 0:128], lhsT=SP_, rhs=TRI, start=True, stop=True)), r=[("SP",), ("CN",)], w=[("PS", 3)])
        S.op("act", (lambda e: e.activation(out=EB, in_=PS[:, 3, 0:128], func=AF.Exp, scale=-i16)), r=[("PS", 3)], w=[("EB",)])
        S.op("act", (lambda e: e.activation(out=KG, in_=PS[:, 3, 0:128], func=AF.Exp, scale=i16)), r=[("PS", 3)], w=[("KG",)])
        S.op("dve", (lambda e: e.tensor_tensor(out=KG, in0=KG, in1=BK[:, csl], op=ALU.mult)), r=[("KG",), ("BK",)], w=[("KG",)])
        for h in range(4):
            S.op("dve", (lambda e, h=h: e.scalar_tensor_tensor(out=QGM[:, h * 128:(h + 1) * 128], in0=BQ[:, csl], scalar=CN[:, C_HMS + h:C_HMS + h + 1], in1=EB, op0=ALU.mult, op1=ALU.mult)),
                 r=[("BQ",), ("EB",), ("CN",)], w=[("QGM", h)])
        S.op("pe", (lambda e: e.matmul(PS[:, 2, 128:256], lhsT=KG, rhs=IDENT, start=True, stop=True)), r=[("KG",), ("CN",)], w=[("PS", 2)])
        S.op("act", (lambda e: e.activation(out=KGT, in_=PS[:, 2, 128:256], func=AF.Copy)), r=[("PS", 2)], w=[("KGT",)])
        for h in range(4):
            S.op("pe", (lambda e, h=h: e.matmul(PS[:, 4, h * 128:(h + 1) * 128], lhsT=KG, rhs=QGM[:, h * 128:(h + 1) * 128], start=True, stop=True)),
                 r=[("KG",), ("QGM", h)], w=[("PS", 4)])
        for h in range(4):
            S.op("dve", (lambda e, h=h: e.tensor_tensor(out=AT[:, h * 128:(h + 1) * 128], in0=PS[:, 4, h * 128:(h + 1) * 128], in1=TRI, op=ALU.mult)),
                 r=[("PS", 4), ("CN",)], w=[("AT", h)])
        for h in range(4):
            oap = PS[(h % 2) * 64:(h % 2) * 64 + 64, 5, (h // 2) * 128:(h // 2 + 1) * 128]
            S.op("pe", (lambda e, h=h, oap=oap: e.matmul(oap, lhsT=BV[:, h * 64:(h + 1) * 64], rhs=AT[:, h * 128:(h + 1) * 128], start=True, stop=False)),
                 r=[("BV",), ("AT", h)], w=[("PS", 5)])
            S.op("pe", (lambda e, h=h, oap=oap: e.matmul(oap, lhsT=SALL, rhs=QGM[:, h * 128:(h + 1) * 128], start=False, stop=True)),
                 r=[("SALL",), ("QGM", h)], w=[("PS", 5)])
        S.op("pe", (lambda e: e.matmul(PS[:, 3, 128:384], lhsT=KGT, rhs=BV, start=True, stop=True)), r=[("KGT",), ("BV",)], w=[("PS", 3)])
        for h in range(4):
            S.op("dve", (lambda e, h=h: e.scalar_tensor_tensor(out=SALL, in0=PS[:, 3, 128 + h * 64:128 + (h + 1) * 64], scalar=CN[:, C_HM + h:C_HM + h + 1], in1=SALL, op0=ALU.mult, op1=ALU.add)),
                 r=[("PS", 3), ("SALL",), ("CN",)], w=[("SALL",)])
        S.op("dve", (lambda e: e.tensor_scalar_mul(out=SALL, in0=SALL, scalar1=EB[:, 127:128])), r=[("SALL",), ("EB",)], w=[("SALL",)])
        S.op("act", (lambda e: e.activation(out=OSQ, in_=PS[:, 5, 0:256], func=AF.Square)), r=[("PS", 5)], w=[("OSQ",)])
        S.op("pe", (lambda e: e.matmul(PS[:, 6, 0:256], lhsT=BLK, rhs=OSQ, start=True, stop=True)), r=[("OSQ",), ("CN",)], w=[("PS", 6)])
        S.op("act", (lambda e: e.activation(out=RS, in_=PS[:, 6, 0:256], func=AF.Ln, bias=EPS_AP, scale=1.0 / 64.0)), r=[("PS", 6), ("CN",)], w=[("RS",)])
        S.op("act", (lambda e: e.activation(out=RS, in_=RS, func=AF.Exp, scale=-0.5)), r=[("RS",)], w=[("RS",)])
        S.op("dve", (lambda e: e.tensor_tensor(out=OSQ, in0=PS[:, 5, 0:256], in1=RS, op=ALU.mult)), r=[("PS", 5), ("RS",), ("OSQ",)], w=[("OSQ",)])
        for rc in range(2):
            S.op("dve", (lambda e, rc=rc: e.scalar_tensor_tensor(out=MT[:, 3 + rc, t0:t0 + 128], in0=OSQ[:, rc * 128:(rc + 1) * 128], scalar=PR[:, P_GNW + l * 2 + rc:P_GNW + l * 2 + rc + 1], in1=BR[rc][:, csl], op0=ALU.mult, op1=ALU.mult)),
                 r=[("OSQ",), ("PR",), ("BR", rc)], w=[("MT", 3 + rc, tb)])

    def ffn_block(s, l, tb, last):
        norm_stats(tb, 6 + tb % 2)
        for c in ALLC:
            S.op("dve", (lambda e, c=c: e.scalar_tensor_tensor(out=H2[:, c, :], in0=XT[:, c, tb_sl(tb)], scalar=PR[:, P_NW2 + l * 8 + c:P_NW2 + l * 8 + c + 1], in1=RB[:, tb_sl(tb)], op0=ALU.mult, op1=ALU.mult)),
                 r=xt_res([c], [tb]) + [("RB", tb), ("PR",)], w=[("H2", c)])
        for fc in range(32):
            sl = []
            for hf in range(2):
                src = wff1_d[l].rearrange("(c p) f -> p c f", p=128)[:, hf * 4:(hf + 1) * 4, fc * 128:(fc + 1) * 128]
                sl.append(wload(lambda v: v.rearrange("p (c f) -> p c f", f=128), src))
            bank = pj_bank()
            for c in ALLC:
                wv = WR[:, sl[c // 4], :].rearrange("p (c f) -> p c f", f=128)
                S.op("pe", (lambda e, c=c, wv=wv, bank=bank: e.matmul(PS[:, bank, :], lhsT=wv[:, c % 4, :], rhs=H2[:, c, :], start=(c == 0), stop=(c == 7))),
                     r=[("W", sl[c // 4]), ("H2", c)], w=[("PS", bank)])
            av = MT[:, fc // 4, (fc % 4) * 512:(fc % 4 + 1) * 512]
            S.op("act", (lambda e, av=av, bank=bank: e.activation(out=av, in_=PS[:, bank, :], func=AF.Relu)),
                 r=[("PS", bank)], w=[("MT", fc // 4, fc % 4)])
            S.op("pool", (lambda e, av=av: e.tensor_tensor(out=av, in0=av, in1=av, op=ALU.mult)),
                 r=[("MT", fc // 4, fc % 4)], w=[("MT", fc // 4, fc % 4)])
        for half in range(2):
            for fc in range(32):
                src = wff2_d[l][fc * 128:(fc + 1) * 128, half * 512:(half + 1) * 512]
                slot = wload(lambda v: v, src)
                av = MT[:, fc // 4, (fc % 4) * 512:(fc % 4 + 1) * 512]
                for o4 in range(4):
                    S.op("pe", (lambda e, slot=slot, o4=o4, av=av, fc=fc: e.matmul(PS[:, 2 + o4, :], lhsT=WR[:, slot, o4 * 128:(o4 + 1) * 128], rhs=av, start=(fc == 0), stop=(fc == 31))),
                         r=[("W", slot), ("MT", fc // 4, fc % 4)], w=[("PS", 2 + o4)])
            for o4 in range(4):
                oc = half * 4 + o4
                S.op("dve", (lambda e, oc=oc, o4=o4: e.tensor_tensor(out=XT[:, oc, tb_sl(tb)], in0=PS[:, 2 + o4, :], in1=XT[:, oc, tb_sl(tb)], op=ALU.add)),
                     r=[("PS", 2 + o4)] + xt_res([oc], [tb]), w=xt_res([oc], [tb]))

        if last:
            norm_stats(tb, 6 + tb % 2)
            for tt in range(4):
                t = tb * 4 + tt
                ob = OUTT[t % 2]
                for c in ALLC:
                    S.op("dve", (lambda e, c=c, t=t: e.scalar_tensor_tensor(out=YN[:, c * 128:(c + 1) * 128], in0=XT[:, c, t * 128:(t + 1) * 128], scalar=PR[:, P_FW + c:P_FW + c + 1], in1=RB[:, t * 128:(t + 1) * 128], op0=ALU.mult, op1=ALU.mult)),
                         r=xt_res([c], [tb]) + [("RB", tb), ("PR",)], w=[("WK", "yn", c)])
                    S.op("pe", (lambda e, c=c: e.matmul(PS[:, c // 4, (c % 4) * 128:(c % 4 + 1) * 128], lhsT=YN[:, c * 128:(c + 1) * 128], rhs=IDENT, start=True, stop=True)),
                         r=[("WK", "yn", c), ("CN",)], w=[("PS", c // 4)])
                for hb in range(2):
                    S.op("act", (lambda e, hb=hb, ob=ob: e.activation(out=ob[:, hb * 512:(hb + 1) * 512], in_=PS[:, hb, :], func=AF.Copy)),
                         r=[("PS", hb)], w=[("WK", "out", t % 2)])
                S.dma("sp", out_d[s, t * 128:(t + 1) * 128, :], ob, ("out", t % 2), r=[("WK", "out", t % 2)], w=[("OUTD",)])


    for s in range(nseq):
        S.barrier()
        for t in range(16):
            xin = XIN[t % 2]
            S.dma("sp", xin, x_d[s, t * 128:(t + 1) * 128, :], ("xin", t % 2), w=[("WK", "xin", t % 2)])
            for c in ALLC:
                S.op("pe", (lambda e, c=c, xin=xin: e.matmul(PS[:, 6 + c // 4, (c % 4) * 128:(c % 4 + 1) * 128], lhsT=xin[:, c * 128:(c + 1) * 128], rhs=IDENT, start=True, stop=True)),
                     r=[("WK", "xin", t % 2), ("CN",)], w=[("PS", 6 + c // 4)])
            for hb in range(2):
                S.op("dve", (lambda e, t=t, hb=hb: e.tensor_copy(out=XT[:, hb * 4:(hb + 1) * 4, t * 128:(t + 1) * 128], in_=PS[:, 6 + hb, :].rearrange("p (k t) -> p k t", t=128))),
                     r=[("PS", 6 + hb)], w=xt_res(range(hb * 4, hb * 4 + 4), [t // 4]))

        for l in range(nlayers):
            last = (l == nlayers - 1)
            for tb in range(4):
                norm_stats(tb, 6 + tb % 2)
            for u in range(16):
                bank = 6 + u % 2
                S.op("pe", (lambda e, u=u, bank=bank: e.matmul(PS[:, bank, 0:1], lhsT=RB[:, u * 128:(u + 1) * 128], rhs=IDENT[:, 0:1], start=True, stop=True)),
                     r=[("RB", u // 4), ("CN",)], w=[("PS", bank)])
                S.op("dve", (lambda e, u=u, bank=bank: e.tensor_copy(out=RT[:, u:u + 1], in_=PS[:, bank, 0:1])),
                     r=[("PS", bank)], w=[("RT",)])
            S.barrier()
            if dbg in ("skipmix", "noB", "onlyA", "onlyB"):
                S.op("pool", (lambda e: e.memset(MT[:, :, :], 0.0)), w=[("MT", c, tb) for c in ALLC for tb in range(4)])
            if dbg != "skipmix":
                if dbg not in ("noB", "onlyA"):
                    mixer_B(l)
                S.barrier()
                if dbg != "onlyB":
                    mixer_AC(l)
            S.barrier()

            for oc in range(NCH):
                sl = []
                for hf in range(2):
                    src = wout_d[l].rearrange("(c p) f -> p c f", p=128)[:, hf * 4:(hf + 1) * 4, oc * 128:(oc + 1) * 128]
                    sl.append(wload(lambda v: v.rearrange("p (c f) -> p c f", f=128), src))
                for tb in range(4):
                    bank = pj_bank()
                    for mc in ALLC:
                        wv = WR[:, sl[mc // 4], :].rearrange("p (c f) -> p c f", f=128)
                        S.op("pe", (lambda e, mc=mc, wv=wv, bank=bank, tb=tb: e.matmul(PS[:, bank, :], lhsT=wv[:, mc % 4, :], rhs=MT[:, mc, tb_sl(tb)], start=(mc == 0), stop=(mc == 7))),
                             r=[("W", sl[mc // 4]), ("MT", mc, tb)], w=[("PS", bank)])
                    S.op("dve", (lambda e, oc=oc, bank=bank, tb=tb: e.tensor_tensor(out=XT[:, oc, tb_sl(tb)], in0=PS[:, bank, :], in1=XT[:, oc, tb_sl(tb)], op=ALU.add)),
                         r=[("PS", bank)] + xt_res([oc], [tb]), w=xt_res([oc], [tb]))

            for tb in range(4):
                ffn_block(s, l, tb, last)

    S.op("sp", (lambda e: e.nop()), r=[("WK", "out", 0), ("WK", "out", 1)], w=[("WK", "out", 0), ("WK", "out", 1)])
    S.emit(nc, stack)
    stack.close()
    return nc


def rel_bucket_np(d):
    n = np.maximum(d, 0)
    exact = 16
    logv = np.log(np.maximum(n, 1).astype(np.float32) / np.float32(exact)) / np.float32(math.log(2048 / exact))
    large = np.minimum(exact + (logv.astype(np.float32) * np.float32(32 - exact)).astype(np.int32), 31)
    return np.where(n < exact, n, large)


def host_tables(rel_bias):
    rel_bias = np.asarray(rel_bias, np.float32)
    NEG = np.float32(-BIG)
    jj = np.arange(128)[:, None]
    ii = np.arange(128)[None, :]
    ta = np.zeros((6, 128, 1536), np.float32)
    for h in range(6):
        def tile(dil, prev):
            d = ii - jj + (128 if prev else 0)
            valid = (d <= 128) if prev else (d >= 0)
            v = rel_bias[rel_bucket_np(np.maximum(d, 0) * dil), h]
            return np.where(valid, v, NEG).astype(np.float32)
        t1 = np.concatenate([tile(1, True), tile(1, False)], 1)
        t2 = np.concatenate([tile(4, True), tile(4, False)], 1)
        t3 = tile(16, False)
        ta[h] = np.concatenate([t1, t1, t2, t2, t3, t3, t3, t3], 1)
    m = np.arange(GW)[None, :]
    d = m - jj - 384
    g = np.zeros((6, 128, GW), np.float32)
    for h in range(6):
        v = rel_bias[rel_bucket_np(np.maximum(d, 0)), 6 + h]
        g[h] = np.where(d >= 0, v, NEG)
    return ta, g


def host_consts():
    c = np.zeros((128, NCONST), np.float32)
    k = np.arange(128)[:, None]
    m = np.arange(128)[None, :]
    c[:, C_IDENT:C_IDENT + 128] = (k == m)
    c[:, C_ONES:C_ONES + 128] = 1.0
    c[:, C_TRI:C_TRI + 128] = (k <= m)
    c[:, C_SWAP:C_SWAP + 128] = (k == (m + 64) % 128)
    c[:, C_BLK:C_BLK + 128] = (k // 64 == m // 64)
    for t in range(16):
        b = t // 2
        for n in range(8):
            c[:, C_PASTNEG + t * 8 + n] = -1e30 if n >= b else 0.0
            c[:, C_NEGPAST2 + t * 8 + n] = -BIG if n < b else 0.0
    for h in range(4):
        c[:, C_HM + h] = (np.arange(128) // 32 == h)
        c[:, C_HMS + h] = (np.arange(128) // 32 == h) * (32 ** -0.5)
    c[:, NCONST - 1] = EPS
    ind = np.zeros((8, S_LEN), np.float32)
    for n in range(8):
        ind[n, n * 256:(n + 1) * 256] = 1.0
    return c, ind


_CACHE = {}


def kernel(x, norm1_w, w_in, gla_w_a2, gla_b_a, gla_norm_w, w_out, norm2_w, w_ff1, w_ff2, rel_bias, final_norm_w):
    ncores = 8
    x = np.ascontiguousarray(np.asarray(x, np.float32))
    nseq = x.shape[0] // ncores
    ta, g = host_tables(rel_bias)
    consts, ind = host_consts()
    par = np.zeros((128, NPAR), np.float32)
    n1 = np.asarray(norm1_w, np.float32).reshape(2, 8, 128)
    n2 = np.asarray(norm2_w, np.float32).reshape(2, 8, 128)
    for l in range(2):
        par[:, P_NW1 + l * 8:P_NW1 + l * 8 + 8] = n1[l].T
        par[:, P_NW2 + l * 8:P_NW2 + l * 8 + 8] = n2[l].T
        par[:, P_GNW + l * 2:P_GNW + l * 2 + 2] = np.asarray(gla_norm_w, np.float32)[l].reshape(2, 128).T
    par[:, P_FW:P_FW + 8] = np.asarray(final_norm_w, np.float32).reshape(8, 128).T
    wa2 = np.ascontiguousarray(np.asarray(gla_w_a2, np.float32).transpose(1, 0, 2).reshape(16, 256))
    ba = np.ascontiguousarray(np.asarray(gla_b_a, np.float32).reshape(1, 256))
    if "nc" not in _CACHE:
        _CACHE["nc"] = build_program(nseq)
    nc = _CACHE["nc"]
    shared = dict(w_in=np.ascontiguousarray(np.asarray(w_in, np.float32)), w_out=np.ascontiguousarray(np.asarray(w_out, np.float32)),
                  w_ff1=np.ascontiguousarray(np.asarray(w_ff1, np.float32)), w_ff2=np.ascontiguousarray(np.asarray(w_ff2, np.float32)),
                  ta=ta, gtab=g, consts=consts, ind=ind, params=par, wa2=wa2, ba=ba)
    in_maps = []
    for i in range(ncores):
        m = dict(shared)
        m["x"] = x[i * nseq:(i + 1) * nseq]
        in_maps.append(m)
    res = run_bass_kernel_spmd(nc, in_maps, core_ids=list(range(ncores)))
    return np.concatenate([r["out"] for r in res.results], axis=0)
```

```python
import math
import contextlib
import numpy as np
import concourse.bass as bass
import concourse.mybir as mybir
from concourse.bass_utils import run_bass_kernel_spmd

F32 = mybir.dt.float32
AF = mybir.ActivationFunctionType
ALU = mybir.AluOpType
AX = mybir.AxisListType

S_LEN = 2048
D = 1024
NCH = 8
DFF = 4096
IN_W = 3088
EPS = 1e-6
BIG = 30000.0
OFF = dict(aq=0, ak=384, av=768, bq=1152, bk=1280, bv=1408, br=1664, ba=1920, cq=1936, ck=2320, cv=2704)
C_IDENT, C_ONES, C_TRI, C_SWAP, C_BLK, C_PASTNEG, C_NEGPAST2, C_HM, C_HMS, NCONST = 0, 128, 256, 384, 512, 640, 768, 896, 900, 912
P_NW1, P_NW2, P_FW, P_GNW, NPAR = 0, 16, 32, 40, 48
GW = 2432
SAME_ENGINE_SYNC = True


class Sched:
    ENGS = ("pe", "act", "dve", "pool", "sp")

    def __init__(self):
        self.ops = {e: [] for e in self.ENGS}
        self.last_w = {}
        self.readers = {}
        self.dma_cnt = {}

    def _deps(self, reads, writes):
        deps = set()
        for r in reads:
            ev = self.last_w.get(r)
            if ev is not None:
                deps.add(ev)
        for w in writes:
            ev = self.last_w.get(w)
            if ev is not None:
                deps.add(ev)
            for ev in self.readers.get(w, {}).values():
                deps.add(ev)
        return deps

    def _record(self, ev, reads, writes):
        for r in reads:
            self.readers.setdefault(r, {})[ev[:2]] = ev
        for w in writes:
            self.last_w[w] = ev
            self.readers[w] = {}

    def op(self, eng, fn, r=(), w=()):
        deps = self._deps(r, w)
        ev = ("e", eng, len(self.ops[eng]))
        self.ops[eng].append((fn, deps, ev))
        self._record(ev, r, w)

    def dma(self, eng, out, in_, sem, r=(), w=()):
        deps = self._deps(r, w)
        self.dma_cnt[sem] = self.dma_cnt.get(sem, 0) + 1
        ev = ("d", sem, self.dma_cnt[sem])
        self.ops[eng].append((lambda e: e.dma_start(out=out, in_=in_), deps, ev))
        self._record(ev, r, w)

    def barrier(self):
        deps = set()
        for e in self.ENGS:
            if self.ops[e]:
                deps.add(self.ops[e][-1][2] if self.ops[e][-1][2][0] == "e" else None)
        deps.discard(None)
        for e in self.ENGS:
            for i in range(len(self.ops[e]) - 1, -1, -1):
                if self.ops[e][i][2][0] == "e":
                    deps.add(self.ops[e][i][2])
                    break
        for k, cnt in self.dma_cnt.items():
            deps.add(("d", k, cnt))
        for e in self.ENGS:
            ev = ("e", e, len(self.ops[e]))
            self.ops[e].append(((lambda eng: eng.nop()), set(deps), ev))

    def emit(self, nc, stack):
        marked = {e: set() for e in self.ENGS}
        for e in self.ENGS:
            for (_, deps, _) in self.ops[e]:
                for d in deps:
                    if d[0] == "e":
                        if d[1] == e and (e == "pe" or not SAME_ENGINE_SYNC):
                            continue
                        marked[d[1]].add(d[2])
        val = {}
        for e in self.ENGS:
            for i, idx in enumerate(sorted(marked[e])):
                val[(e, idx)] = i + 1
        sems = {e: stack.enter_context(nc.semaphore("s_" + e)) for e in self.ENGS}
        dsems = {}
        for k in self.dma_cnt:
            dsems[k] = stack.enter_context(nc.semaphore("d_" + "_".join(str(x) for x in k)))
        block = stack.enter_context(nc.Block())

        def replay(ename, eng):
            known = {}
            for (fn, deps, ev) in self.ops[ename]:
                need = {}
                for d in deps:
                    if d[0] == "e":
                        if d[1] == ename and (ename == "pe" or not SAME_ENGINE_SYNC):
                            continue
                        key, v = ("e", d[1]), val[(d[1], d[2])]
                    else:
                        key, v = ("d", d[1]), 16 * d[2]
                    if v > need.get(key, 0):
                        need[key] = v
                for key, v in need.items():
                    if known.get(key, 0) >= v:
                        continue
                    known[key] = v
                    eng.wait_ge(sems[key[1]] if key[0] == "e" else dsems[key[1]], v)
                ins = fn(eng)
                if ev[0] == "d":
                    ins.then_inc(dsems[ev[1]], 16)
                elif (ename, ev[2]) in val:
                    ins.then_inc(sems[ename], 1)

        block.tensor(lambda e: replay("pe", e))
        block.scalar(lambda e: replay("act", e))
        block.vector(lambda e: replay("dve", e))
        block.gpsimd(lambda e: replay("pool", e))
        block.sync(lambda e: replay("sp", e))


def build_program(nseq, nlayers=2, dbg=None):
    nc = bass.Bass("TRN2", target_bir_lowering=False)
    dt = lambda name, shape, kind="ExternalInput": nc.dram_tensor(name, shape, F32, kind=kind).ap()
    x_d = dt("x", [nseq, S_LEN, D])
    win_d = dt("w_in", [2, D, IN_W])
    wout_d = dt("w_out", [2, D, D])
    wff1_d = dt("w_ff1", [2, D, DFF])
    wff2_d = dt("w_ff2", [2, DFF, D])
    ta_d = dt("ta", [6, 128, 1536])
    g_d = dt("gtab", [6, 128, GW])
    const_d = dt("consts", [128, NCONST])
    ind_d = dt("ind", [8, S_LEN])
    par_d = dt("params", [128, NPAR])
    wa2_d = dt("wa2", [16, 256])
    ba_d = dt("ba", [1, 256])
    out_d = dt("out", [nseq, S_LEN, D], kind="ExternalOutput")

    S = Sched()
    stack = contextlib.ExitStack()
    sb = lambda name, shape: stack.enter_context(nc.sbuf_tensor(name, shape, F32))
    XT = sb("XT", [128, NCH, S_LEN])
    MT = sb("MT", [128, NCH, S_LEN])
    RB = sb("RB", [128, S_LEN])
    RT = sb("RT", [128, 16])
    CN = sb("CN", [128, NCONST])
    PR = sb("PR", [128, NPAR])
    WA2 = sb("WA2", [16, 256])
    BA = sb("BA", [1, 256])
    NSLOT = 6
    WR = sb("WR", [128, NSLOT, 512])
    WORKN = 13312
    WK = sb("WK", [128, WORKN])
    PS = stack.enter_context(nc.psum_tensor("PS", [128, 8, 512], F32))

    IDENT = CN[:, C_IDENT:C_IDENT + 128]
    ONES = CN[:, C_ONES:C_ONES + 128]
    TRI = CN[:, C_TRI:C_TRI + 128]
    SWAP = CN[:, C_SWAP:C_SWAP + 128]
    BLK = CN[:, C_BLK:C_BLK + 128]

    QA = WK[:, 0:2048]
    KA = WK[:, 2048:4096]
    VE = WK[:, 4096:6144].rearrange("p (t f) -> p t f", f=128)
    PT = [WK[:, 6144:6656], WK[:, 6656:7168], WK[:, 12288:12800]]
    SBK = [2, 3, 1]
    ACC = WK[:, 7168:9216]
    TAB = WK[:, 9216:9728]
    GT = WK[:, 7168:9600]
    KM = WK[:, 9600:9608]
    VT = WK[:, 9728:11776]
    MSC = WK[:, 11776:12800]
    RCP = MSC[:, 0:512]
    GM = MSC[:, 512:640]
    TH = MSC[:, 640:768]
    NS = MSC[:, 768:896]
    MK = MSC[:, 896:1024]
    OSB = WK[:, 12800:13312]
    H2 = WK[:, 7168:11264].rearrange("p (c t) -> p c t", t=512)
    BW = WK[:, 0:6272].rearrange("p (c f) -> p c f", f=784)
    BQ = WK[:, 6272:6784]
    BK = WK[:, 6784:7296]
    BR = [WK[:, 7296:7808], WK[:, 7808:8320]]
    ALR = WK[:, 8320:8832]
    BV = WK[:, 8832:9088]
    SP_ = WK[:, 9088:9216]
    EB = WK[:, 9216:9344]
    KG = WK[:, 9344:9472]
    QGM = WK[:, 9472:9984]
    KGT = WK[:, 9984:10112]
    AT = WK[:, 10112:10624]
    OSQ = WK[:, 10624:10880]
    RS = WK[:, 10880:11136]
    SALL = WK[:, 11136:11200]
    XIN = [WK[:, 0:1024], WK[:, 1024:2048]]
    YN = WK[:, 2048:3072]
    OUTT = [WK[:, 3072:4096], WK[:, 4096:5120]]
    SQ = [WK[:, 6144:6656], WK[:, 6656:7168]]
    WKR = ("WK",)

    def tb_sl(tb):
        return slice(tb * 512, (tb + 1) * 512)

    def xt_res(cs, tbs):
        return [("XT", c, tb) for c in cs for tb in tbs]

    ALLC = range(NCH)

    S.dma("sp", CN[:, :], const_d[:, :], ("c", 0), w=[("CN",)])
    S.dma("sp", PR[:, :], par_d[:, :], ("c", 1), w=[("PR",)])
    S.dma("sp", WA2[:, :], wa2_d[:, :], ("c", 2), w=[("WA2",)])
    S.dma("sp", BA[:, :], ba_d[:, :], ("c", 3), w=[("BA",)])

    wstate = {"n": 0}

    def wload(dst_view_fn, src_ap):
        slot = wstate["n"] % NSLOT
        wstate["n"] += 1
        S.dma("sp", dst_view_fn(WR[:, slot, :]), src_ap, ("w", slot), w=[("W", slot)])
        return slot

    def wscale(slot, ncs, fcols, pcol0):
        v = WR[:, slot, :].rearrange("p (c f) -> p c f", f=fcols)
        sc = PR[:, pcol0:pcol0 + ncs].rearrange("p (c o) -> p c o", o=1).broadcast_to([128, ncs, fcols])
        S.op("dve", (lambda e: e.tensor_tensor(out=v, in0=v, in1=sc, op=ALU.mult)), r=[("PR",)], w=[("W", slot)])

    pj = {"n": 0}

    def pj_bank():
        b = pj["n"] % 2
        pj["n"] += 1
        return b

    def norm_stats(tb, bank):
        for c in ALLC:
            sq = SQ[c % 2]
            S.op("act", (lambda e, c=c, sq=sq: e.activation(out=sq, in_=XT[:, c, tb_sl(tb)], func=AF.Square)),
                 r=xt_res([c], [tb]), w=[("PT", c % 2)])
            S.op("pe", (lambda e, c=c, sq=sq: e.matmul(PS[:, bank, :], lhsT=ONES, rhs=sq, start=(c == 0), stop=(c == 7))),
                 r=[("PT", c % 2), ("CN",)], w=[("PS", bank)])
        S.op("act", (lambda e: e.activation(out=RB[:, tb_sl(tb)], in_=PS[:, bank, :], func=AF.Ln, bias=EPS_AP, scale=1.0 / D)),
             r=[("PS", bank), ("CN",)], w=[("RB", tb)])
        S.op("act", (lambda e: e.activation(out=RB[:, tb_sl(tb)], in_=RB[:, tb_sl(tb)], func=AF.Exp, scale=-0.5)),
             r=[("RB", tb)], w=[("RB", tb)])

    EPS_AP = CN[:, NCONST - 1:NCONST]


    def tcols(b, u):
        if b == 1:
            return slice(u * 128, (u + 1) * 128)
        if b == 4:
            r_, st = u // 4, u % 4
            st0 = 512 * st + r_
            return slice(st0, st0 + 4 * 127 + 1, 4)
        return slice(u, u + 16 * 127 + 1, 16)

    QK_ALL = [("QA", tb) for tb in range(4)] + [("KA", tb) for tb in range(4)]

    def vt_project_pair(l, sls):
        for tb in range(4):
            bank = pj_bank()
            for c in ALLC:
                wv = WR[:, sls[c // 4], :].rearrange("p (c f) -> p c f", f=128)
                S.op("pe", (lambda e, c=c, bank=bank, tb=tb, wv=wv: e.matmul(PS[:, bank, :], lhsT=wv[:, c % 4, :], rhs=XT[:, c, tb_sl(tb)], start=(c == 0), stop=(c == 7))),
                     r=[("W", sls[c // 4])] + xt_res([c], [tb]), w=[("PS", bank)])
            S.op("dve", (lambda e, bank=bank, tb=tb: e.tensor_tensor(out=VT[:, tb_sl(tb)], in0=PS[:, bank, :], in1=RB[:, tb_sl(tb)], op=ALU.mult)),
                 r=[("PS", bank), ("RB", tb)], w=[("VT", tb)])

    def ve_build(b, hs):
        for t8 in range(2):
            bank = pj_bank()
            for tt in range(8):
                u = t8 * 8 + tt
                S.op("pe", (lambda e, u=u, tt=tt, bank=bank: e.matmul(PS[:, bank, tt * 64:(tt + 1) * 64], lhsT=VT[hs * 64:hs * 64 + 64, tcols(b, u)], rhs=IDENT[hs * 64:hs * 64 + 64, hs * 64:hs * 64 + 64], start=True, stop=True)),
                     r=[("VT", tb) for tb in range(4)] + [("CN",)], w=[("PS", bank)])
            S.op("act", (lambda e, t8=t8, bank=bank: e.activation(out=VE[:, t8 * 8:(t8 + 1) * 8, 0:64], in_=PS[:, bank, :].rearrange("p (t f) -> p t f", f=64), func=AF.Copy)),
                 r=[("PS", bank)], w=[("VE", u) for u in range(t8 * 8, t8 * 8 + 8)])

    gctr = {"n": 0}

    def normalize_to_mt(src, srckey, chunk, rows, tb):
        S.op("dve", (lambda e: e.reciprocal(out=RCP[0:64, :], in_=src[64:128, :])), r=[srckey], w=[("RCP",)])
        S.op("dve", (lambda e: e.tensor_tensor(out=MT[rows, chunk, tb_sl(tb)], in0=src[0:64, :], in1=RCP[0:64, :], op=ALU.mult)),
             r=[srckey, ("RCP",)], w=[("MT", chunk, tb)])

    def branch_A(l, h, bi, b, cb=None):
        S.dma("sp", TAB, ta_d[h][:, bi * 512:(bi + 1) * 512], ("tab",), w=[("TAB",)])
        ve_build(b, h % 2)
        if cb is not None:
            cb()
        if b == 16:
            groups = [[(u + i, None) for i in range(4)] for u in range(0, 16, 4)]
        else:
            groups = []
            for u in range(0, 16, 2):
                g_ = []
                for uu in (u, u + 1):
                    has_prev = (uu > 0) if b == 1 else (uu % 4 > 0)
                    g_.append((uu, uu - 1 if has_prev else None))
                groups.append(g_)
        n0 = gctr["n"]
        gctr["n"] += len(groups)

        def score_phase(gi):
            g_ = groups[gi]
            n = n0 + gi
            sbk, pt, ptk = SBK[n % 3], PT[n % 3], ("PT", n % 3)
            if b == 16:
                for i, (uu, _) in enumerate(g_):
                    S.op("pe", (lambda e, i=i, uu=uu: e.matmul(PS[:, sbk, i * 128:(i + 1) * 128], lhsT=KA[0:64, tcols(b, uu)], rhs=QA[0:64, tcols(b, uu)], start=True, stop=True)),
                         r=QK_ALL, w=[("PS", sbk)])
            else:
                (u0_, pv0), (u1_, _) = g_
                c0, c1 = tcols(b, u0_), tcols(b, u1_)
                both = slice(c0.start, c1.stop, c0.step)
                k0 = pv0 if pv0 is not None else u0_
                S.op("pe", (lambda e: e.matmul(PS[:, sbk, 0:128], lhsT=KA[0:64, tcols(b, k0)], rhs=QA[0:64, c0], start=True, stop=True)),
                     r=QK_ALL, w=[("PS", sbk)])
                S.op("pe", (lambda e: e.matmul(PS[:, sbk, 128:384], lhsT=KA[0:64, c0], rhs=QA[0:64, both], start=True, stop=True)),
                     r=QK_ALL, w=[("PS", sbk)])
                S.op("pe", (lambda e: e.matmul(PS[:, sbk, 384:512], lhsT=KA[0:64, c1], rhs=QA[0:64, c1], start=True, stop=True)),
                     r=QK_ALL, w=[("PS", sbk)])
            S.op("dve", (lambda e: e.tensor_tensor(out=pt, in0=PS[:, sbk, :], in1=TAB[:, 0:512], op=ALU.add)),
                 r=[("PS", sbk), ("TAB",)], w=[ptk])
            S.op("act", (lambda e: e.activation(out=pt, in_=pt, func=AF.Exp)), r=[ptk], w=[ptk])

        def pv_phase(gi):
            g_ = groups[gi]
            n = n0 + gi
            obk, pt, ptk = 4 + n % 2, PT[n % 3], ("PT", n % 3)
            if b == 16:
                for i, (uu, _) in enumerate(g_):
                    S.op("pe", (lambda e, i=i, uu=uu: e.matmul(PS[:, obk, i * 128:(i + 1) * 128], lhsT=VE[:, uu, :], rhs=pt[:, i * 128:(i + 1) * 128], start=True, stop=True)),
                         r=[ptk, ("VE", uu)], w=[("PS", obk)])
            else:
                (u0_, pv0), (u1_, _) = g_
                S.op("pe", (lambda e: e.matmul(PS[:, obk, 0:256], lhsT=VE[:, u0_, :], rhs=pt[:, 128:384], start=True, stop=False)),
                     r=[ptk, ("VE", u0_)], w=[("PS", obk)])
                if pv0 is not None:
                    S.op("pe", (lambda e: e.matmul(PS[:, obk, 0:128], lhsT=VE[:, pv0, :], rhs=pt[:, 0:128], start=False, stop=False)),
                         r=[ptk, ("VE", pv0)], w=[("PS", obk)])
                S.op("pe", (lambda e: e.matmul(PS[:, obk, 128:256], lhsT=VE[:, u1_, :], rhs=pt[:, 384:512], start=False, stop=True)),
                     r=[ptk, ("VE", u1_)], w=[("PS", obk)])
            u0 = g_[0][0]
            if b == 1:
                S.op("dve", (lambda e: e.tensor_copy(out=ACC[:, u0 * 128:(u0 + 2) * 128], in_=PS[:, obk, 0:256])),
                     r=[("PS", obk)], w=[("ACC",)])
            elif b == 4:
                r_, st = u0 // 4, u0 % 4
                st0 = 512 * st + r_
                dst = ACC[:, st0:st0 + 4 * 255 + 1:4]
                S.op("dve", (lambda e: e.tensor_tensor(out=dst, in0=PS[:, obk, 0:256], in1=dst, op=ALU.add)),
                     r=[("PS", obk), ("ACC",)], w=[("ACC",)])
            else:
                dst = ACC.rearrange("p (i r) -> p r i", r=16)[:, u0:u0 + 4, :]
                S.op("dve", (lambda e: e.tensor_tensor(out=dst, in0=PS[:, obk, :].rearrange("p (r i) -> p r i", i=128), in1=dst, op=ALU.add)),
                     r=[("PS", obk), ("ACC",)], w=[("ACC",)])

        score_phase(0)
        score_phase(1)
        for gi in range(len(groups)):
            if gi + 2 < len(groups):
                score_phase(gi + 2)
            pv_phase(gi)

    def head_A(l, h, cb=None):
        chunk, rows = h // 2, slice((h % 2) * 64, (h % 2) * 64 + 64)
        for bi, b in enumerate((1, 4, 16)):
            branch_A(l, h, bi, b, cb if bi == 2 else None)
        for tb in range(4):
            normalize_to_mt(ACC[:, tb_sl(tb)], ("ACC",), chunk, rows, tb)

    def head_C(l, h, cb=None):
        chunk, rows = 5 + h // 2, slice((h % 2) * 64, (h % 2) * 64 + 64)
        S.op("dve", (lambda e: e.tensor_reduce(out=KM[0:64, 0:8], in_=KA[0:64, :].rearrange("p (n j) -> p n j", j=256), axis=AX.X, op=ALU.add)),
             r=QK_ALL, w=[("KM",)])
        for t in range(16):
            S.op("pe", (lambda e, t=t: e.matmul(PS[:, 7, t * 8:(t + 1) * 8], lhsT=QA[0:64, t * 128:(t + 1) * 128], rhs=KM[0:64, 0:8], start=True, stop=True)),
                 r=QK_ALL + [("KM",)], w=[("PS", 7)])
        S.op("dve", (lambda e: e.tensor_tensor(out=GM, in0=PS[:, 7, 0:128], in1=CN[:, C_PASTNEG:C_PASTNEG + 128], op=ALU.add)),
             r=[("PS", 7), ("CN",)], w=[("GM",)])
        for t in range(16):
            S.op("dve", (lambda e, t=t: e.max(out=TH[:, t * 8:(t + 1) * 8], in_=GM[:, t * 8:(t + 1) * 8])), r=[("GM",)], w=[("TH", t)])
            S.op("dve", (lambda e, t=t: e.tensor_single_scalar(out=NS[:, t * 8:(t + 1) * 8], in_=GM[:, t * 8:(t + 1) * 8], scalar=TH[:, t * 8 + 2:t * 8 + 3], op=ALU.is_lt)),
                 r=[("GM",), ("TH", t)], w=[("NS", t)])
        S.op("dve", (lambda e: e.tensor_tensor(out=MK, in0=NS, in1=CN[:, C_NEGPAST2:C_NEGPAST2 + 128], op=ALU.mult)),
             r=[("NS", t) for t in range(16)] + [("CN",)], w=[("MK",)])
        ve_build(1, h % 2)
        if cb is not None:
            cb()
        for tb in range(4):
            for tt in range(4):
                t = tb * 4 + tt
                S.op("pe", (lambda e, t=t, tt=tt: e.matmul(PS[64:72, 7, tt * 128:(tt + 1) * 128], lhsT=MK[:, t * 8:(t + 1) * 8], rhs=IDENT, start=True, stop=True)),
                     r=[("MK",), ("CN",)], w=[("PS", 7)])
            S.op("act", (lambda e, tb=tb: e.activation(out=QA[64:72, tb_sl(tb)], in_=PS[64:72, 7, :], func=AF.Copy)),
                 r=[("PS", 7)], w=[("QA", tb)])
        items = [(m, jt) for m in range(4) for jt in range(4 * (m + 1))]
        n0 = gctr["n"]
        gctr["n"] += len(items)

        def score_phase(k):
            m, jt = items[k]
            n = n0 + k
            sbk, pt, ptk = SBK[n % 3], PT[n % 3], ("PT", n % 3)
            q0 = 256 if jt >= 4 * m + 2 else 0
            S.op("pe", (lambda e: e.matmul(PS[:, sbk, q0:512], lhsT=KA[0:72, jt * 128:(jt + 1) * 128], rhs=QA[0:72, m * 512 + q0:(m + 1) * 512], start=True, stop=True)),
                 r=QK_ALL + [("KA", "ind")], w=[("PS", sbk)])
            g0 = 512 * m - 128 * jt + 384
            S.op("dve", (lambda e: e.tensor_tensor(out=pt[:, q0:512], in0=PS[:, sbk, q0:512], in1=GT[:, g0 + q0:g0 + 512], op=ALU.add)),
                 r=[("PS", sbk), ("TAB",)], w=[ptk])
            S.op("act", (lambda e: e.activation(out=pt[:, q0:512], in_=pt[:, q0:512], func=AF.Exp)), r=[ptk], w=[ptk])

        def pv_phase(k):
            m, jt = items[k]
            n = n0 + k
            nk = 4 * (m + 1)
            obk, pt, ptk = 4 + m % 2, PT[n % 3], ("PT", n % 3)
            q0 = 256 if jt >= 4 * m + 2 else 0
            S.op("pe", (lambda e: e.matmul(PS[:, obk, q0:512], lhsT=VE[:, jt, :], rhs=pt[:, q0:512], start=(jt == 0), stop=(jt == nk - 1))),
                 r=[ptk, ("VE", jt)], w=[("PS", obk)])
            if jt == nk - 1:
                S.op("act", (lambda e: e.activation(out=OSB, in_=PS[:, obk, :], func=AF.Copy)), r=[("PS", obk)], w=[("OSB",)])
                normalize_to_mt(OSB, ("OSB",), chunk, rows, m)

        score_phase(0)
        score_phase(1)
        for k in range(len(items)):
            if k + 2 < len(items):
                score_phase(k + 2)
            pv_phase(k)

    def mixer_AC(l):
        S.op("pool", (lambda e: e.memset(VE[:, :, 64:128], 1.0)), w=[("VE", u) for u in range(16)])
        S.dma("sp", KA[64:72, :], ind_d[:, :], ("ind",), w=[("KA", "ind")])
        win_v = win_d[l].rearrange("(c p) f -> p c f", p=128)

        def wl2(colsets):
            sls = []
            for hf in range(2):
                slot = wstate["n"] % NSLOT
                wstate["n"] += 1
                v = WR[:, slot, :].rearrange("p (c f) -> p c f", f=128)
                o = 0
                for (c0, w_) in colsets:
                    S.dma("sp", v[:, :, o:o + w_], win_v[:, hf * 4:(hf + 1) * 4, c0:c0 + w_], ("w", slot), w=[("W", slot)])
                    o += w_
                wscale(slot, 4, 128, P_NW1 + l * 8 + hf * 4)
                sls.append(slot)
            return sls

        heads = [(mixn, h) for mixn in ("A", "C") if not (dbg == "onlyA" and mixn == "C") for h in range(6)]

        def load_head(i):
            mixn, h = heads[i]
            qo, ko, vo = (OFF["cq"], OFF["ck"], OFF["cv"]) if mixn == "C" else (OFF["aq"], OFF["ak"], OFF["av"])
            sqk = wl2([(qo + h * 64, 64), (ko + h * 64, 64)])
            svv = wl2([(vo + h * 64, 128)]) if h % 2 == 0 else None
            return sqk, svv

        pend = load_head(0)
        svv_cur = None
        for i, (mixn, h) in enumerate(heads):
            isC = mixn == "C"
            sqk, svv = pend
            if isC:
                S.dma("sp", GT, g_d[h], ("tab",), w=[("TAB",), ("ACC",)])
            for tb in range(4):
                bank = pj_bank()
                for c in ALLC:
                    wv = WR[:, sqk[c // 4], :].rearrange("p (c f) -> p c f", f=128)
                    S.op("pe", (lambda e, c=c, wv=wv, bank=bank, tb=tb: e.matmul(PS[:, bank, :], lhsT=wv[:, c % 4, :], rhs=XT[:, c, tb_sl(tb)], start=(c == 0), stop=(c == 7))),
                         r=[("W", sqk[c // 4])] + xt_res([c], [tb]), w=[("PS", bank)])
                S.op("dve", (lambda e, bank=bank, tb=tb: e.scalar_tensor_tensor(out=QA[0:64, tb_sl(tb)], in0=PS[0:64, bank, :], scalar=0.125, in1=RB[0:64, tb_sl(tb)], op0=ALU.mult, op1=ALU.mult)),
                     r=[("PS", bank), ("RB", tb)], w=[("QA", tb)])
                S.op("dve", (lambda e, bank=bank, tb=tb: e.tensor_tensor(out=KA[0:64, tb_sl(tb)], in0=PS[64:128, bank, :], in1=RB[64:128, tb_sl(tb)], op=ALU.mult)),
                     r=[("PS", bank), ("RB", tb)], w=[("KA", tb)])
            if i == 0:
                vt_project_pair(l, svv)
            if i + 1 < len(heads):
                pend = load_head(i + 1)
            cb = None
            if h % 2 == 1 and i + 1 < len(heads):
                nsvv = pend[1]
                cb = (lambda nsvv=nsvv: vt_project_pair(l, nsvv))
            if isC:
                head_C(l, h, cb)
            else:
                head_A(l, h, cb)

    def mixer_B(l):
        win_v = win_d[l].rearrange("(c p) f -> p c f", p=128)
        b0 = OFF["bq"]
        S.dma("sp", BW[:, 0:4, :], win_v[:, 0:4, b0:b0 + 784], ("bw",), w=[("BW",)])
        S.dma("sp", BW[:, 4:8, :], win_v[:, 4:8, b0:b0 + 784], ("bw",), w=[("BW",)])
        bwsc = PR[:, P_NW1 + l * 8:P_NW1 + l * 8 + 8].rearrange("p (c o) -> p c o", o=1).broadcast_to([128, 8, 784])
        S.op("dve", (lambda e: e.tensor_tensor(out=BW, in0=BW, in1=bwsc, op=ALU.mult)), r=[("PR",)], w=[("BW",)])
        S.op("pool", (lambda e: e.memset(SALL, 0.0)), w=[("SALL",)])
        i16 = 1.0 / 16.0
        for tb in range(4):
            def proj(cols, M, dst, key, tb=tb):
                bank = pj_bank()
                for c in ALLC:
                    S.op("pe", (lambda e, c=c, bank=bank: e.matmul(PS[0:M, bank, :], lhsT=BW[:, c, cols], rhs=XT[:, c, tb_sl(tb)], start=(c == 0), stop=(c == 7))),
                         r=[("BW",)] + xt_res([c], [tb]), w=[("PS", bank)])
                S.op("dve", (lambda e, bank=bank: e.tensor_tensor(out=dst[0:M, :], in0=PS[0:M, bank, :], in1=RB[0:M, tb_sl(tb)], op=ALU.mult)),
                     r=[("PS", bank), ("RB", tb)], w=[key])
            proj(slice(0, 128), 128, BQ, ("BQ",))
            proj(slice(128, 256), 128, BK, ("BK",))
            for rc in range(2):
                proj(slice(512 + rc * 128, 512 + (rc + 1) * 128), 128, BR[rc], ("BR", rc))
                S.op("act", (lambda e, rc=rc: e.activation(out=BR[rc], in_=BR[rc], func=AF.Silu)), r=[("BR", rc)], w=[("BR", rc)])
            proj(slice(768, 784), 16, ALR, ("ALR",))
            for ch in range(4):
                chunk_B(l, tb, ch)

    def chunk_B(l, tb, ch):
        i16 = 1.0 / 16.0
        g = tb * 4 + ch
        t0 = g * 128
        csl = slice(ch * 128, (ch + 1) * 128)
        bank = pj_bank()
        for c in ALLC:
            S.op("pe", (lambda e, c=c: e.matmul(PS[:, bank, 0:256], lhsT=XT[:, c, t0:t0 + 128], rhs=BW[:, c, 256:512], start=(c == 0), stop=(c == 7))),
                 r=[("BW",)] + xt_res([c], [tb]), w=[("PS", bank)])
        S.op("act", (lambda e: e.activation(out=BV, in_=PS[:, bank, 0:256], func=AF.Copy, scale=RT[:, g:g + 1])),
             r=[("PS", bank), ("RT",)], w=[("BV",)])
        S.op("pe", (lambda e: e.matmul(PS[:, 2, 0:128], lhsT=ALR[0:16, csl], rhs=WA2[0:16, l * 128:(l + 1) * 128], start=True, stop=False)),
             r=[("ALR",), ("WA2",)], w=[("PS", 2)])
        S.op("pe", (lambda e: e.matmul(PS[:, 2, 0:128], lhsT=ONES[0:1, 0:128], rhs=BA[0:1, l * 128:(l + 1) * 128], start=False, stop=True)),
             r=[("CN",), ("BA",)], w=[("PS", 2)])
        S.op("act", (lambda e: e.activation(out=SP_, in_=PS[:, 2, 0:128], func=AF.Exp, scale=-1.0)), r=[("PS", 2)], w=[("SP",)])
        S.op("act", (lambda e: e.activation(out=SP_, in_=SP_, func=AF.Ln, bias=1.0, scale=1.0)), r=[("SP",)], w=[("SP",)])
        S.op("pe", (lambda e: e.matmul(PS[:, 3, 0:128], lhsT=SP_, rhs=TRI, start=True, stop=True)), r=[("SP",), ("CN",)], w=[("PS", 3)])
        S.op("act", (lambda e: e.activation(out=EB, in_=PS[:, 3, 0:128], func=AF.Exp, scale=-i16)), r=[("PS", 3)], w=[("EB",)])
        S.op("act", (lambda e: e.activation(out=KG, in_=PS[:, 3, 0:128], func=AF.Exp, scale=i16)), r=[("PS", 3)], w=[("KG",)])
        S.op("dve", (lambda e: e.tensor_tensor(out=KG, in0=KG, in1=BK[:, csl], op=ALU.mult)), r=[("KG",), ("BK",)], w=[("KG",)])
        for h in range(4):
            S.op("dve", (lambda e, h=h: e.scalar_tensor_tensor(out=QGM[:, h * 128:(h + 1) * 128], in0=BQ[:, csl], scalar=CN[:, C_HMS + h:C_HMS + h + 1], in1=EB, op0=ALU.mult, op1=ALU.mult)),
                 r=[("BQ",), ("EB",), ("CN",)], w=[("QGM", h)])
        S.op("pe", (lambda e: e.matmul(PS[:, 2, 128:256], lhsT=KG, rhs=IDENT, start=True, stop=True)), r=[("KG",), ("CN",)], w=[("PS", 2)])
        S.op("act", (lambda e: e.activation(out=KGT, in_=PS[:, 2, 128:256], func=AF.Copy)), r=[("PS", 2)], w=[("KGT",)])
        for h in range(4):
            S.op("pe", (lambda e, h=h: e.matmul(PS[:, 4, h * 128:(h + 1) * 128], lhsT=KG, rhs=QGM[:, h * 128:(h + 1) * 128], start=True, stop=True)),
                 r=[("KG",), ("QGM", h)], w=[("PS", 4)])
        for h in range(4):
            S.op("dve", (lambda e, h=h: e.tensor_tensor(out=AT[:, h * 128:(h + 1) * 128], in0=PS[:, 4, h * 128:(h + 1) * 128], in1=TRI, op=ALU.mult)),
                 r=[("PS", 4), ("CN",)], w=[("AT", h)])
        for h in range(4):
            oap = PS[(h % 2) * 64:(h % 2) * 64 + 64, 5, (h // 2) * 128:(h // 2 + 1) * 128]
            S.op("pe", (lambda e, h=h, oap=oap: e.matmul(oap, lhsT=BV[:, h * 64:(h + 1) * 64], rhs=AT[:, h * 128:(h + 1) * 128], start=True, stop=False)),
                 r=[("BV",), ("AT", h)], w=[("PS", 5)])
            S.op("pe", (lambda e, h=h, oap=oap: e.matmul(oap, lhsT=SALL, rhs=QGM[:, h * 128:(h + 1) * 128], start=False, stop=True)),
                 r=[("SALL",), ("QGM", h)], w=[("PS", 5)])
        S.op("pe", (lambda e: e.matmul(PS[:, 3, 128:384], lhsT=KGT, rhs=BV, start=True, stop=True)), r=[("KGT",), ("BV",)], w=[("PS", 3)])
        for h in range(4):
            S.op("dve", (lambda e, h=h: e.scalar_tensor_tensor(out=SALL, in0=PS[:, 3, 128 + h * 64:128 + (h + 1) * 64], scalar=CN[:, C_HM + h:C_HM + h + 1], in1=SALL, op0=ALU.mult, op1=ALU.add)),
                 r=[("PS", 3), ("SALL",), ("CN",)], w=[("SALL",)])
        S.op("dve", (lambda e: e.tensor_scalar_mul(out=SALL, in0=SALL, scalar1=EB[:, 127:128])), r=[("SALL",), ("EB",)], w=[("SALL",)])
        S.op("act", (lambda e: e.activation(out=OSQ, in_=PS[:, 5, 0:256], func=AF.Square)), r=[("PS", 5)], w=[("OSQ",)])
        S.op("pe", (lambda e: e.matmul(PS[:, 6, 0:256], lhsT=BLK, rhs=OSQ, start=True, stop=True)), r=[("OSQ",), ("CN",)], w=[("PS", 6)])
        S.op("act", (lambda e: e.activation(out=RS, in_=PS[:, 6, 0:256], func=AF.Ln, bias=EPS_AP, scale=1.0 / 64.0)), r=[("PS", 6), ("CN",)], w=[("RS",)])
        S.op("act", (lambda e: e.activation(out=RS, in_=RS, func=AF.Exp, scale=-0.5)), r=[("RS",)], w=[("RS",)])
        S.op("dve", (lambda e: e.tensor_tensor(out=OSQ, in0=PS[:, 5, 0:256], in1=RS, op=ALU.mult)), r=[("PS", 5), ("RS",), ("OSQ",)], w=[("OSQ",)])
        for rc in range(2):
            S.op("dve", (lambda e, rc=rc: e.scalar_tensor_tensor(out=MT[:, 3 + rc, t0:t0 + 128], in0=OSQ[:, rc * 128:(rc + 1) * 128], scalar=PR[:, P_GNW + l * 2 + rc:P_GNW + l * 2 + rc + 1], in1=BR[rc][:, csl], op0=ALU.mult, op1=ALU.mult)),
                 r=[("OSQ",), ("PR",), ("BR", rc)], w=[("MT", 3 + rc, tb)])

    def ffn_block(s, l, tb, last):
        norm_stats(tb, 6 + tb % 2)
        for c in ALLC:
            S.op("dve", (lambda e, c=c: e.scalar_tensor_tensor(out=H2[:, c, :], in0=XT[:, c, tb_sl(tb)], scalar=PR[:, P_NW2 + l * 8 + c:P_NW2 + l * 8 + c + 1], in1=RB[:, tb_sl(tb)], op0=ALU.mult, op1=ALU.mult)),
                 r=xt_res([c], [tb]) + [("RB", tb), ("PR",)], w=[("H2", c)])
        for fc in range(32):
            sl = []
            for hf in range(2):
                src = wff1_d[l].rearrange("(c p) f -> p c f", p=128)[:, hf * 4:(hf + 1) * 4, fc * 128:(fc + 1) * 128]
                sl.append(wload(lambda v: v.rearrange("p (c f) -> p c f", f=128), src))
            bank = pj_bank()
            for c in ALLC:
                wv = WR[:, sl[c // 4], :].rearrange("p (c f) -> p c f", f=128)
                S.op("pe", (lambda e, c=c, wv=wv, bank=bank: e.matmul(PS[:, bank, :], lhsT=wv[:, c % 4, :], rhs=H2[:, c, :], start=(c == 0), stop=(c == 7))),
                     r=[("W", sl[c // 4]), ("H2", c)], w=[("PS", bank)])
            av = MT[:, fc // 4, (fc % 4) * 512:(fc % 4 + 1) * 512]
            S.op("act", (lambda e, av=av, bank=bank: e.activation(out=av, in_=PS[:, bank, :], func=AF.Relu)),
                 r=[("PS", bank)], w=[("MT", fc // 4, fc % 4)])
            S.op("pool", (lambda e, av=av: e.tensor_tensor(out=av, in0=av, in1=av, op=ALU.mult)),
                 r=[("MT", fc // 4, fc % 4)], w=[("MT", fc // 4, fc % 4)])
        for half in range(2):
            for fc in range(32):
                src = wff2_d[l][fc * 128:(fc + 1) * 128, half * 512:(half + 1) * 512]
                slot = wload(lambda v: v, src)
                av = MT[:, fc // 4, (fc % 4) * 512:(fc % 4 + 1) * 512]
                for o4 in range(4):
                    S.op("pe", (lambda e, slot=slot, o4=o4, av=av, fc=fc: e.matmul(PS[:, 2 + o4, :], lhsT=WR[:, slot, o4 * 128:(o4 + 1) * 128], rhs=av, start=(fc == 0), stop=(fc == 31))),
                         r=[("W", slot), ("MT", fc // 4, fc % 4)], w=[("PS", 2 + o4)])
            for o4 in range(4):
                oc = half * 4 + o4
                S.op("dve", (lambda e, oc=oc, o4=o4: e.tensor_tensor(out=XT[:, oc, tb_sl(tb)], in0=PS[:, 2 + o4, :], in1=XT[:, oc, tb_sl(tb)], op=ALU.add)),
                     r=[("PS", 2 + o4)] + xt_res([oc], [tb]), w=xt_res([oc], [tb]))

        if last:
            norm_stats(tb, 6 + tb % 2)
            for tt in range(4):
                t = tb * 4 + tt
                ob = OUTT[t % 2]
                for c in ALLC:
                    S.op("dve", (lambda e, c=c, t=t: e.scalar_tensor_tensor(out=YN[:, c * 128:(c + 1) * 128], in0=XT[:, c, t * 128:(t + 1) * 128], scalar=PR[:, P_FW + c:P_FW + c + 1], in1=RB[:, t * 128:(t + 1) * 128], op0=ALU.mult, op1=ALU.mult)),
                         r=xt_res([c], [tb]) + [("RB", tb), ("PR",)], w=[("WK", "yn", c)])
                    S.op("pe", (lambda e, c=c: e.matmul(PS[:, c // 4, (c % 4) * 128:(c % 4 + 1) * 128], lhsT=YN[:, c * 128:(c + 1) * 128], rhs=IDENT, start=True, stop=True)),
                         r=[("WK", "yn", c), ("CN",)], w=[("PS", c // 4)])
                for hb in range(2):
                    S.op("act", (lambda e, hb=hb, ob=ob: e.activation(out=ob[:, hb * 512:(hb + 1) * 512], in_=PS[:, hb, :], func=AF.Copy)),
                         r=[("PS", hb)], w=[("WK", "out", t % 2)])
                S.dma("sp", out_d[s, t * 128:(t + 1) * 128, :], ob, ("out", t % 2), r=[("WK", "out", t % 2)], w=[("OUTD",)])


    for s in range(nseq):
        S.barrier()
        for t in range(16):
            xin = XIN[t % 2]
            S.dma("sp", xin, x_d[s, t * 128:(t + 1) * 128, :], ("xin", t % 2), w=[("WK", "xin", t % 2)])
            for c in ALLC:
                S.op("pe", (lambda e, c=c, xin=xin: e.matmul(PS[:, 6 + c // 4, (c % 4) * 128:(c % 4 + 1) * 128], lhsT=xin[:, c * 128:(c + 1) * 128], rhs=IDENT, start=True, stop=True)),
                     r=[("WK", "xin", t % 2), ("CN",)], w=[("PS", 6 + c // 4)])
            for hb in range(2):
                S.op("dve", (lambda e, t=t, hb=hb: e.tensor_copy(out=XT[:, hb * 4:(hb + 1) * 4, t * 128:(t + 1) * 128], in_=PS[:, 6 + hb, :].rearrange("p (k t) -> p k t", t=128))),
                     r=[("PS", 6 + hb)], w=xt_res(range(hb * 4, hb * 4 + 4), [t // 4]))

        for l in range(nlayers):
            last = (l == nlayers - 1)
            for tb in range(4):
                norm_stats(tb, 6 + tb % 2)
            for u in range(16):
                bank = 6 + u % 2
                S.op("pe", (lambda e, u=u, bank=bank: e.matmul(PS[:, bank, 0:1], lhsT=RB[:, u * 128:(u + 1) * 128], rhs=IDENT[:, 0:1], start=True, stop=True)),
                     r=[("RB", u // 4), ("CN",)], w=[("PS", bank)])
                S.op("dve", (lambda e, u=u, bank=bank: e.tensor_copy(out=RT[:, u:u + 1], in_=PS[:, bank, 0:1])),
                     r=[("PS", bank)], w=[("RT",)])
            S.barrier()
            if dbg in ("skipmix", "noB", "onlyA", "onlyB"):
                S.op("pool", (lambda e: e.memset(MT[:, :, :], 0.0)), w=[("MT", c, tb) for c in ALLC for tb in range(4)])
            if dbg != "skipmix":
                if dbg not in ("noB", "onlyA"):
                    mixer_B(l)
                S.barrier()
                if dbg != "onlyB":
                    mixer_AC(l)
            S.barrier()

            for oc in range(NCH):
                sl = []
                for hf in range(2):
                    src = wout_d[l].rearrange("(c p) f -> p c f", p=128)[:, hf * 4:(hf + 1) * 4, oc * 128:(oc + 1) * 128]
                    sl.append(wload(lambda v: v.rearrange("p (c f) -> p c f", f=128), src))
                for tb in range(4):
                    bank = pj_bank()
                    for mc in ALLC:
                        wv = WR[:, sl[mc // 4], :].rearrange("p (c f) -> p c f", f=128)
                        S.op("pe", (lambda e, mc=mc, wv=wv, bank=bank, tb=tb: e.matmul(PS[:, bank, :], lhsT=wv[:, mc % 4, :], rhs=MT[:, mc, tb_sl(tb)], start=(mc == 0), stop=(mc == 7))),
                             r=[("W", sl[mc // 4]), ("MT", mc, tb)], w=[("PS", bank)])
                    S.op("dve", (lambda e, oc=oc, bank=bank, tb=tb: e.tensor_tensor(out=XT[:, oc, tb_sl(tb)], in0=PS[:, bank, :], in1=XT[:, oc, tb_sl(tb)], op=ALU.add)),
                         r=[("PS", bank)] + xt_res([oc], [tb]), w=xt_res([oc], [tb]))

            for tb in range(4):
                ffn_block(s, l, tb, last)

    S.op("sp", (lambda e: e.nop()), r=[("WK", "out", 0), ("WK", "out", 1)], w=[("WK", "out", 0), ("WK", "out", 1)])
    S.emit(nc, stack)
    stack.close()
    return nc


def rel_bucket_np(d):
    n = np.maximum(d, 0)
    exact = 16
    logv = np.log(np.maximum(n, 1).astype(np.float32) / np.float32(exact)) / np.float32(math.log(2048 / exact))
    large = np.minimum(exact + (logv.astype(np.float32) * np.float32(32 - exact)).astype(np.int32), 31)
    return np.where(n < exact, n, large)


def host_tables(rel_bias):
    rel_bias = np.asarray(rel_bias, np.float32)
    NEG = np.float32(-BIG)
    jj = np.arange(128)[:, None]
    ii = np.arange(128)[None, :]
    ta = np.zeros((6, 128, 1536), np.float32)
    for h in range(6):
        def tile(dil, prev):
            d = ii - jj + (128 if prev else 0)
            valid = (d <= 128) if prev else (d >= 0)
            v = rel_bias[rel_bucket_np(np.maximum(d, 0) * dil), h]
            return np.where(valid, v, NEG).astype(np.float32)
        t1 = np.concatenate([tile(1, True), tile(1, False)], 1)
        t2 = np.concatenate([tile(4, True), tile(4, False)], 1)
        t3 = tile(16, False)
        ta[h] = np.concatenate([t1, t1, t2, t2, t3, t3, t3, t3], 1)
    m = np.arange(GW)[None, :]
    d = m - jj - 384
    g = np.zeros((6, 128, GW), np.float32)
    for h in range(6):
        v = rel_bias[rel_bucket_np(np.maximum(d, 0)), 6 + h]
        g[h] = np.where(d >= 0, v, NEG)
    return ta, g


def host_consts():
    c = np.zeros((128, NCONST), np.float32)
    k = np.arange(128)[:, None]
    m = np.arange(128)[None, :]
    c[:, C_IDENT:C_IDENT + 128] = (k == m)
    c[:, C_ONES:C_ONES + 128] = 1.0
    c[:, C_TRI:C_TRI + 128] = (k <= m)
    c[:, C_SWAP:C_SWAP + 128] = (k == (m + 64) % 128)
    c[:, C_BLK:C_BLK + 128] = (k // 64 == m // 64)
    for t in range(16):
        b = t // 2
        for n in range(8):
            c[:, C_PASTNEG + t * 8 + n] = -1e30 if n >= b else 0.0
            c[:, C_NEGPAST2 + t * 8 + n] = -BIG if n < b else 0.0
    for h in range(4):
        c[:, C_HM + h] = (np.arange(128) // 32 == h)
        c[:, C_HMS + h] = (np.arange(128) // 32 == h) * (32 ** -0.5)
    c[:, NCONST - 1] = EPS
    ind = np.zeros((8, S_LEN), np.float32)
    for n in range(8):
        ind[n, n * 256:(n + 1) * 256] = 1.0
    return c, ind


_CACHE = {}


def kernel(x, norm1_w, w_in, gla_w_a2, gla_b_a, gla_norm_w, w_out, norm2_w, w_ff1, w_ff2, rel_bias, final_norm_w):
    ncores = 8
    x = np.ascontiguousarray(np.asarray(x, np.float32))
    nseq = x.shape[0] // ncores
    ta, g = host_tables(rel_bias)
    consts, ind = host_consts()
    par = np.zeros((128, NPAR), np.float32)
    n1 = np.asarray(norm1_w, np.float32).reshape(2, 8, 128)
    n2 = np.asarray(norm2_w, np.float32).reshape(2, 8, 128)
    for l in range(2):
        par[:, P_NW1 + l * 8:P_NW1 + l * 8 + 8] = n1[l].T
        par[:, P_NW2 + l * 8:P_NW2 + l * 8 + 8] = n2[l].T
        par[:, P_GNW + l * 2:P_GNW + l * 2 + 2] = np.asarray(gla_norm_w, np.float32)[l].reshape(2, 128).T
    par[:, P_FW:P_FW + 8] = np.asarray(final_norm_w, np.float32).reshape(8, 128).T
    wa2 = np.ascontiguousarray(np.asarray(gla_w_a2, np.float32).transpose(1, 0, 2).reshape(16, 256))
    ba = np.ascontiguousarray(np.asarray(gla_b_a, np.float32).reshape(1, 256))
    if "nc" not in _CACHE:
        _CACHE["nc"] = build_program(nseq)
    nc = _CACHE["nc"]
    shared = dict(w_in=np.ascontiguousarray(np.asarray(w_in, np.float32)), w_out=np.ascontiguousarray(np.asarray(w_out, np.float32)),
                  w_ff1=np.ascontiguousarray(np.asarray(w_ff1, np.float32)), w_ff2=np.ascontiguousarray(np.asarray(w_ff2, np.float32)),
                  ta=ta, gtab=g, consts=consts, ind=ind, params=par, wa2=wa2, ba=ba)
    in_maps = []
    for i in range(ncores):
        m = dict(shared)
        m["x"] = x[i * nseq:(i + 1) * nseq]
        in_maps.append(m)
    res = run_bass_kernel_spmd(nc, in_maps, core_ids=list(range(ncores)))
    return np.concatenate([r["out"] for r in res.results], axis=0)
```

```python
import math
import contextlib
import numpy as np
import concourse.bass as bass
import concourse.mybir as mybir
from concourse.bass_utils import run_bass_kernel_spmd

F32 = mybir.dt.float32
AF = mybir.ActivationFunctionType
ALU = mybir.AluOpType
AX = mybir.AxisListType

S_LEN = 2048
D = 1024
NCH = 8
DFF = 4096
IN_W = 3088
EPS = 1e-6
BIG = 30000.0
OFF = dict(aq=0, ak=384, av=768, bq=1152, bk=1280, bv=1408, br=1664, ba=1920, cq=1936, ck=2320, cv=2704)
C_IDENT, C_ONES, C_TRI, C_SWAP, C_BLK, C_PASTNEG, C_NEGPAST2, C_HM, C_HMS, NCONST = 0, 128, 256, 384, 512, 640, 768, 896, 900, 912
P_NW1, P_NW2, P_FW, P_GNW, NPAR = 0, 16, 32, 40, 48
GW = 2432
SAME_ENGINE_SYNC = True


class Sched:
    ENGS = ("pe", "act", "dve", "pool", "sp")

    def __init__(self):
        self.ops = {e: [] for e in self.ENGS}
        self.last_w = {}
        self.readers = {}
        self.dma_cnt = {}

    def _deps(self, reads, writes):
        deps = set()
        for r in reads:
            ev = self.last_w.get(r)
            if ev is not None:
                deps.add(ev)
        for w in writes:
            ev = self.last_w.get(w)
            if ev is not None:
                deps.add(ev)
            for ev in self.readers.get(w, {}).values():
                deps.add(ev)
        return deps

    def _record(self, ev, reads, writes):
        for r in reads:
            self.readers.setdefault(r, {})[ev[:2]] = ev
        for w in writes:
            self.last_w[w] = ev
            self.readers[w] = {}

    def op(self, eng, fn, r=(), w=()):
        deps = self._deps(r, w)
        ev = ("e", eng, len(self.ops[eng]))
        self.ops[eng].append((fn, deps, ev))
        self._record(ev, r, w)

    def dma(self, eng, out, in_, sem, r=(), w=()):
        deps = self._deps(r, w)
        self.dma_cnt[sem] = self.dma_cnt.get(sem, 0) + 1
        ev = ("d", sem, self.dma_cnt[sem])
        self.ops[eng].append((lambda e: e.dma_start(out=out, in_=in_), deps, ev))
        self._record(ev, r, w)

    def barrier(self):
        deps = set()
        for e in self.ENGS:
            if self.ops[e]:
                deps.add(self.ops[e][-1][2] if self.ops[e][-1][2][0] == "e" else None)
        deps.discard(None)
        for e in self.ENGS:
            for i in range(len(self.ops[e]) - 1, -1, -1):
                if self.ops[e][i][2][0] == "e":
                    deps.add(self.ops[e][i][2])
                    break
        for k, cnt in self.dma_cnt.items():
            deps.add(("d", k, cnt))
        for e in self.ENGS:
            ev = ("e", e, len(self.ops[e]))
            self.ops[e].append(((lambda eng: eng.nop()), set(deps), ev))

    def emit(self, nc, stack):
        marked = {e: set() for e in self.ENGS}
        for e in self.ENGS:
            for (_, deps, _) in self.ops[e]:
                for d in deps:
                    if d[0] == "e":
                        if d[1] == e and (e == "pe" or not SAME_ENGINE_SYNC):
                            continue
                        marked[d[1]].add(d[2])
        val = {}
        for e in self.ENGS:
            for i, idx in enumerate(sorted(marked[e])):
                val[(e, idx)] = i + 1
        sems = {e: stack.enter_context(nc.semaphore("s_" + e)) for e in self.ENGS}
        dsems = {}
        for k in self.dma_cnt:
            dsems[k] = stack.enter_context(nc.semaphore("d_" + "_".join(str(x) for x in k)))
        block = stack.enter_context(nc.Block())

        def replay(ename, eng):
            known = {}
            for (fn, deps, ev) in self.ops[ename]:
                need = {}
                for d in deps:
                    if d[0] == "e":
                        if d[1] == ename and (ename == "pe" or not SAME_ENGINE_SYNC):
                            continue
                        key, v = ("e", d[1]), val[(d[1], d[2])]
                    else:
                        key, v = ("d", d[1]), 16 * d[2]
                    if v > need.get(key, 0):
                        need[key] = v
                for key, v in need.items():
                    if known.get(key, 0) >= v:
                        continue
                    known[key] = v
                    eng.wait_ge(sems[key[1]] if key[0] == "e" else dsems[key[1]], v)
                ins = fn(eng)
                if ev[0] == "d":
                    ins.then_inc(dsems[ev[1]], 16)
                elif (ename, ev[2]) in val:
                    ins.then_inc(sems[ename], 1)

        block.tensor(lambda e: replay("pe", e))
        block.scalar(lambda e: replay("act", e))
        block.vector(lambda e: replay("dve", e))
        block.gpsimd(lambda e: replay("pool", e))
        block.sync(lambda e: replay("sp", e))


def build_program(nseq, nlayers=2, dbg=None):
    nc = bass.Bass("TRN2", target_bir_lowering=False)
    dt = lambda name, shape, kind="ExternalInput": nc.dram_tensor(name, shape, F32, kind=kind).ap()
    x_d = dt("x", [nseq, S_LEN, D])
    win_d = dt("w_in", [2, D, IN_W])
    wout_d = dt("w_out", [2, D, D])
    wff1_d = dt("w_ff1", [2, D, DFF])
    wff2_d = dt("w_ff2", [2, DFF, D])
    ta_d = dt("ta", [6, 128, 1536])
    g_d = dt("gtab", [6, 128, GW])
    const_d = dt("consts", [128, NCONST])
    ind_d = dt("ind", [8, S_LEN])
    par_d = dt("params", [128, NPAR])
    wa2_d = dt("wa2", [16, 256])
    ba_d = dt("ba", [1, 256])
    out_d = dt("out", [nseq, S_LEN, D], kind="ExternalOutput")

    S = Sched()
    stack = contextlib.ExitStack()
    sb = lambda name, shape: stack.enter_context(nc.sbuf_tensor(name, shape, F32))
    XT = sb("XT", [128, NCH, S_LEN])
    MT = sb("MT", [128, NCH, S_LEN])
    RB = sb("RB", [128, S_LEN])
    RT = sb("RT", [128, 16])
    CN = sb("CN", [128, NCONST])
    PR = sb("PR", [128, NPAR])
    WA2 = sb("WA2", [16, 256])
    BA = sb("BA", [1, 256])
    NSLOT = 6
    WR = sb("WR", [128, NSLOT, 512])
    WORKN = 13312
    WK = sb("WK", [128, WORKN])
    PS = stack.enter_context(nc.psum_tensor("PS", [128, 8, 512], F32))

    IDENT = CN[:, C_IDENT:C_IDENT + 128]
    ONES = CN[:, C_ONES:C_ONES + 128]
    TRI = CN[:, C_TRI:C_TRI + 128]
    SWAP = CN[:, C_SWAP:C_SWAP + 128]
    BLK = CN[:, C_BLK:C_BLK + 128]

    QA = WK[:, 0:2048]
    KA = WK[:, 2048:4096]
    VE = WK[:, 4096:6144].rearrange("p (t f) -> p t f", f=128)
    PT = [WK[:, 6144:6656], WK[:, 6656:7168], WK[:, 12288:12800]]
    SBK = [2, 3, 1]
    ACC = WK[:, 7168:9216]
    TAB = WK[:, 9216:9728]
    GT = WK[:, 7168:9600]
    KM = WK[:, 9600:9608]
    VT = WK[:, 9728:11776]
    MSC = WK[:, 11776:12800]
    RCP = MSC[:, 0:512]
    GM = MSC[:, 512:640]
    TH = MSC[:, 640:768]
    NS = MSC[:, 768:896]
    MK = MSC[:, 896:1024]
    OSB = WK[:, 12800:13312]
    H2 = WK[:, 7168:11264].rearrange("p (c t) -> p c t", t=512)
    BW = WK[:, 0:6272].rearrange("p (c f) -> p c f", f=784)
    BQ = WK[:, 6272:6784]
    BK = WK[:, 6784:7296]
    BR = [WK[:, 7296:7808], WK[:, 7808:8320]]
    ALR = WK[:, 8320:8832]
    BV = WK[:, 8832:9088]
    SP_ = WK[:, 9088:9216]
    EB = WK[:, 9216:9344]
    KG = WK[:, 9344:9472]
    QGM = WK[:, 9472:9984]
    KGT = WK[:, 9984:10112]
    AT = WK[:, 10112:10624]
    OSQ = WK[:, 10624:10880]
    RS = WK[:, 10880:11136]
    SALL = WK[:, 11136:11200]
    XIN = [WK[:, 0:1024], WK[:, 1024:2048]]
    YN = WK[:, 2048:3072]
    OUTT = [WK[:, 3072:4096], WK[:, 4096:5120]]
    SQ = [WK[:, 6144:6656], WK[:, 6656:7168]]
    WKR = ("WK",)

    def tb_sl(tb):
        return slice(tb * 512, (tb + 1) * 512)

    def xt_res(cs, tbs):
        return [("XT", c, tb) for c in cs for tb in tbs]

    ALLC = range(NCH)

    S.dma("sp", CN[:, :], const_d[:, :], ("c", 0), w=[("CN",)])
    S.dma("sp", PR[:, :], par_d[:, :], ("c", 1), w=[("PR",)])
    S.dma("sp", WA2[:, :], wa2_d[:, :], ("c", 2), w=[("WA2",)])
    S.dma("sp", BA[:, :], ba_d[:, :], ("c", 3), w=[("BA",)])

    wstate = {"n": 0}

    def wload(dst_view_fn, src_ap, q="sp"):
        slot = wstate["n"] % NSLOT
        wstate["n"] += 1
        S.dma(q, dst_view_fn(WR[:, slot, :]), src_ap, ("w", slot), w=[("W", slot)])
        return slot

    def wscale(slot, ncs, fcols, pcol0):
        v = WR[:, slot, :].rearrange("p (c f) -> p c f", f=fcols)
        sc = PR[:, pcol0:pcol0 + ncs].rearrange("p (c o) -> p c o", o=1).broadcast_to([128, ncs, fcols])
        S.op("dve", (lambda e: e.tensor_tensor(out=v, in0=v, in1=sc, op=ALU.mult)), r=[("PR",)], w=[("W", slot)])

    pj = {"n": 0}

    def pj_bank():
        b = pj["n"] % 2
        pj["n"] += 1
        return b

    def norm_stats(tb, bank):
        for c in ALLC:
            sq = SQ[c % 2]
            S.op("act", (lambda e, c=c, sq=sq: e.activation(out=sq, in_=XT[:, c, tb_sl(tb)], func=AF.Square)),
                 r=xt_res([c], [tb]), w=[("PT", c % 2)])
            S.op("pe", (lambda e, c=c, sq=sq: e.matmul(PS[:, bank, :], lhsT=ONES, rhs=sq, start=(c == 0), stop=(c == 7))),
                 r=[("PT", c % 2), ("CN",)], w=[("PS", bank)])
        S.op("act", (lambda e: e.activation(out=RB[:, tb_sl(tb)], in_=PS[:, bank, :], func=AF.Ln, bias=EPS_AP, scale=1.0 / D)),
             r=[("PS", bank), ("CN",)], w=[("RB", tb)])
        S.op("act", (lambda e: e.activation(out=RB[:, tb_sl(tb)], in_=RB[:, tb_sl(tb)], func=AF.Exp, scale=-0.5)),
             r=[("RB", tb)], w=[("RB", tb)])

    EPS_AP = CN[:, NCONST - 1:NCONST]


    def tcols(b, u):
        if b == 1:
            return slice(u * 128, (u + 1) * 128)
        if b == 4:
            r_, st = u // 4, u % 4
            st0 = 512 * st + r_
            return slice(st0, st0 + 4 * 127 + 1, 4)
        return slice(u, u + 16 * 127 + 1, 16)

    QK_ALL = [("QA", tb) for tb in range(4)] + [("KA", tb) for tb in range(4)]

    def vt_project_pair(l, sls):
        for tb in range(4):
            bank = pj_bank()
            for c in ALLC:
                wv = WR[:, sls[c // 4], :].rearrange("p (c f) -> p c f", f=128)
                S.op("pe", (lambda e, c=c, bank=bank, tb=tb, wv=wv: e.matmul(PS[:, bank, :], lhsT=wv[:, c % 4, :], rhs=XT[:, c, tb_sl(tb)], start=(c == 0), stop=(c == 7))),
                     r=[("W", sls[c // 4])] + xt_res([c], [tb]), w=[("PS", bank)])
            S.op("dve", (lambda e, bank=bank, tb=tb: e.tensor_tensor(out=VT[:, tb_sl(tb)], in0=PS[:, bank, :], in1=RB[:, tb_sl(tb)], op=ALU.mult)),
                 r=[("PS", bank), ("RB", tb)], w=[("VT", tb)])

    def ve_build(b, hs):
        for t8 in range(2):
            bank = pj_bank()
            for tt in range(8):
                u = t8 * 8 + tt
                S.op("pe", (lambda e, u=u, tt=tt, bank=bank: e.matmul(PS[:, bank, tt * 64:(tt + 1) * 64], lhsT=VT[hs * 64:hs * 64 + 64, tcols(b, u)], rhs=IDENT[hs * 64:hs * 64 + 64, hs * 64:hs * 64 + 64], start=True, stop=True)),
                     r=[("VT", tb) for tb in range(4)] + [("CN",)], w=[("PS", bank)])
            S.op("act", (lambda e, t8=t8, bank=bank: e.activation(out=VE[:, t8 * 8:(t8 + 1) * 8, 0:64], in_=PS[:, bank, :].rearrange("p (t f) -> p t f", f=64), func=AF.Copy)),
                 r=[("PS", bank)], w=[("VE", u) for u in range(t8 * 8, t8 * 8 + 8)])

    gctr = {"n": 0}

    def normalize_to_mt(src, srckey, chunk, rows, tb):
        S.op("dve", (lambda e: e.reciprocal(out=RCP[0:64, :], in_=src[64:128, :])), r=[srckey], w=[("RCP",)])
        S.op("dve", (lambda e: e.tensor_tensor(out=MT[rows, chunk, tb_sl(tb)], in0=src[0:64, :], in1=RCP[0:64, :], op=ALU.mult)),
             r=[srckey, ("RCP",)], w=[("MT", chunk, tb)])

    def branch_A(l, h, bi, b, cb=None):
        S.dma("sp", TAB, ta_d[h][:, bi * 512:(bi + 1) * 512], ("tab",), w=[("TAB",)])
        ve_build(b, h % 2)
        if cb is not None:
            cb()
        if b == 16:
            groups = [[(u + i, None) for i in range(4)] for u in range(0, 16, 4)]
        else:
            groups = []
            for u in range(0, 16, 2):
                g_ = []
                for uu in (u, u + 1):
                    has_prev = (uu > 0) if b == 1 else (uu % 4 > 0)
                    g_.append((uu, uu - 1 if has_prev else None))
                groups.append(g_)
        n0 = gctr["n"]
        gctr["n"] += len(groups)

        def score_phase(gi):
            g_ = groups[gi]
            n = n0 + gi
            sbk, pt, ptk = SBK[n % 3], PT[n % 3], ("PT", n % 3)
            if b == 16:
                for i, (uu, _) in enumerate(g_):
                    S.op("pe", (lambda e, i=i, uu=uu: e.matmul(PS[:, sbk, i * 128:(i + 1) * 128], lhsT=KA[0:64, tcols(b, uu)], rhs=QA[0:64, tcols(b, uu)], start=True, stop=True)),
                         r=QK_ALL, w=[("PS", sbk)])
            else:
                (u0_, pv0), (u1_, _) = g_
                c0, c1 = tcols(b, u0_), tcols(b, u1_)
                both = slice(c0.start, c1.stop, c0.step)
                k0 = pv0 if pv0 is not None else u0_
                S.op("pe", (lambda e: e.matmul(PS[:, sbk, 0:128], lhsT=KA[0:64, tcols(b, k0)], rhs=QA[0:64, c0], start=True, stop=True)),
                     r=QK_ALL, w=[("PS", sbk)])
                S.op("pe", (lambda e: e.matmul(PS[:, sbk, 128:384], lhsT=KA[0:64, c0], rhs=QA[0:64, both], start=True, stop=True)),
                     r=QK_ALL, w=[("PS", sbk)])
                S.op("pe", (lambda e: e.matmul(PS[:, sbk, 384:512], lhsT=KA[0:64, c1], rhs=QA[0:64, c1], start=True, stop=True)),
                     r=QK_ALL, w=[("PS", sbk)])
            S.op("dve", (lambda e: e.tensor_tensor(out=pt, in0=PS[:, sbk, :], in1=TAB[:, 0:512], op=ALU.add)),
                 r=[("PS", sbk), ("TAB",)], w=[ptk])
            S.op("act", (lambda e: e.activation(out=pt, in_=pt, func=AF.Exp)), r=[ptk], w=[ptk])

        def pv_phase(gi):
            g_ = groups[gi]
            n = n0 + gi
            obk, pt, ptk = 4 + n % 2, PT[n % 3], ("PT", n % 3)
            if b == 16:
                for i, (uu, _) in enumerate(g_):
                    S.op("pe", (lambda e, i=i, uu=uu: e.matmul(PS[:, obk, i * 128:(i + 1) * 128], lhsT=VE[:, uu, :], rhs=pt[:, i * 128:(i + 1) * 128], start=True, stop=True)),
                         r=[ptk, ("VE", uu)], w=[("PS", obk)])
            else:
                (u0_, pv0), (u1_, _) = g_
                S.op("pe", (lambda e: e.matmul(PS[:, obk, 0:256], lhsT=VE[:, u0_, :], rhs=pt[:, 128:384], start=True, stop=False)),
                     r=[ptk, ("VE", u0_)], w=[("PS", obk)])
                if pv0 is not None:
                    S.op("pe", (lambda e: e.matmul(PS[:, obk, 0:128], lhsT=VE[:, pv0, :], rhs=pt[:, 0:128], start=False, stop=False)),
                         r=[ptk, ("VE", pv0)], w=[("PS", obk)])
                S.op("pe", (lambda e: e.matmul(PS[:, obk, 128:256], lhsT=VE[:, u1_, :], rhs=pt[:, 384:512], start=False, stop=True)),
                     r=[ptk, ("VE", u1_)], w=[("PS", obk)])
            u0 = g_[0][0]
            if b == 1:
                S.op("dve", (lambda e: e.tensor_copy(out=ACC[:, u0 * 128:(u0 + 2) * 128], in_=PS[:, obk, 0:256])),
                     r=[("PS", obk)], w=[("ACC",)])
            elif b == 4:
                r_, st = u0 // 4, u0 % 4
                st0 = 512 * st + r_
                dst = ACC[:, st0:st0 + 4 * 255 + 1:4]
                S.op("dve", (lambda e: e.tensor_tensor(out=dst, in0=PS[:, obk, 0:256], in1=dst, op=ALU.add)),
                     r=[("PS", obk), ("ACC",)], w=[("ACC",)])
            else:
                dst = ACC.rearrange("p (i r) -> p r i", r=16)[:, u0:u0 + 4, :]
                S.op("dve", (lambda e: e.tensor_tensor(out=dst, in0=PS[:, obk, :].rearrange("p (r i) -> p r i", i=128), in1=dst, op=ALU.add)),
                     r=[("PS", obk), ("ACC",)], w=[("ACC",)])

        score_phase(0)
        score_phase(1)
        for gi in range(len(groups)):
            if gi + 2 < len(groups):
                score_phase(gi + 2)
            pv_phase(gi)

    def head_A(l, h, cb=None):
        chunk, rows = h // 2, slice((h % 2) * 64, (h % 2) * 64 + 64)
        for bi, b in enumerate((1, 4, 16)):
            branch_A(l, h, bi, b, cb if bi == 2 else None)
        for tb in range(4):
            normalize_to_mt(ACC[:, tb_sl(tb)], ("ACC",), chunk, rows, tb)

    def head_C(l, h, cb=None):
        chunk, rows = 5 + h // 2, slice((h % 2) * 64, (h % 2) * 64 + 64)
        S.op("dve", (lambda e: e.tensor_reduce(out=KM[0:64, 0:8], in_=KA[0:64, :].rearrange("p (n j) -> p n j", j=256), axis=AX.X, op=ALU.add)),
             r=QK_ALL, w=[("KM",)])
        for t in range(16):
            S.op("pe", (lambda e, t=t: e.matmul(PS[:, 7, t * 8:(t + 1) * 8], lhsT=QA[0:64, t * 128:(t + 1) * 128], rhs=KM[0:64, 0:8], start=True, stop=True)),
                 r=QK_ALL + [("KM",)], w=[("PS", 7)])
        S.op("dve", (lambda e: e.tensor_tensor(out=GM, in0=PS[:, 7, 0:128], in1=CN[:, C_PASTNEG:C_PASTNEG + 128], op=ALU.add)),
             r=[("PS", 7), ("CN",)], w=[("GM",)])
        for t in range(16):
            S.op("dve", (lambda e, t=t: e.max(out=TH[:, t * 8:(t + 1) * 8], in_=GM[:, t * 8:(t + 1) * 8])), r=[("GM",)], w=[("TH", t)])
            S.op("dve", (lambda e, t=t: e.tensor_single_scalar(out=NS[:, t * 8:(t + 1) * 8], in_=GM[:, t * 8:(t + 1) * 8], scalar=TH[:, t * 8 + 2:t * 8 + 3], op=ALU.is_lt)),
                 r=[("GM",), ("TH", t)], w=[("NS", t)])
        S.op("dve", (lambda e: e.tensor_tensor(out=MK, in0=NS, in1=CN[:, C_NEGPAST2:C_NEGPAST2 + 128], op=ALU.mult)),
             r=[("NS", t) for t in range(16)] + [("CN",)], w=[("MK",)])
        ve_build(1, h % 2)
        if cb is not None:
            cb()
        for tb in range(4):
            for tt in range(4):
                t = tb * 4 + tt
                S.op("pe", (lambda e, t=t, tt=tt: e.matmul(PS[64:72, 7, tt * 128:(tt + 1) * 128], lhsT=MK[:, t * 8:(t + 1) * 8], rhs=IDENT, start=True, stop=True)),
                     r=[("MK",), ("CN",)], w=[("PS", 7)])
            S.op("act", (lambda e, tb=tb: e.activation(out=QA[64:72, tb_sl(tb)], in_=PS[64:72, 7, :], func=AF.Copy)),
                 r=[("PS", 7)], w=[("QA", tb)])
        items = [(m, jt) for m in range(4) for jt in range(4 * (m + 1))]
        n0 = gctr["n"]
        gctr["n"] += len(items)

        def score_phase(k):
            m, jt = items[k]
            n = n0 + k
            sbk, pt, ptk = SBK[n % 3], PT[n % 3], ("PT", n % 3)
            q0 = 256 if jt >= 4 * m + 2 else 0
            S.op("pe", (lambda e: e.matmul(PS[:, sbk, q0:512], lhsT=KA[0:72, jt * 128:(jt + 1) * 128], rhs=QA[0:72, m * 512 + q0:(m + 1) * 512], start=True, stop=True)),
                 r=QK_ALL + [("KA", "ind")], w=[("PS", sbk)])
            g0 = 512 * m - 128 * jt + 384
            S.op("dve", (lambda e: e.tensor_tensor(out=pt[:, q0:512], in0=PS[:, sbk, q0:512], in1=GT[:, g0 + q0:g0 + 512], op=ALU.add)),
                 r=[("PS", sbk), ("TAB",)], w=[ptk])
            S.op("act", (lambda e: e.activation(out=pt[:, q0:512], in_=pt[:, q0:512], func=AF.Exp)), r=[ptk], w=[ptk])

        def pv_phase(k):
            m, jt = items[k]
            n = n0 + k
            nk = 4 * (m + 1)
            obk, pt, ptk = 4 + m % 2, PT[n % 3], ("PT", n % 3)
            q0 = 256 if jt >= 4 * m + 2 else 0
            S.op("pe", (lambda e: e.matmul(PS[:, obk, q0:512], lhsT=VE[:, jt, :], rhs=pt[:, q0:512], start=(jt == 0), stop=(jt == nk - 1))),
                 r=[ptk, ("VE", jt)], w=[("PS", obk)])
            if jt == nk - 1:
                S.op("act", (lambda e: e.activation(out=OSB, in_=PS[:, obk, :], func=AF.Copy)), r=[("PS", obk)], w=[("OSB",)])
                normalize_to_mt(OSB, ("OSB",), chunk, rows, m)

        score_phase(0)
        score_phase(1)
        for k in range(len(items)):
            if k + 2 < len(items):
                score_phase(k + 2)
            pv_phase(k)

    def mixer_AC(l):
        S.op("pool", (lambda e: e.memset(VE[:, :, 64:128], 1.0)), w=[("VE", u) for u in range(16)])
        S.dma("sp", KA[64:72, :], ind_d[:, :], ("ind",), w=[("KA", "ind")])
        win_v = win_d[l].rearrange("(c p) f -> p c f", p=128)

        def wl2(colsets):
            sls = []
            for hf in range(2):
                slot = wstate["n"] % NSLOT
                wstate["n"] += 1
                v = WR[:, slot, :].rearrange("p (c f) -> p c f", f=128)
                o = 0
                for (c0, w_) in colsets:
                    S.dma("sp", v[:, :, o:o + w_], win_v[:, hf * 4:(hf + 1) * 4, c0:c0 + w_], ("w", slot), w=[("W", slot)])
                    o += w_
                wscale(slot, 4, 128, P_NW1 + l * 8 + hf * 4)
                sls.append(slot)
            return sls

        heads = [(mixn, h) for mixn in ("A", "C") if not (dbg == "onlyA" and mixn == "C") for h in range(6)]

        def load_head(i):
            mixn, h = heads[i]
            qo, ko, vo = (OFF["cq"], OFF["ck"], OFF["cv"]) if mixn == "C" else (OFF["aq"], OFF["ak"], OFF["av"])
            sqk = wl2([(qo + h * 64, 64), (ko + h * 64, 64)])
            svv = wl2([(vo + h * 64, 128)]) if h % 2 == 0 else None
            return sqk, svv

        pend = load_head(0)
        svv_cur = None
        for i, (mixn, h) in enumerate(heads):
            isC = mixn == "C"
            sqk, svv = pend
            if isC:
                S.dma("sp", GT, g_d[h], ("tab",), w=[("TAB",), ("ACC",)])
            for tb in range(4):
                bank = pj_bank()
                for c in ALLC:
                    wv = WR[:, sqk[c // 4], :].rearrange("p (c f) -> p c f", f=128)
                    S.op("pe", (lambda e, c=c, wv=wv, bank=bank, tb=tb: e.matmul(PS[:, bank, :], lhsT=wv[:, c % 4, :], rhs=XT[:, c, tb_sl(tb)], start=(c == 0), stop=(c == 7))),
                         r=[("W", sqk[c // 4])] + xt_res([c], [tb]), w=[("PS", bank)])
                S.op("dve", (lambda e, bank=bank, tb=tb: e.scalar_tensor_tensor(out=QA[0:64, tb_sl(tb)], in0=PS[0:64, bank, :], scalar=0.125, in1=RB[0:64, tb_sl(tb)], op0=ALU.mult, op1=ALU.mult)),
                     r=[("PS", bank), ("RB", tb)], w=[("QA", tb)])
                S.op("dve", (lambda e, bank=bank, tb=tb: e.tensor_tensor(out=KA[0:64, tb_sl(tb)], in0=PS[64:128, bank, :], in1=RB[64:128, tb_sl(tb)], op=ALU.mult)),
                     r=[("PS", bank), ("RB", tb)], w=[("KA", tb)])
            if i == 0:
                vt_project_pair(l, svv)
            if i + 1 < len(heads):
                pend = load_head(i + 1)
            cb = None
            if h % 2 == 1 and i + 1 < len(heads):
                nsvv = pend[1]
                cb = (lambda nsvv=nsvv: vt_project_pair(l, nsvv))
            if isC:
                head_C(l, h, cb)
            else:
                head_A(l, h, cb)

    def mixer_B(l):
        win_v = win_d[l].rearrange("(c p) f -> p c f", p=128)
        b0 = OFF["bq"]
        S.dma("sp", BW[:, 0:4, :], win_v[:, 0:4, b0:b0 + 784], ("bw",), w=[("BW",)])
        S.dma("sp", BW[:, 4:8, :], win_v[:, 4:8, b0:b0 + 784], ("bw",), w=[("BW",)])
        bwsc = PR[:, P_NW1 + l * 8:P_NW1 + l * 8 + 8].rearrange("p (c o) -> p c o", o=1).broadcast_to([128, 8, 784])
        S.op("dve", (lambda e: e.tensor_tensor(out=BW, in0=BW, in1=bwsc, op=ALU.mult)), r=[("PR",)], w=[("BW",)])
        S.op("pool", (lambda e: e.memset(SALL, 0.0)), w=[("SALL",)])
        i16 = 1.0 / 16.0
        for tb in range(4):
            def proj(cols, M, dst, key, tb=tb):
                bank = pj_bank()
                for c in ALLC:
                    S.op("pe", (lambda e, c=c, bank=bank: e.matmul(PS[0:M, bank, :], lhsT=BW[:, c, cols], rhs=XT[:, c, tb_sl(tb)], start=(c == 0), stop=(c == 7))),
                         r=[("BW",)] + xt_res([c], [tb]), w=[("PS", bank)])
                S.op("dve", (lambda e, bank=bank: e.tensor_tensor(out=dst[0:M, :], in0=PS[0:M, bank, :], in1=RB[0:M, tb_sl(tb)], op=ALU.mult)),
                     r=[("PS", bank), ("RB", tb)], w=[key])
            proj(slice(0, 128), 128, BQ, ("BQ",))
            proj(slice(128, 256), 128, BK, ("BK",))
            for rc in range(2):
                proj(slice(512 + rc * 128, 512 + (rc + 1) * 128), 128, BR[rc], ("BR", rc))
                S.op("act", (lambda e, rc=rc: e.activation(out=BR[rc], in_=BR[rc], func=AF.Silu)), r=[("BR", rc)], w=[("BR", rc)])
            proj(slice(768, 784), 16, ALR, ("ALR",))
            for ch in range(4):
                chunk_B(l, tb, ch)

    def chunk_B(l, tb, ch):
        i16 = 1.0 / 16.0
        g = tb * 4 + ch
        t0 = g * 128
        csl = slice(ch * 128, (ch + 1) * 128)
        bank = pj_bank()
        for c in ALLC:
            S.op("pe", (lambda e, c=c: e.matmul(PS[:, bank, 0:256], lhsT=XT[:, c, t0:t0 + 128], rhs=BW[:, c, 256:512], start=(c == 0), stop=(c == 7))),
                 r=[("BW",)] + xt_res([c], [tb]), w=[("PS", bank)])
        S.op("act", (lambda e: e.activation(out=BV, in_=PS[:, bank, 0:256], func=AF.Copy, scale=RT[:, g:g + 1])),
             r=[("PS", bank), ("RT",)], w=[("BV",)])
        S.op("pe", (lambda e: e.matmul(PS[:, 2, 0:128], lhsT=ALR[0:16, csl], rhs=WA2[0:16, l * 128:(l + 1) * 128], start=True, stop=False)),
             r=[("ALR",), ("WA2",)], w=[("PS", 2)])
        S.op("pe", (lambda e: e.matmul(PS[:, 2, 0:128], lhsT=ONES[0:1, 0:128], rhs=BA[0:1, l * 128:(l + 1) * 128], start=False, stop=True)),
             r=[("CN",), ("BA",)], w=[("PS", 2)])
        S.op("act", (lambda e: e.activation(out=SP_, in_=PS[:, 2, 0:128], func=AF.Exp, scale=-1.0)), r=[("PS", 2)], w=[("SP",)])
        S.op("act", (lambda e: e.activation(out=SP_, in_=SP_, func=AF.Ln, bias=1.0, scale=1.0)), r=[("SP",)], w=[("SP",)])
        S.op("pe", (lambda e: e.matmul(PS[:, 3, 0:128], lhsT=SP_, rhs=TRI, start=True, stop=True)), r=[("SP",), ("CN",)], w=[("PS", 3)])
        S.op("act", (lambda e: e.activation(out=EB, in_=PS[:, 3, 0:128], func=AF.Exp, scale=-i16)), r=[("PS", 3)], w=[("EB",)])
        S.op("act", (lambda e: e.activation(out=KG, in_=PS[:, 3, 0:128], func=AF.Exp, scale=i16)), r=[("PS", 3)], w=[("KG",)])
        S.op("dve", (lambda e: e.tensor_tensor(out=KG, in0=KG, in1=BK[:, csl], op=ALU.mult)), r=[("KG",), ("BK",)], w=[("KG",)])
        for h in range(4):
            S.op("dve", (lambda e, h=h: e.scalar_tensor_tensor(out=QGM[:, h * 128:(h + 1) * 128], in0=BQ[:, csl], scalar=CN[:, C_HMS + h:C_HMS + h + 1], in1=EB, op0=ALU.mult, op1=ALU.mult)),
                 r=[("BQ",), ("EB",), ("CN",)], w=[("QGM", h)])
        S.op("pe", (lambda e: e.matmul(PS[:, 2, 128:256], lhsT=KG, rhs=IDENT, start=True, stop=True)), r=[("KG",), ("CN",)], w=[("PS", 2)])
        S.op("act", (lambda e: e.activation(out=KGT, in_=PS[:, 2, 128:256], func=AF.Copy)), r=[("PS", 2)], w=[("KGT",)])
        for h in range(4):
            S.op("pe", (lambda e, h=h: e.matmul(PS[:, 4, h * 128:(h + 1) * 128], lhsT=KG, rhs=QGM[:, h * 128:(h + 1) * 128], start=True, stop=True)),
                 r=[("KG",), ("QGM", h)], w=[("PS", 4)])
        for h in range(4):
            S.op("dve", (lambda e, h=h: e.tensor_tensor(out=AT[:, h * 128:(h + 1) * 128], in0=PS[:, 4, h * 128:(h + 1) * 128], in1=TRI, op=ALU.mult)),
                 r=[("PS", 4), ("CN",)], w=[("AT", h)])
        for h in range(4):
            oap = PS[(h % 2) * 64:(h % 2) * 64 + 64, 5, (h // 2) * 128:(h // 2 + 1) * 128]
            S.op("pe", (lambda e, h=h, oap=oap: e.matmul(oap, lhsT=BV[:, h * 64:(h + 1) * 64], rhs=AT[:, h * 128:(h + 1) * 128], start=True, stop=False)),
                 r=[("BV",), ("AT", h)], w=[("PS", 5)])
            S.op("pe", (lambda e, h=h, oap=oap: e.matmul(oap, lhsT=SALL, rhs=QGM[:, h * 128:(h + 1) * 128], start=False, stop=True)),
                 r=[("SALL",), ("QGM", h)], w=[("PS", 5)])
        S.op("pe", (lambda e: e.matmul(PS[:, 3, 128:384], lhsT=KGT, rhs=BV, start=True, stop=True)), r=[("KGT",), ("BV",)], w=[("PS", 3)])
        for h in range(4):
            S.op("dve", (lambda e, h=h: e.scalar_tensor_tensor(out=SALL, in0=PS[:, 3, 128 + h * 64:128 + (h + 1) * 64], scalar=CN[:, C_HM + h:C_HM + h + 1], in1=SALL, op0=ALU.mult, op1=ALU.add)),
                 r=[("PS", 3), ("SALL",), ("CN",)], w=[("SALL",)])
        S.op("dve", (lambda e: e.tensor_scalar_mul(out=SALL, in0=SALL, scalar1=EB[:, 127:128])), r=[("SALL",), ("EB",)], w=[("SALL",)])
        S.op("act", (lambda e: e.activation(out=OSQ, in_=PS[:, 5, 0:256], func=AF.Square)), r=[("PS", 5)], w=[("OSQ",)])
        S.op("pe", (lambda e: e.matmul(PS[:, 6, 0:256], lhsT=BLK, rhs=OSQ, start=True, stop=True)), r=[("OSQ",), ("CN",)], w=[("PS", 6)])
        S.op("act", (lambda e: e.activation(out=RS, in_=PS[:, 6, 0:256], func=AF.Ln, bias=EPS_AP, scale=1.0 / 64.0)), r=[("PS", 6), ("CN",)], w=[("RS",)])
        S.op("act", (lambda e: e.activation(out=RS, in_=RS, func=AF.Exp, scale=-0.5)), r=[("RS",)], w=[("RS",)])
        S.op("dve", (lambda e: e.tensor_tensor(out=OSQ, in0=PS[:, 5, 0:256], in1=RS, op=ALU.mult)), r=[("PS", 5), ("RS",), ("OSQ",)], w=[("OSQ",)])
        for rc in range(2):
            S.op("dve", (lambda e, rc=rc: e.scalar_tensor_tensor(out=MT[:, 3 + rc, t0:t0 + 128], in0=OSQ[:, rc * 128:(rc + 1) * 128], scalar=PR[:, P_GNW + l * 2 + rc:P_GNW + l * 2 + rc + 1], in1=BR[rc][:, csl], op0=ALU.mult, op1=ALU.mult)),
                 r=[("OSQ",), ("PR",), ("BR", rc)], w=[("MT", 3 + rc, tb)])

    def ffn_block(s, l, tb, last):
        norm_stats(tb, 6 + tb % 2)
        for c in ALLC:
            S.op("dve", (lambda e, c=c: e.scalar_tensor_tensor(out=H2[:, c, :], in0=XT[:, c, tb_sl(tb)], scalar=PR[:, P_NW2 + l * 8 + c:P_NW2 + l * 8 + c + 1], in1=RB[:, tb_sl(tb)], op0=ALU.mult, op1=ALU.mult)),
                 r=xt_res([c], [tb]) + [("RB", tb), ("PR",)], w=[("H2", c)])
        for fc in range(32):
            sl = []
            for hf in range(2):
                src = wff1_d[l].rearrange("(c p) f -> p c f", p=128)[:, hf * 4:(hf + 1) * 4, fc * 128:(fc + 1) * 128]
                sl.append(wload(lambda v: v.rearrange("p (c f) -> p c f", f=128), src))
            bank = pj_bank()
            for c in ALLC:
                wv = WR[:, sl[c // 4], :].rearrange("p (c f) -> p c f", f=128)
                S.op("pe", (lambda e, c=c, wv=wv, bank=bank: e.matmul(PS[:, bank, :], lhsT=wv[:, c % 4, :], rhs=H2[:, c, :], start=(c == 0), stop=(c == 7))),
                     r=[("W", sl[c // 4]), ("H2", c)], w=[("PS", bank)])
            av = MT[:, fc // 4, (fc % 4) * 512:(fc % 4 + 1) * 512]
            S.op("act", (lambda e, av=av, bank=bank: e.activation(out=av, in_=PS[:, bank, :], func=AF.Relu)),
                 r=[("PS", bank)], w=[("MT", fc // 4, fc % 4)])
            S.op("pool", (lambda e, av=av: e.tensor_tensor(out=av, in0=av, in1=av, op=ALU.mult)),
                 r=[("MT", fc // 4, fc % 4)], w=[("MT", fc // 4, fc % 4)])
        for half in range(2):
            for fc in range(32):
                src = wff2_d[l][fc * 128:(fc + 1) * 128, half * 512:(half + 1) * 512]
                slot = wload(lambda v: v, src, q="act")
                av = MT[:, fc // 4, (fc % 4) * 512:(fc % 4 + 1) * 512]
                for o4 in range(4):
                    S.op("pe", (lambda e, slot=slot, o4=o4, av=av, fc=fc: e.matmul(PS[:, 2 + o4, :], lhsT=WR[:, slot, o4 * 128:(o4 + 1) * 128], rhs=av, start=(fc == 0), stop=(fc == 31))),
                         r=[("W", slot), ("MT", fc // 4, fc % 4)], w=[("PS", 2 + o4)])
            for o4 in range(4):
                oc = half * 4 + o4
                S.op("dve", (lambda e, oc=oc, o4=o4: e.tensor_tensor(out=XT[:, oc, tb_sl(tb)], in0=PS[:, 2 + o4, :], in1=XT[:, oc, tb_sl(tb)], op=ALU.add)),
                     r=[("PS", 2 + o4)] + xt_res([oc], [tb]), w=xt_res([oc], [tb]))

        if last:
            norm_stats(tb, 6 + tb % 2)
            for tt in range(4):
                t = tb * 4 + tt
                ob = OUTT[t % 2]
                for c in ALLC:
                    S.op("dve", (lambda e, c=c, t=t: e.scalar_tensor_tensor(out=YN[:, c * 128:(c + 1) * 128], in0=XT[:, c, t * 128:(t + 1) * 128], scalar=PR[:, P_FW + c:P_FW + c + 1], in1=RB[:, t * 128:(t + 1) * 128], op0=ALU.mult, op1=ALU.mult)),
                         r=xt_res([c], [tb]) + [("RB", tb), ("PR",)], w=[("WK", "yn", c)])
                    S.op("pe", (lambda e, c=c: e.matmul(PS[:, c // 4, (c % 4) * 128:(c % 4 + 1) * 128], lhsT=YN[:, c * 128:(c + 1) * 128], rhs=IDENT, start=True, stop=True)),
                         r=[("WK", "yn", c), ("CN",)], w=[("PS", c // 4)])
                for hb in range(2):
                    S.op("act", (lambda e, hb=hb, ob=ob: e.activation(out=ob[:, hb * 512:(hb + 1) * 512], in_=PS[:, hb, :], func=AF.Copy)),
                         r=[("PS", hb)], w=[("WK", "out", t % 2)])
                S.dma("sp", out_d[s, t * 128:(t + 1) * 128, :], ob, ("out", t % 2), r=[("WK", "out", t % 2)], w=[("OUTD",)])


    for s in range(nseq):
        S.barrier()
        for t in range(16):
            xin = XIN[t % 2]
            S.dma("sp" if t % 2 == 0 else "act", xin, x_d[s, t * 128:(t + 1) * 128, :], ("xin", t % 2), w=[("WK", "xin", t % 2)])
            for c in ALLC:
                S.op("pe", (lambda e, c=c, xin=xin: e.matmul(PS[:, 6 + c // 4, (c % 4) * 128:(c % 4 + 1) * 128], lhsT=xin[:, c * 128:(c + 1) * 128], rhs=IDENT, start=True, stop=True)),
                     r=[("WK", "xin", t % 2), ("CN",)], w=[("PS", 6 + c // 4)])
            for hb in range(2):
                S.op("dve", (lambda e, t=t, hb=hb: e.tensor_copy(out=XT[:, hb * 4:(hb + 1) * 4, t * 128:(t + 1) * 128], in_=PS[:, 6 + hb, :].rearrange("p (k t) -> p k t", t=128))),
                     r=[("PS", 6 + hb)], w=xt_res(range(hb * 4, hb * 4 + 4), [t // 4]))

        for l in range(nlayers):
            last = (l == nlayers - 1)
            for tb in range(4):
                norm_stats(tb, 6 + tb % 2)
            for u in range(16):
                bank = 6 + u % 2
                S.op("pe", (lambda e, u=u, bank=bank: e.matmul(PS[:, bank, 0:1], lhsT=RB[:, u * 128:(u + 1) * 128], rhs=IDENT[:, 0:1], start=True, stop=True)),
                     r=[("RB", u // 4), ("CN",)], w=[("PS", bank)])
                S.op("dve", (lambda e, u=u, bank=bank: e.tensor_copy(out=RT[:, u:u + 1], in_=PS[:, bank, 0:1])),
                     r=[("PS", bank)], w=[("RT",)])
            S.barrier()
            if dbg in ("skipmix", "noB", "onlyA", "onlyB"):
                S.op("pool", (lambda e: e.memset(MT[:, :, :], 0.0)), w=[("MT", c, tb) for c in ALLC for tb in range(4)])
            if dbg != "skipmix":
                if dbg not in ("noB", "onlyA"):
                    mixer_B(l)
                S.barrier()
                if dbg != "onlyB":
                    mixer_AC(l)
            S.barrier()

            for oc in range(NCH):
                sl = []
                for hf in range(2):
                    src = wout_d[l].rearrange("(c p) f -> p c f", p=128)[:, hf * 4:(hf + 1) * 4, oc * 128:(oc + 1) * 128]
                    sl.append(wload(lambda v: v.rearrange("p (c f) -> p c f", f=128), src))
                for tb in range(4):
                    bank = pj_bank()
                    for mc in ALLC:
                        wv = WR[:, sl[mc // 4], :].rearrange("p (c f) -> p c f", f=128)
                        S.op("pe", (lambda e, mc=mc, wv=wv, bank=bank, tb=tb: e.matmul(PS[:, bank, :], lhsT=wv[:, mc % 4, :], rhs=MT[:, mc, tb_sl(tb)], start=(mc == 0), stop=(mc == 7))),
                             r=[("W", sl[mc // 4]), ("MT", mc, tb)], w=[("PS", bank)])
                    S.op("dve", (lambda e, oc=oc, bank=bank, tb=tb: e.tensor_tensor(out=XT[:, oc, tb_sl(tb)], in0=PS[:, bank, :], in1=XT[:, oc, tb_sl(tb)], op=ALU.add)),
                         r=[("PS", bank)] + xt_res([oc], [tb]), w=xt_res([oc], [tb]))

            for tb in range(4):
                ffn_block(s, l, tb, last)

    S.op("sp", (lambda e: e.nop()), r=[("WK", "out", 0), ("WK", "out", 1)], w=[("WK", "out", 0), ("WK", "out", 1)])
    S.emit(nc, stack)
    stack.close()
    return nc


def rel_bucket_np(d):
    n = np.maximum(d, 0)
    exact = 16
    logv = np.log(np.maximum(n, 1).astype(np.float32) / np.float32(exact)) / np.float32(math.log(2048 / exact))
    large = np.minimum(exact + (logv.astype(np.float32) * np.float32(32 - exact)).astype(np.int32), 31)
    return np.where(n < exact, n, large)


def host_tables(rel_bias):
    rel_bias = np.asarray(rel_bias, np.float32)
    NEG = np.float32(-BIG)
    jj = np.arange(128)[:, None]
    ii = np.arange(128)[None, :]
    ta = np.zeros((6, 128, 1536), np.float32)
    for h in range(6):
        def tile(dil, prev):
            d = ii - jj + (128 if prev else 0)
            valid = (d <= 128) if prev else (d >= 0)
            v = rel_bias[rel_bucket_np(np.maximum(d, 0) * dil), h]
            return np.where(valid, v, NEG).astype(np.float32)
        t1 = np.concatenate([tile(1, True), tile(1, False)], 1)
        t2 = np.concatenate([tile(4, True), tile(4, False)], 1)
        t3 = tile(16, False)
        ta[h] = np.concatenate([t1, t1, t2, t2, t3, t3, t3, t3], 1)
    m = np.arange(GW)[None, :]
    d = m - jj - 384
    g = np.zeros((6, 128, GW), np.float32)
    for h in range(6):
        v = rel_bias[rel_bucket_np(np.maximum(d, 0)), 6 + h]
        g[h] = np.where(d >= 0, v, NEG)
    return ta, g


def host_consts():
    c = np.zeros((128, NCONST), np.float32)
    k = np.arange(128)[:, None]
    m = np.arange(128)[None, :]
    c[:, C_IDENT:C_IDENT + 128] = (k == m)
    c[:, C_ONES:C_ONES + 128] = 1.0
    c[:, C_TRI:C_TRI + 128] = (k <= m)
    c[:, C_SWAP:C_SWAP + 128] = (k == (m + 64) % 128)
    c[:, C_BLK:C_BLK + 128] = (k // 64 == m // 64)
    for t in range(16):
        b = t // 2
        for n in range(8):
            c[:, C_PASTNEG + t * 8 + n] = -1e30 if n >= b else 0.0
            c[:, C_NEGPAST2 + t * 8 + n] = -BIG if n < b else 0.0
    for h in range(4):
        c[:, C_HM + h] = (np.arange(128) // 32 == h)
        c[:, C_HMS + h] = (np.arange(128) // 32 == h) * (32 ** -0.5)
    c[:, NCONST - 1] = EPS
    ind = np.zeros((8, S_LEN), np.float32)
    for n in range(8):
        ind[n, n * 256:(n + 1) * 256] = 1.0
    return c, ind


_CACHE = {}


def kernel(x, norm1_w, w_in, gla_w_a2, gla_b_a, gla_norm_w, w_out, norm2_w, w_ff1, w_ff2, rel_bias, final_norm_w):
    ncores = 8
    x = np.ascontiguousarray(np.asarray(x, np.float32))
    nseq = x.shape[0] // ncores
    ta, g = host_tables(rel_bias)
    consts, ind = host_consts()
    par = np.zeros((128, NPAR), np.float32)
    n1 = np.asarray(norm1_w, np.float32).reshape(2, 8, 128)
    n2 = np.asarray(norm2_w, np.float32).reshape(2, 8, 128)
    for l in range(2):
        par[:, P_NW1 + l * 8:P_NW1 + l * 8 + 8] = n1[l].T
        par[:, P_NW2 + l * 8:P_NW2 + l * 8 + 8] = n2[l].T
        par[:, P_GNW + l * 2:P_GNW + l * 2 + 2] = np.asarray(gla_norm_w, np.float32)[l].reshape(2, 128).T
    par[:, P_FW:P_FW + 8] = np.asarray(final_norm_w, np.float32).reshape(8, 128).T
    wa2 = np.ascontiguousarray(np.asarray(gla_w_a2, np.float32).transpose(1, 0, 2).reshape(16, 256))
    ba = np.ascontiguousarray(np.asarray(gla_b_a, np.float32).reshape(1, 256))
    if "nc" not in _CACHE:
        _CACHE["nc"] = build_program(nseq)
    nc = _CACHE["nc"]
    shared = dict(w_in=np.ascontiguousarray(np.asarray(w_in, np.float32)), w_out=np.ascontiguousarray(np.asarray(w_out, np.float32)),
                  w_ff1=np.ascontiguousarray(np.asarray(w_ff1, np.float32)), w_ff2=np.ascontiguousarray(np.asarray(w_ff2, np.float32)),
                  ta=ta, gtab=g, consts=consts, ind=ind, params=par, wa2=wa2, ba=ba)
    in_maps = []
    for i in range(ncores):
        m = dict(shared)
        m["x"] = x[i * nseq:(i + 1) * nseq]
        in_maps.append(m)
    res = run_bass_kernel_spmd(nc, in_maps, core_ids=list(range(ncores)))
    return np.concatenate([r["out"] for r in res.results], axis=0)
```
